# Optimizing a Trainium2 kernel written in Bass

```python
import math
import jax, jax.numpy as jnp
from jax import lax
import numpy as np

D_MODEL = 1024
BATCH = 16
SEQ = 4096
DEPTH = 1
DEC_BATCH = 8
DEC_SEQ = 4096
PAST_LEN = 128

D_MIX = D_MODEL
D_LRU = D_MIX // 2
D_POOL = D_MIX - D_LRU
LRU_HEADS = 8
LRU_HEAD_DIM = D_LRU // LRU_HEADS
CONV_WIDTH = 4
CONV_LEFT = 2
RG_C = 8.0
POOL_WINDOWS = (2, 4, 8, 16)
POOL_GROUPS = len(POOL_WINDOWS)
POOL_GROUP_DIM = D_POOL // POOL_GROUPS
D_FF = int(math.ceil((8 * D_MODEL / 3) / 256) * 256)
N_DIR = 2
EPS = 1e-6

kernel_name = "hymba_rglru_pool_encoder"


def rms_norm(x, g):
    xf = x.astype(jnp.float32)
    y = xf * lax.rsqrt(jnp.mean(xf * xf, axis=-1, keepdims=True) + EPS)
    return (y * g.astype(jnp.float32)).astype(x.dtype)


def centred_depthwise_conv(u, w, b):
    S = u.shape[1]
    up = jnp.pad(u, ((0, 0), (CONV_LEFT, CONV_WIDTH - 1 - CONV_LEFT), (0, 0)))
    out = b
    for k in range(CONV_WIDTH):
        out = out + up[:, k:k + S] * w[k]
    return out


def _linear_combine(c1, c2):
    a1, b1 = c1
    a2, b2 = c2
    return a1 * a2, a2 * b1 + b2


def rg_lru_direction(u, w_a, b_a, w_x, b_x, lam, reverse):
    B, S, C = u.shape
    uh = u.reshape(B, S, LRU_HEADS, LRU_HEAD_DIM)
    r = jax.nn.sigmoid(jnp.einsum('bshi,hij->bshj', uh, w_a.astype(jnp.float32)).reshape(B, S, C) + b_a.astype(jnp.float32))
    i = jax.nn.sigmoid(jnp.einsum('bshi,hij->bshj', uh, w_x.astype(jnp.float32)).reshape(B, S, C) + b_x.astype(jnp.float32))
    log_a = -RG_C * r * jax.nn.softplus(-lam.astype(jnp.float32))
    a = jnp.exp(log_a)
    mult = jnp.sqrt(jnp.maximum(-jnp.expm1(2.0 * log_a), 0.0))
    bterm = mult * (i * u)
    _, h = lax.associative_scan(_linear_combine, (a, bterm), axis=1, reverse=reverse)
    return h


def centred_window_mean(u, w):
    S = u.shape[1]
    half = w // 2
    up = jnp.pad(u, ((0, 0), (half, w - half - 1), (0, 0)))
    cs = jnp.pad(lax.cumsum(up, axis=1), ((0, 0), (1, 0), (0, 0)))
    sums = cs[:, w:w + S] - cs[:, :S]
    t = np.arange(S)
    count = (np.minimum(t + w - half, S) - np.maximum(t - half, 0)).astype(np.float32)
    return sums / jnp.asarray(count)[None, :, None]


def pool_mixer(u, pool_w, pool_b, pool_scale):
    B, S, _ = u.shape
    uf = u.astype(jnp.float32)
    groups = []
    for g, w in enumerate(POOL_WINDOWS):
        ug = uf[..., g * POOL_GROUP_DIM:(g + 1) * POOL_GROUP_DIM]
        groups.append(centred_window_mean(ug, w) - ug)
    pg = jnp.stack(groups, axis=2).astype(u.dtype)
    y = jnp.einsum('bsgi,gij->bsgj', pg, pool_w).reshape(B, S, D_POOL) + pool_b
    return y * pool_scale


def token_mixer(h, w_in, conv_w, conv_b, rg_wa, rg_ba, rg_wx, rg_bx, rg_lam, pool_w, pool_b, pool_scale, w_out):
    proj = h @ w_in
    u_lru = proj[..., :D_LRU]
    gate = proj[..., D_LRU:2 * D_LRU]
    u_pool = proj[..., 2 * D_LRU:]
    u = centred_depthwise_conv(u_lru, conv_w, conv_b).astype(jnp.float32)
    h_fwd = rg_lru_direction(u, rg_wa[0], rg_ba[0], rg_wx[0], rg_bx[0], rg_lam[0], reverse=False)
    h_bwd = rg_lru_direction(u, rg_wa[1], rg_ba[1], rg_wx[1], rg_bx[1], rg_lam[1], reverse=True)
    y_lru = ((h_fwd + h_bwd) * jax.nn.gelu(gate.astype(jnp.float32))).astype(h.dtype)
    y_pool = pool_mixer(u_pool, pool_w, pool_b, pool_scale)
    return jnp.concatenate([y_lru, y_pool], axis=-1) @ w_out


def swiglu_ffn(h, w_gate, w_up, w_down):
    return (jax.nn.silu(h @ w_gate) * (h @ w_up)) @ w_down


def encoder_layer(x, l, pre_norm_mix, post_norm_mix, w_in, conv_w, conv_b, rg_wa, rg_ba, rg_wx, rg_bx, rg_lam,
                  pool_w, pool_b, pool_scale, w_out, pre_norm_ffn, post_norm_ffn, w_gate, w_up, w_down):
    h = rms_norm(x, pre_norm_mix[l])
    m = token_mixer(h, w_in[l], conv_w[l], conv_b[l], rg_wa[l], rg_ba[l], rg_wx[l], rg_bx[l], rg_lam[l],
                    pool_w[l], pool_b[l], pool_scale[l], w_out[l])
    x = x + rms_norm(m, post_norm_mix[l])
    h = rms_norm(x, pre_norm_ffn[l])
    f = swiglu_ffn(h, w_gate[l], w_up[l], w_down[l])
    return x + rms_norm(f, post_norm_ffn[l])


def setup_inputs(seed: int = 0) -> dict:
    key = jax.random.key(seed)
    ks = jax.random.split(key, 24)
    f32 = jnp.float32

    def nrm(k, shape, scale):
        return jax.random.normal(k, shape, f32) * scale

    a8 = jax.random.uniform(ks[10], (DEPTH, N_DIR, D_LRU), f32, minval=0.9, maxval=0.999)
    a_base = a8 ** (1.0 / RG_C)
    rg_lam = jnp.log(a_base) - jnp.log1p(-a_base)
    return {
        "x_prompt": nrm(ks[0], (BATCH, SEQ, D_MODEL), 1.0),
        "x_sample": nrm(ks[1], (DEC_BATCH, DEC_SEQ, D_MODEL), 1.0),
        "pre_norm_mix": 1.0 + nrm(ks[2], (DEPTH, D_MODEL), 0.05),
        "post_norm_mix": 1.0 + nrm(ks[3], (DEPTH, D_MODEL), 0.05),
        "w_in": nrm(ks[4], (DEPTH, D_MODEL, 2 * D_LRU + D_POOL), D_MODEL ** -0.5),
        "conv_w": nrm(ks[5], (DEPTH, CONV_WIDTH, D_LRU), CONV_WIDTH ** -0.5),
        "conv_b": nrm(ks[6], (DEPTH, D_LRU), 0.01),
        "rg_wa": nrm(ks[7], (DEPTH, N_DIR, LRU_HEADS, LRU_HEAD_DIM, LRU_HEAD_DIM), LRU_HEAD_DIM ** -0.5),
        "rg_ba": nrm(ks[8], (DEPTH, N_DIR, D_LRU), 0.1),
        "rg_wx": nrm(ks[9], (DEPTH, N_DIR, LRU_HEADS, LRU_HEAD_DIM, LRU_HEAD_DIM), LRU_HEAD_DIM ** -0.5),
        "rg_bx": nrm(ks[11], (DEPTH, N_DIR, D_LRU), 0.1),
        "rg_lam": rg_lam,
        "pool_w": nrm(ks[12], (DEPTH, POOL_GROUPS, POOL_GROUP_DIM, POOL_GROUP_DIM), POOL_GROUP_DIM ** -0.5),
        "pool_b": nrm(ks[13], (DEPTH, D_POOL), 0.01),
        "pool_scale": 1.0 + nrm(ks[14], (DEPTH, D_POOL), 0.1),
        "w_out": nrm(ks[15], (DEPTH, D_MIX, D_MODEL), D_MIX ** -0.5),
        "pre_norm_ffn": 1.0 + nrm(ks[16], (DEPTH, D_MODEL), 0.05),
        "post_norm_ffn": 1.0 + nrm(ks[17], (DEPTH, D_MODEL), 0.05),
        "w_gate": nrm(ks[18], (DEPTH, D_MODEL, D_FF), D_MODEL ** -0.5),
        "w_up": nrm(ks[19], (DEPTH, D_MODEL, D_FF), D_MODEL ** -0.5),
        "w_down": nrm(ks[20], (DEPTH, D_FF, D_MODEL), D_FF ** -0.5),
    }


def reference(x_prompt, x_sample, pre_norm_mix, post_norm_mix, w_in, conv_w, conv_b, rg_wa, rg_ba, rg_wx, rg_bx,
              rg_lam, pool_w, pool_b, pool_scale, w_out, pre_norm_ffn, post_norm_ffn, w_gate, w_up, w_down):
    y_prompt = x_prompt
    y_sample = x_sample
    for l in range(DEPTH):
        y_prompt = encoder_layer(y_prompt, l, pre_norm_mix, post_norm_mix, w_in, conv_w, conv_b, rg_wa, rg_ba,
                                 rg_wx, rg_bx, rg_lam, pool_w, pool_b, pool_scale, w_out, pre_norm_ffn,
                                 post_norm_ffn, w_gate, w_up, w_down)
        y_sample = encoder_layer(y_sample, l, pre_norm_mix, post_norm_mix, w_in, conv_w, conv_b, rg_wa, rg_ba,
                                 rg_wx, rg_bx, rg_lam, pool_w, pool_b, pool_scale, w_out, pre_norm_ffn,
                                 post_norm_ffn, w_gate, w_up, w_down)
    return (y_prompt, y_sample)
```

```python
import numpy as np
import ml_dtypes
from contextlib import ExitStack
import concourse.bass as bass
import concourse.mybir as mybir
from concourse.bass_utils import run_bass_kernel_spmd

F32 = mybir.dt.float32
BF16 = mybir.dt.bfloat16
F32R = mybir.dt.float32r
AF = mybir.ActivationFunctionType
ALU = mybir.AluOpType

NCORES = 8
D = 1024
SEQ = 4096
NSEQ = 3
NTOK = NSEQ * SEQ
DL = 512
DFF = 2816
NFF = DFF // 128
T1 = 512
HALO = 16
RW = T1 + HALO
NT1 = SEQ // T1
NSTG = NT1 + 1
T2 = 256
NT2 = SEQ // T2
EPS = 1e-6
WINS = (2, 4, 8, 16)

VC_CW, VC_CB, VC_BA, VC_BX, VC_LAM, VC_PB, VC_PS, VC_G1, VC_G3, NV = 0, 16, 20, 28, 36, 44, 48, 52, 60, 68


VERBOSE = False


class Buf:
    __slots__ = ("name", "w", "r", "dsem", "grp", "psum", "gbase")

    def __init__(self, name="", psum=False):
        self.name = name
        self.psum = psum
        self.w = None
        self.r = []
        self.grp = []
        self.gbase = set()
        self.dsem = None


class Node:
    __slots__ = ("id", "eng", "kind", "fn", "out", "in_", "sem", "deps", "cost", "lat", "start", "fin", "val", "clock", "tab")


class Sched:
    ENGS = ("tensor", "vector", "scalar", "gpsimd", "sync")

    def __init__(self, nc, es):
        self.nc = nc
        self.es = es
        self.sems = []
        self.semcount = []
        self.engsem = {}
        self.known = {}
        for n in self.ENGS:
            h = es.enter_context(nc.semaphore("e_" + n))
            self.sems.append(h)
            self.semcount.append(0)
            self.engsem[n] = len(self.sems) - 1
            self.known[n] = {}
        self.nodes = []
        self.done = {}
        self.nid = 0
        self.nwaits = 0
        self.ninst = 0
        self.sim_time = 0.0

    @staticmethod
    def _deps(reads, writes):
        deps = set()
        for b in reads:
            if b.w is not None:
                deps.add(b.w)
            deps.update(b.grp)
            if b.psum:
                deps.update(b.r)
        for b in writes:
            if b.w is not None:
                deps.add(b.w)
            deps.update(b.grp)
            deps.update(b.r)
        return deps

    def _node(self, eng, kind, deps, cost, lat):
        n = Node()
        n.id = self.nid
        self.nid += 1
        n.eng, n.kind, n.deps, n.cost, n.lat = eng, kind, deps, cost, lat
        n.fn = n.out = n.in_ = n.sem = None
        n.tab = None
        self.nodes.append(n)
        self.ninst += 1
        return n

    def op(self, eng, fn, reads=(), writes=(), n=512, two=False, tab=None):
        if eng == "tensor":
            cost = 8.0 + n / 2.35
        elif eng == "scalar":
            cost = 200.0 + 0.78 * n
        elif eng == "vector":
            cost = 100.0 + 1.25 * n
        else:
            cost = 400.0 + (2.3 if two else 0.95) * n
        n = self._node(eng, "op", self._deps(reads, writes), cost, cost)
        n.fn = fn
        n.tab = tab
        n.sem = self.engsem[eng]
        for b in writes:
            b.w = n.id
            b.r = []
            b.grp = []
        for b in reads:
            if b.w != n.id:
                b.r.append(n.id)
        return n.id

    def dma(self, q, out, in_, reads=(), writes=(), sembuf=None, group=False, nbytes=65536):
        if sembuf.dsem is None:
            h = self.es.enter_context(self.nc.semaphore("d%d" % len(self.sems)))
            self.sems.append(h)
            self.semcount.append(0)
            sembuf.dsem = len(self.sems) - 1
        deps = self._deps(reads, writes)
        if group:
            for b in writes:
                if not b.grp:
                    b.gbase = set(deps)
                deps -= set(b.grp)
                deps |= b.gbase
        issue = 1000.0 if q == "gpsimd" else 60.0
        n = self._node(q, "dma", deps, issue, 2000.0 + nbytes / 150.0)
        n.out, n.in_, n.sem = out, in_, sembuf.dsem
        for b in writes:
            if group:
                b.grp.append(n.id)
            else:
                b.grp = []
                b.r = []
            b.w = n.id
        for b in reads:
            b.r.append(n.id)
        return n.id

    def flush(self, final_engines=None):
        import heapq
        nodes = self.nodes
        self.nodes = []
        byid = {n.id: n for n in nodes}
        ndep = {}
        users = {}
        ready_t = {}
        for n in nodes:
            cnt = 0
            for d in n.deps:
                if d in byid:
                    cnt += 1
                    users.setdefault(d, []).append(n.id)
            ndep[n.id] = cnt
            ready_t[n.id] = 0.0
        free = {e: 0.0 for e in self.ENGS}
        pend = {e: [] for e in self.ENGS}
        avail = {e: [] for e in self.ENGS}
        for n in nodes:
            if ndep[n.id] == 0:
                heapq.heappush(pend[n.eng], (0.0, n.id))
        order = []
        remaining = len(nodes)
        SEMLAT = 150.0
        cur_tab = [None]
        while remaining:
            best = None
            for e in self.ENGS:
                if avail[e]:
                    t = free[e]
                elif pend[e]:
                    t = max(free[e], pend[e][0][0])
                else:
                    continue
                if best is None or t < best[0]:
                    best = (t, e)
            t, e = best
            while pend[e] and pend[e][0][0] <= t:
                rt, i = heapq.heappop(pend[e])
                heapq.heappush(avail[e], i)
            extra = 0.0
            if e == "scalar":
                lst = avail[e]
                cur = cur_tab[0]

                def isfree(i):
                    tb_ = byid[i].tab
                    return tb_ is None or tb_ == cur or (tb_ == "tanh" and cur in ("exp", "gelu", "silu"))
                same = [i for i in lst if isfree(i)]
                nsq = sum(1 for i in lst if byid[i].tab == "sqrt")
                held = set()
                if cur != "sqrt" and 0 < nsq < 4:
                    held |= set(i for i in lst if byid[i].tab == "sqrt" and t - ready_t[i] < 8000.0)
                if cur != "sqrt":
                    held |= set(i for i in lst if byid[i].tab == "gelu" and t - ready_t[i] < 30000.0)
                rest = [i for i in lst if i not in held]
                lst2 = rest if rest else lst
                oldest = min(lst2)
                if cur == "sqrt" and not same:
                    gl = [i for i in lst if byid[i].tab == "gelu"]
                    if gl:
                        oldest = min(gl)
                if same and not (not isfree(oldest) and t - ready_t[oldest] > 25000.0):
                    i = min(same)
                else:
                    i = oldest
                lst.remove(i)
                heapq.heapify(lst)
                if not isfree(i):
                    tb_ = byid[i].tab
                    cur_tab[0] = "exp" if tb_ == "tanh" else tb_
                    extra = 1300.0
            else:
                i = heapq.heappop(avail[e])
            n = byid[i]
            n.start = t + extra
            t = t + extra
            free[e] = t + n.cost
            n.fin = t + n.lat
            order.append(n)
            remaining -= 1
            for u in users.get(i, ()):
                ndep[u] -= 1
                if ready_t[u] < n.fin + SEMLAT:
                    ready_t[u] = n.fin + SEMLAT
                if ndep[u] == 0:
                    un = byid[u]
                    heapq.heappush(pend[un.eng], (ready_t[u], u))
        self.sim_time += max([n.fin for n in order] + [0.0])
        if VERBOSE:
            mk = max([n.fin for n in order] + [0.0])
            busy = {e: sum(n.cost for n in order if n.eng == e) for e in self.ENGS}
            print("[sched] phase makespan %.0f us; busy " % (mk / 1e3) + " ".join("%s=%.0f%%" % (e[:4], 100 * busy[e] / mk) for e in self.ENGS))
        prog = {e: [] for e in self.ENGS}
        for n in order:
            E = n.eng
            known = self.known[E]
            need = {}
            for d in n.deps:
                if d in byid:
                    dn = byid[d]
                    ev = (dn.sem, dn.val, dn.clock)
                else:
                    ev = self.done[d]
                sem, val, clock = ev
                if E == "tensor" and sem == self.engsem["tensor"]:
                    continue
                if known.get(sem, 0) >= val:
                    continue
                if need.get(sem, (0, None))[0] < val:
                    need[sem] = (val, clock)
            for sem, (val, clock) in need.items():
                if known.get(sem, 0) >= val:
                    continue
                prog[E].append(("wait", sem, val))
                self.nwaits += 1
                for s2, v2 in clock.items():
                    if known.get(s2, 0) < v2:
                        known[s2] = v2
                known[sem] = val
            if n.kind == "op":
                self.semcount[n.sem] += 1
                prog[E].append(("op", n.fn))
            else:
                self.semcount[n.sem] += 16
                prog[E].append(("dma", n.out, n.in_, n.sem))
            n.val = self.semcount[n.sem]
            n.clock = dict(known)
        for n in order:
            self.done[n.id] = (n.sem, n.val, n.clock)
            n.fn = n.out = n.in_ = None
        for e in (final_engines or self.ENGS):
            known = self.known[e]
            for sem, cnt in enumerate(self.semcount):
                if cnt > 0 and known.get(sem, 0) < cnt and not (e == "tensor" and sem == self.engsem["tensor"]):
                    prog[e].append(("wait", sem, cnt))
                    known[sem] = cnt
        sems = self.sems
        with self.nc.Block() as block:
            def runner(e):
                def run(h):
                    mysem = sems[self.engsem[e]]
                    for item in prog[e]:
                        if item[0] == "wait":
                            h.wait_ge(sems[item[1]], item[2])
                        elif item[0] == "op":
                            item[1](h).then_inc(mysem, 1)
                        else:
                            h.dma_start(out=item[1], in_=item[2]).then_inc(sems[item[3]], 16)
                return run
            block.tensor(runner("tensor"))
            block.vector(runner("vector"))
            block.scalar(runner("scalar"))
            block.gpsimd(runner("gpsimd"))
            block.sync(runner("sync"))


class Ring:
    def __init__(self, tiles, psum=False):
        self.tiles = tiles
        self.bufs = [Buf(psum=psum) for _ in tiles]
        self.i = 0

    def next(self):
        j = self.i % len(self.tiles)
        self.i += 1
        return self.tiles[j], self.bufs[j]


def build_program():
    nc = bass.Bass("TRN2", target_bir_lowering=False)
    dt = lambda name, shape, dtype, kind: nc.dram_tensor(name, shape, dtype, kind=kind).ap()
    x_d = dt("x", [NTOK, D], F32, "ExternalInput")
    vecs_d = dt("vecs", [128, NV], F32, "ExternalInput")
    g2_d = dt("g2", [1, D], F32, "ExternalInput")
    g4_d = dt("g4", [1, D], F32, "ExternalInput")
    ident_d = dt("ident", [128, 128], BF16, "ExternalInput")
    ec_d = dt("ec", [128, 4, 16], F32, "ExternalInput")
    w_in_d = dt("w_in", [D, 3 * DL], F32, "ExternalInput")
    rg_wa_d = dt("rg_wa", [2, 8, 64, 64], F32, "ExternalInput")
    rg_wx_d = dt("rg_wx", [2, 8, 64, 64], F32, "ExternalInput")
    pool_w_d = dt("pool_w", [4, 128, 128], F32, "ExternalInput")
    w_out_d = dt("w_out", [D, D], F32, "ExternalInput")
    w_gate_d = dt("w_gate", [D, DFF], F32, "ExternalInput")
    w_up_d = dt("w_up", [D, DFF], F32, "ExternalInput")
    w_down_d = dt("w_down", [DFF, D], F32, "ExternalInput")
    y_d = dt("y", [NTOK, D], F32, "ExternalOutput")
    sa_d = dt("stash_a", [NSEQ, 4, 128, SEQ], F32, "Internal")
    sb_d = dt("stash_b", [NSEQ, 4, 128, SEQ], F32, "Internal")
    sy_d = dt("stash_y", [NSEQ, 4, 128, SEQ], BF16, "Internal")

    with ExitStack() as es:
        S = Sched(nc, es)
        op, dma = S.op, S.dma

        def sb(ctx, name, shape, dtype):
            return ctx.enter_context(nc.sbuf_tensor(name, shape, dtype))

        def ps(ctx, name, shape, dtype):
            return ctx.enter_context(nc.psum_tensor(name, shape, dtype))

        V = sb(es, "V", [128, NV], F32); bV = Buf()
        CC = sb(es, "CC", [128, 44], F32); bCC = Buf()
        IDN = sb(es, "IDN", [128, 128], BF16); bIDN = Buf()
        NEGH = sb(es, "NEGH", [128, 1], F32); bNEGH = Buf()
        HL = sb(es, "HL", [128, NSEQ * (NSTG + 1), 4], F32)
        PPt = sb(es, "PPt", [128, NSEQ * (NSTG + 1), 4], F32)
        HH = sb(es, "HH", [128, NSEQ * (NSTG + 1), 4], F32)
        bHL = [[Buf() for _ in range(NSTG + 1)] for _ in range(NSEQ)]
        bHH = [[Buf() for _ in range(NSTG + 1)] for _ in range(NSEQ)]
        ST = sb(es, "ST", [128, 64], F32)
        bST = [Buf() for _ in range(16)]
        st_i = [0]

        def hidx(s, k):
            return s * (NSTG + 1) + k

        dma("sync", V[:], vecs_d, writes=[bV], sembuf=bV)
        dma("sync", IDN[:], ident_d, writes=[bIDN], sembuf=bIDN)
        op("gpsimd", lambda e: e.memset(NEGH[:], -0.5), writes=[bNEGH], n=1)

        lam = V[:, VC_LAM:VC_LAM + 8]
        t0, t1, t2 = CC[:, 32:40], CC[:, 0:8], CC[:, 8:16]
        op("vector", lambda e: e.tensor_scalar(out=t0, in0=lam, scalar1=-1.0, scalar2=None, op0=ALU.mult), reads=[bV], writes=[bCC], n=8)
        op("scalar", lambda e: e.activation(out=t1, in_=t0, func=AF.Abs), reads=[bCC], writes=[bCC], n=8)
        op("scalar", lambda e: e.activation(out=t1, in_=t1, func=AF.Exp, scale=-1.0), reads=[bCC], writes=[bCC], n=8, tab="ln")
        op("scalar", lambda e: e.activation(out=t1, in_=t1, func=AF.Ln, bias=1.0), reads=[bCC], writes=[bCC], n=8, tab="ln")
        op("vector", lambda e: e.tensor_scalar(out=t0, in0=t0, scalar1=0.0, scalar2=None, op0=ALU.max), reads=[bCC], writes=[bCC], n=8)
        op("vector", lambda e: e.tensor_tensor(out=t0, in0=t0, in1=t1, op=ALU.add), reads=[bCC], writes=[bCC], n=8)
        op("vector", lambda e: e.tensor_scalar(out=t1, in0=t0, scalar1=-8.0, scalar2=None, op0=ALU.mult), reads=[bCC], writes=[bCC], n=8)
        op("vector", lambda e: e.tensor_scalar(out=t2, in0=t0, scalar1=-4.0, scalar2=None, op0=ALU.mult), reads=[bCC], writes=[bCC], n=8)
        op("vector", lambda e: e.tensor_scalar(out=CC[:, 16:32], in0=V[:, VC_BA:VC_BA + 16], scalar1=0.5, scalar2=None, op0=ALU.mult),
           reads=[bV, bCC], writes=[bCC], n=16)
        op("vector", lambda e: e.tensor_tensor(out=CC[:, 40:44], in0=V[:, VC_PB:VC_PB + 4], in1=V[:, VC_PS:VC_PS + 4], op=ALU.mult),
           reads=[bV, bCC], writes=[bCC], n=4)

        def stat_slot():
            j = st_i[0] % 16
            st_i[0] += 1
            return ST[:, 4 * j:4 * j + 4], bST[j]

        def rstd_from(c, b, cols, width):
            if cols == 2:
                op("gpsimd", lambda e: e.tensor_tensor(out=c[:, 0:1], in0=c[:, 0:1], in1=c[:, 1:2], op=ALU.add), reads=[b], writes=[b], n=1, two=True)
            op("gpsimd", lambda e: e.tensor_scalar(out=c[:, 2:3], in0=c[:, 0:1], scalar1=1.0 / width, scalar2=EPS, op0=ALU.mult, op1=ALU.add),
               reads=[b], writes=[b], n=1)
            op("gpsimd", lambda e: e.tensor_tensor(out=c[:, 3:4], in0=c[:, 2:3], in1=NEGH[:], op=ALU.pow), reads=[b, bNEGH], writes=[b], n=1, two=True)
            return c[:, 3:4]

        with ExitStack() as p1:
            WIN = sb(p1, "WIN", [128, 8, 3 * DL], BF16); bWIN = Buf()
            BD = sb(p1, "BD", [128, 16, 128], BF16); bBD1 = Buf()
            PWt = sb(p1, "PWt", [128, 4, 128], BF16); bPWt = Buf()
            IDF = sb(p1, "IDF", [128, 128], F32); bIDF = Buf()
            DG = sb(p1, "DG", [128, 16, 128], F32); bDG = Buf()
            EC = sb(p1, "EC", [128, 4, 16], F32); bEC = Buf()
            ZER = sb(p1, "ZER", [128, T1], F32); bZER = Buf()
            HCt = sb(p1, "HCt", [128, 4], F32); bHC = [Buf() for _ in range(4)]
            XB = Ring([sb(p1, "XB%d" % i, [128, D], F32) for i in range(4)])
            XN = Ring([sb(p1, "XN%d" % i, [128, D], BF16) for i in range(2)])
            HT = [sb(p1, "HT%d" % i, [128, 8, T1], BF16) for i in range(2)]
            bHT = [[Buf() for _ in range(4)] for _ in range(2)]
            RU = [sb(p1, "RU%d" % i, [128, 4, RW], F32) for i in range(2)]
            RG = [sb(p1, "RG%d" % i, [128, 4, RW], F32) for i in range(2)]
            RP = [sb(p1, "RP%d" % i, [128, 4, RW], F32) for i in range(2)]
            bRU = [[Buf() for _ in range(4)] for _ in range(2)]
            bRG = [[Buf() for _ in range(4)] for _ in range(2)]
            bRP = [[Buf() for _ in range(4)] for _ in range(2)]
            bHALO = [[Buf() for _ in range(2)] for _ in range(3)]
            FU = sb(p1, "FU", [128, 4, 32], F32); FG = sb(p1, "FG", [128, 4, 32], F32); FP = sb(p1, "FP", [128, 4, 32], F32)
            bFU = [Buf() for _ in range(4)]; bFG = [Buf() for _ in range(4)]; bFP = [Buf() for _ in range(4)]
            UU = Ring([sb(p1, "UU%d" % i, [128, T1], F32) for i in range(3)])
            UB = Ring([sb(p1, "UB%d" % i, [128, T1], BF16) for i in range(2)])
            THR = Ring([sb(p1, "THR%d" % i, [128, T1], F32) for i in range(2)])
            THI = [Ring([sb(p1, "THI%d_%d" % (d, i), [128, T1], F32) for i in range(2)]) for d in range(2)]
            AAr = [Ring([sb(p1, "AA%d_%d" % (d, i), [128, T1], F32) for i in range(2)]) for d in range(2)]
            A2r = [Ring([sb(p1, "A2%d_%d" % (d, i), [128, T1], F32) for i in range(2)]) for d in range(2)]
            HFr = Ring([sb(p1, "HF%d" % i, [128, T1], F32) for i in range(2)])
            HLr = Ring([sb(p1, "HLr%d" % i, [128, T1], F32) for i in range(2)])
            PCr = Ring([sb(p1, "PC%d" % i, [128, T1], F32) for i in range(2)])
            AOr = Ring([sb(p1, "AO%d" % i, [128, T1], F32) for i in range(2)])
            SGB = {}
            PS1 = sb(p1, "PS1", [128, RW], F32); bPS1 = Buf()
            PS2 = sb(p1, "PS2", [128, RW], F32); bPS2 = Buf()
            PGr = Ring([sb(p1, "PG%d" % i, [128, T1], BF16) for i in range(2)])
            YPr = Ring([sb(p1, "YP%d" % i, [128, T1], BF16) for i in range(2)])
            TPp = Ring([ps(p1, "TP%d" % i, [128, 8, 128], BF16) for i in range(2)], psum=True)
            WPp = Ring([ps(p1, "WP%d" % i, [128, T1], F32) for i in range(2)], psum=True)
            CVp = Ring([ps(p1, "CV%d" % i, [128, T1], F32) for i in range(1)], psum=True)
            GPp = Ring([ps(p1, "GP%d" % i, [128, T1], F32) for i in range(2)], psum=True)
            PMp = Ring([ps(p1, "PM%d" % i, [128, T1], F32) for i in range(1)], psum=True)

            dma("gpsimd", WIN[:], w_in_d.rearrange("(kc p) f -> p kc f", p=128), writes=[bWIN], sembuf=bWIN, nbytes=6 << 20)
            op("gpsimd", lambda e: e.memset(BD[:], 0.0), writes=[bBD1], n=2048)
            for m, wsrc in enumerate((rg_wa_d, rg_wx_d)):
                for d in range(2):
                    for h in range(8):
                        c, hf_ = h // 2, h % 2
                        idx = m * 8 + d * 4 + c
                        dma("gpsimd", BD[hf_ * 64:(hf_ + 1) * 64, idx, hf_ * 64:(hf_ + 1) * 64], wsrc[d, h],
                            writes=[bBD1], sembuf=bBD1, group=True, nbytes=16384)
            dma("gpsimd", PWt[:], pool_w_d.rearrange("g i j -> i g j"), writes=[bPWt], sembuf=bPWt, nbytes=1 << 18)
            dma("sync", EC[:], ec_d, writes=[bEC], sembuf=bEC)
            op("vector", lambda e: e.tensor_copy(out=IDF[:], in_=IDN[:]), reads=[bIDN], writes=[bIDF], n=128)
            for k in range(4):
                for c in range(4):
                    op("vector", lambda e, k=k, c=c: e.tensor_scalar(out=DG[:, 4 * k + c, :], in0=IDF[:], scalar1=V[:, VC_CW + 4 * k + c:VC_CW + 4 * k + c + 1],
                                                                     scalar2=None, op0=ALU.mult),
                       reads=[bIDF, bV, bDG], writes=[bDG], n=128)
            op("gpsimd", lambda e: e.memset(ZER[:], 0.0), writes=[bZER])
            op("gpsimd", lambda e: e.memset(FU[:], 0.0), writes=bFU, n=128)
            op("gpsimd", lambda e: e.memset(FG[:], 0.0), writes=bFG, n=128)
            op("gpsimd", lambda e: e.memset(FP[:], 0.0), writes=bFP, n=128)
            op("gpsimd", lambda e: e.memset(HH[:], 0.0), writes=[b for row in bHH for b in row], n=120)

            def x_block(i, tb):
                slot = i % 2
                r0 = i * T1 + tb * 128
                xb, bxb = XB.next()
                dma("sync", xb[:], x_d[r0:r0 + 128, :], writes=[bxb], sembuf=bxb, nbytes=1 << 19)
                xn, bxn = XN.next()
                c, bst = stat_slot()
                op("scalar", lambda e: e.activation(out=xn[:], in_=xb[:], func=AF.Square, accum_out=c[:, 0:1]), reads=[bxb], writes=[bst, bxn], n=D)
                rstd = rstd_from(c, bst, 1, D)
                op("vector", lambda e: e.tensor_scalar(out=xn[:], in0=xb[:], scalar1=rstd, scalar2=None, op0=ALU.mult), reads=[bxb, bst], writes=[bxn], n=D)
                tp, btp = TPp.next()
                for cc in range(8):
                    op("tensor", lambda e, cc=cc: e.transpose(out=tp[:, cc, :], in_=xn[:, cc * 128:(cc + 1) * 128], identity=IDN[:]),
                       reads=[bxn, bIDN], writes=[btp], n=128)
                op("vector", lambda e: e.tensor_tensor(out=HT[slot][:, :, tb * 128:(tb + 1) * 128], in0=tp[:],
                                                       in1=V[:, VC_G1:VC_G1 + 8].unsqueeze(2).to_broadcast([128, 8, 128]), op=ALU.mult),
                   reads=[btp, bV], writes=[bHT[slot][tb]], n=D)

            def w_chunk(i, oc):
                slot = i % 2
                k = i % NT1
                kind, c = oc // 4, oc % 4
                R, bR = ((RU, bRU), (RG, bRG), (RP, bRP))[kind]
                if c == 0:
                    if k == 0:
                        op("gpsimd", lambda e: e.memset(R[slot][:, :, 0:HALO], 0.0), writes=bR[slot], n=64)
                    else:
                        dma("sync", R[slot][:, :, 0:HALO], R[1 - slot][:, :, T1:RW], reads=bR[1 - slot], writes=bR[slot],
                            sembuf=bHALO[kind][slot], nbytes=32768)
                wp, bwp = WPp.next()
                for kc in range(8):
                    op("tensor", lambda e, kc=kc: e.matmul(out=wp[:], lhsT=WIN[:, kc, oc * 128:(oc + 1) * 128], rhs=HT[slot][:, kc, :],
                                                           start=(kc == 0), stop=(kc == 7)),
                       reads=[bWIN] + bHT[slot], writes=[bwp], n=T1)
                op("scalar", lambda e: e.activation(out=R[slot][:, c, HALO:RW], in_=wp[:], func=AF.Copy), reads=[bwp], writes=[bR[slot][c]], n=T1)

            def w_gelu(i):
                slot = i % 2
                op("scalar", lambda e: e.activation(out=RG[slot][:, :, HALO:RW], in_=RG[slot][:, :, HALO:RW], func=AF.Gelu_apprx_tanh),
                   reads=bRG[slot], writes=bRG[slot], n=4 * T1, tab="gelu")

            def lru_front(c, Ru, bRu, c0, n):
                u, bu = UU.next()
                cv, bcv = CVp.next()
                for k in range(4):
                    op("tensor", lambda e, k=k: e.matmul(out=cv[:, :n], lhsT=DG[:, 4 * k + c, :], rhs=Ru[:, c, c0 - 2 + k:c0 - 2 + k + n],
                                                         start=(k == 0), stop=(k == 3)),
                       reads=[bDG, bRu[c]], writes=[bcv], n=4 * n)
                op("scalar", lambda e: e.activation(out=u[:, :n], in_=cv[:, :n], func=AF.Identity, bias=V[:, VC_CB + c:VC_CB + c + 1], scale=1.0),
                   reads=[bcv, bV], writes=[bu], n=n)
                ub, bub = UB.next()
                op("vector", lambda e: e.tensor_scalar(out=ub[:, :n], in0=cv[:, :n], scalar1=V[:, VC_CB + c:VC_CB + c + 1], scalar2=None, op0=ALU.add),
                   reads=[bcv, bV], writes=[bub], n=n)
                res = {"u": (u, bu)}

                def per_dir(d):
                    j = d * 4 + c
                    rp, brp = GPp.next()
                    op("tensor", lambda e: e.matmul(out=rp[:, :n], lhsT=BD[:, j, :], rhs=ub[:, :n], start=True, stop=True),
                       reads=[bBD1, bub], writes=[brp], n=n)
                    thr, bthr = THR.next()
                    op("scalar", lambda e: e.activation(out=thr[:, :n], in_=rp[:, :n], func=AF.Tanh, scale=0.5, bias=CC[:, 16 + j:17 + j]),
                       reads=[brp, bCC], writes=[bthr], n=n, tab="tanh")
                    ip, bip = GPp.next()
                    op("tensor", lambda e: e.matmul(out=ip[:, :n], lhsT=BD[:, 8 + j, :], rhs=ub[:, :n], start=True, stop=True),
                       reads=[bBD1, bub], writes=[bip], n=n)
                    thi, bthi = THI[d].next()
                    op("scalar", lambda e: e.activation(out=thi[:, :n], in_=ip[:, :n], func=AF.Tanh, scale=0.5, bias=CC[:, 24 + j:25 + j]),
                       reads=[bip, bCC], writes=[bthi], n=n, tab="tanh")
                    aa, baa = AAr[d].next()
                    op("scalar", lambda e: e.activation(out=aa[:, :n], in_=thr[:, :n], func=AF.Exp, scale=CC[:, 8 + j:9 + j], bias=CC[:, 8 + j:9 + j]),
                       reads=[bthr, bCC], writes=[baa], n=n, tab="exp")
                    a2, ba2 = A2r[d].next()
                    if d == 0:
                        op("scalar", lambda e: e.activation(out=a2[:, :n], in_=thr[:, :n], func=AF.Exp, scale=CC[:, j:j + 1], bias=CC[:, j:j + 1]),
                           reads=[bthr, bCC], writes=[ba2], n=n, tab="exp")
                    else:
                        op("gpsimd", lambda e: e.tensor_tensor(out=a2[:, :n], in0=aa[:, :n], in1=aa[:, :n], op=ALU.mult),
                           reads=[baa], writes=[ba2], n=n, two=True)
                    op("vector", lambda e: e.scalar_tensor_tensor(out=thi[:, :n], in0=thi[:, :n], scalar=1.0, in1=u[:, :n], op0=ALU.add, op1=ALU.mult),
                       reads=[bthi, bu], writes=[bthi], n=1.2 * n)
                    res[d] = ((thi, bthi), (aa, baa), (a2, ba2))
                per_dir(0)
                per_dir(1)
                return res

            def lru_back(c, res, Rg, bRg, c0, n, s, kst, tok0):
                def sq_dir(d):
                    (thi, bthi), (aa, baa), (a2, ba2) = res[d]
                    op("scalar", lambda e: e.activation(out=a2[:, :n], in_=a2[:, :n], func=AF.Sqrt, scale=-0.25, bias=0.25),
                       reads=[ba2], writes=[ba2], n=n, tab="sqrt")
                    op("gpsimd", lambda e: e.tensor_tensor(out=thi[:, :n], in0=thi[:, :n], in1=a2[:, :n], op=ALU.mult),
                       reads=[bthi, ba2], writes=[bthi], n=n, two=True)
                sq_dir(0)
                sq_dir(1)
                (bt0, bbt0), (a0, ba0), _ = res[0]
                (bt1, bbt1), (a1, ba1), _ = res[1]
                hf, bhf = HFr.next()
                op("vector", lambda e: e.tensor_tensor_scan(out=hf[:, :n], data0=a0[:, :n], data1=bt0[:, :n], initial=HCt[:, c:c + 1],
                                                            op0=ALU.mult, op1=ALU.add),
                   reads=[ba0, bbt0, bHC[c]], writes=[bhf], n=2 * n)
                dma("sync", HCt[:, c:c + 1], hf[:, n - 1:n], reads=[bhf], writes=[bHC[c]], sembuf=bHC[c], nbytes=512)
                hl, bhl = HLr.next()
                op("vector", lambda e: e.tensor_tensor_scan(out=hl[:, 0:n][:, ::-1], data0=a1[:, 0:n][:, ::-1], data1=bt1[:, 0:n][:, ::-1],
                                                            initial=0.0, op0=ALU.mult, op1=ALU.add),
                   reads=[ba1, bbt1], writes=[bhl], n=2 * n)
                pc, bpc = PCr.next()
                op("vector", lambda e: e.tensor_tensor_scan(out=pc[:, 0:n][:, ::-1], data0=a1[:, 0:n][:, ::-1], data1=ZER[:, :n],
                                                            initial=1.0, op0=ALU.mult, op1=ALU.add),
                   reads=[ba1, bZER], writes=[bpc], n=2 * n)
                hi = hidx(s, kst)
                ao, bao = AOr.next()
                op("gpsimd", lambda e: e.tensor_tensor(out=ao[:, :n], in0=hf[:, :n], in1=hl[:, :n], op=ALU.add), reads=[bhf, bhl], writes=[bao], n=n, two=True)
                dma("sync", sa_d[s, c, :, tok0:tok0 + n], ao[:, :n], reads=[bao], sembuf=bao, nbytes=n * 512)
                dma("sync", sb_d[s, c, :, tok0:tok0 + n], pc[:, :n], reads=[bpc], sembuf=bpc, nbytes=n * 512)
                dma("sync", HL[:, hi, c:c + 1], hl[:, 0:1], reads=[bhl], writes=[bHL[s][kst]], sembuf=SGB.setdefault(("hl", id(hl)), Buf()), group=True, nbytes=512)
                dma("sync", PPt[:, hi, c:c + 1], pc[:, 0:1], reads=[bpc], writes=[bHL[s][kst]], sembuf=SGB.setdefault(("pp", id(pc)), Buf()), group=True, nbytes=512)
                sgb = SGB.setdefault(id(Rg), [Buf() for _ in range(4)])[c]
                ta = tok0
                while ta < tok0 + n:
                    k2 = ta // T2
                    tb_ = min(tok0 + n, (k2 + 1) * T2)
                    row0 = (s * NT2 + k2) * T2
                    dma("sync", y_d[row0:row0 + 128, c * T2 + (ta - k2 * T2):c * T2 + (tb_ - k2 * T2)],
                        Rg[:, c, c0 + (ta - tok0):c0 + (tb_ - tok0)], reads=[bRg[c]], sembuf=sgb, group=False, nbytes=(tb_ - ta) * 512)
                    ta = tb_

            def pool_group(g, Rp, bRp, c0, n, s, tok0, first, last):
                w = WINS[g]
                half = w // 2
                src = Rp[:, g, :]
                width = c0 + n + 8
                cur, bcur = src, bRp[g]
                tmps = [(PS1, bPS1), (PS2, bPS2)]
                step = 1
                ti = 0
                while step < half:
                    (dst, bdst) = tmps[ti % 2]
                    ti += 1
                    L = width - (2 * step - 1)
                    op("gpsimd", lambda e, cur=cur, dst=dst, step=step, L=L: e.tensor_tensor(out=dst[:, 0:L], in0=cur[:, 0:L], in1=cur[:, step:step + L], op=ALU.add),
                       reads=[bcur], writes=[bdst], n=L, two=True)
                    cur, bcur = dst, bdst
                    step *= 2
                (dst, bdst) = tmps[ti % 2]
                op("gpsimd", lambda e: e.tensor_tensor(out=dst[:, 0:n], in0=cur[:, c0 - half:c0 - half + n], in1=cur[:, c0:c0 + n], op=ALU.add),
                   reads=[bcur], writes=[bdst], n=n, two=True)
                pg, bpg = PGr.next()
                op("vector", lambda e: e.scalar_tensor_tensor(out=pg[:, :n], in0=dst[:, 0:n], scalar=1.0 / w, in1=src[:, c0:c0 + n],
                                                              op0=ALU.mult, op1=ALU.subtract),
                   reads=[bdst, bRp[g]], writes=[bpg], n=1.5 * n)
                if first or last:
                    e0, off = (0, 0) if first else (8, n - 8)
                    op("vector", lambda e: e.tensor_tensor(out=dst[:, off:off + 8], in0=dst[:, off:off + 8], in1=EC[:, g, e0:e0 + 8], op=ALU.mult),
                       reads=[bdst, bEC], writes=[bdst], n=8)
                    op("vector", lambda e: e.tensor_tensor(out=pg[:, off:off + 8], in0=dst[:, off:off + 8], in1=src[:, c0 + off:c0 + off + 8], op=ALU.subtract),
                       reads=[bdst, bRp[g]], writes=[bpg], n=8)
                pm, bpm = PMp.next()
                op("tensor", lambda e: e.matmul(out=pm[:, :n], lhsT=PWt[:, g, :], rhs=pg[:, :n], start=True, stop=True), reads=[bPWt, bpg], writes=[bpm], n=n)
                yp, byp = YPr.next()
                op("scalar", lambda e: e.activation(out=yp[:, :n], in_=pm[:, :n], func=AF.Identity, scale=V[:, VC_PS + g:VC_PS + g + 1],
                                                    bias=CC[:, 40 + g:41 + g]),
                   reads=[bpm, bV, bCC], writes=[byp], n=n)
                dma("sync", sy_d[s, g, :, tok0:tok0 + n], yp[:, :n], reads=[byp], sembuf=byp, nbytes=n * 256)

            def mix_stage(s, kst, Ru, bRu, Rg, bRg, Rp, bRp, c0, n, tok0):
                first = (kst == 0)
                last = (kst == NT1)
                fr = {}
                fr[0] = lru_front(0, Ru, bRu, c0, n)
                fr[1] = lru_front(1, Ru, bRu, c0, n)
                for c in range(4):
                    lru_back(c, fr[c], Rg, bRg, c0, n, s, kst, tok0)
                    if c + 2 < 4:
                        fr[c + 2] = lru_front(c + 2, Ru, bRu, c0, n)
                    pool_group(c, Rp, bRp, c0, n, s, tok0, first, last)

            NTILES = NSEQ * NT1
            for tb in range(4):
                x_block(0, tb)
            for oc in range(12):
                w_chunk(0, oc)
            w_gelu(0)
            for tb in range(4):
                x_block(1, tb)
            for i in range(NTILES):
                s, k = i // NT1, i % NT1
                slot = i % 2
                if k == 0:
                    for c in range(4):
                        op("gpsimd", lambda e, c=c: e.memset(HCt[:, c:c + 1], 0.0), writes=[bHC[c]], n=1)
                if i + 1 < NTILES:
                    for oc in range(12):
                        w_chunk(i + 1, oc)
                    w_gelu(i + 1)
                if i + 2 < NTILES:
                    for tb in range(4):
                        x_block(i + 2, tb)
                if k == 0:
                    mix_stage(s, 0, RU[slot], bRU[slot], RG[slot], bRG[slot], RP[slot], bRP[slot], HALO, T1 - 8, 0)
                else:
                    mix_stage(s, k, RU[slot], bRU[slot], RG[slot], bRG[slot], RP[slot], bRP[slot], 8, T1, k * T1 - 8)
                if k == NT1 - 1:
                    for c in range(4):
                        op("gpsimd", lambda e, c=c, slot=slot: e.tensor_copy(out=FU[:, c, 0:HALO], in_=RU[slot][:, c, T1:RW]), reads=[bRU[slot][c]], writes=[bFU[c]], n=16)
                        op("gpsimd", lambda e, c=c, slot=slot: e.tensor_copy(out=FG[:, c, 0:HALO], in_=RG[slot][:, c, T1:RW]), reads=[bRG[slot][c]], writes=[bFG[c]], n=16)
                        op("gpsimd", lambda e, c=c, slot=slot: e.tensor_copy(out=FP[:, c, 0:HALO], in_=RP[slot][:, c, T1:RW]), reads=[bRP[slot][c]], writes=[bFP[c]], n=16)
                    mix_stage(s, NT1, FU, bFU, FG, bFG, FP, bFP, 8, 8, SEQ - 8)
                    for kk in range(NT1, -1, -1):
                        hi, hn = hidx(s, kk), hidx(s, kk + 1)
                        op("vector", lambda e, hi=hi, hn=hn: e.tensor_tensor(out=HH[:, hi, :], in0=PPt[:, hi, :], in1=HH[:, hn, :], op=ALU.mult),
                           reads=[bHL[s][kk], bHH[s][kk + 1]], writes=[bHH[s][kk]], n=4)
                        op("vector", lambda e, hi=hi: e.tensor_tensor(out=HH[:, hi, :], in0=HH[:, hi, :], in1=HL[:, hi, :], op=ALU.add),
                           reads=[bHL[s][kk], bHH[s][kk]], writes=[bHH[s][kk]], n=4)
            S.flush()

        with ExitStack() as p2:
            WOUT = sb(p2, "WOUT", [128, 8, D], BF16); bWOUT = Buf()
            dma("gpsimd", WOUT[:], w_out_d.rearrange("(kc p) f -> p kc f", p=128), writes=[bWOUT], sembuf=bWOUT, nbytes=4 << 20)
            WG = sb(p2, "WG", [128, 8, DFF], BF16); bWGp = [Buf() for _ in range(4)]
            WU = sb(p2, "WU", [128, 8, DFF], BF16); bWUp = [Buf() for _ in range(4)]
            WD = sb(p2, "WD", [128, NFF, D], BF16); bWDp = [Buf() for _ in range(4)]
            JP = (0, 4, 10, 16, NFF)

            def piece_of(jf):
                return max(q for q in range(4) if JP[q] <= jf)
            G2 = sb(p2, "G2", [128, D], F32); bG2 = Buf()
            G4 = sb(p2, "G4", [128, D], F32); bG4 = Buf()
            X2 = [sb(p2, "X2_%d" % i, [128, 2, D], F32) for i in range(2)]
            bX2 = [[Buf() for _ in range(2)] for _ in range(2)]
            At = sb(p2, "At", [128, 4, T2], F32); bAt = Buf()
            Bt = sb(p2, "Bt", [128, 4, T2], F32); bBt = Buf()
            Gt = sb(p2, "Gt", [128, 4, T2], F32); bGt = Buf()
            YT = sb(p2, "YT", [128, 8, T2], BF16); bYTl = Buf(); bYTp = Buf()
            H2T = sb(p2, "H2T", [128, 8, T2], BF16); bH2T = [Buf() for _ in range(2)]
            XN2 = Ring([sb(p2, "XN2_%d" % i, [128, D], BF16) for i in range(2)])
            TMP = Ring([sb(p2, "TMP%d" % i, [128, D], F32) for i in range(1)])
            SG = Ring([sb(p2, "SG%d" % i, [128, T2], F32) for i in range(2)])
            ACr = Ring([sb(p2, "AC%d" % i, [128, T2], BF16) for i in range(5)])
            MF = [ps(p2, "MF%d" % i, [128, D], F32) for i in range(2)]; bMF = [Buf(psum=True) for _ in range(2)]
            GUp = Ring([ps(p2, "GU%d" % i, [128, 2, T2], F32) for i in range(3)], psum=True)
            AUX = ps(p2, "AUX", [128, 512], F32); bAUX = Buf(psum=True)
            AUXT = AUX[:].bitcast(BF16).rearrange("p (c t) -> p c t", c=8)

            dma("sync", G2[:], g2_d.partition_broadcast(128), writes=[bG2], sembuf=bG2, nbytes=1 << 19)
            dma("sync", G4[:], g4_d.partition_broadcast(128), writes=[bG4], sembuf=bG4, nbytes=1 << 19)
            wg_v = w_gate_d.rearrange("(kc p) f -> p kc f", p=128)
            wu_v = w_up_d.rearrange("(kc p) f -> p kc f", p=128)
            wd_v = w_down_d.rearrange("(j p) f -> p j f", p=128)
            for q in range(4):
                f0, f1 = JP[q] * 128, JP[q + 1] * 128
                nb = (f1 - f0) * 4096
                dma("gpsimd", WG[:, :, f0:f1], wg_v[:, :, f0:f1], reads=[bWOUT], writes=[bWGp[q]], sembuf=bWGp[q], nbytes=nb)
                dma("gpsimd", WU[:, :, f0:f1], wu_v[:, :, f0:f1], reads=[bWOUT], writes=[bWUp[q]], sembuf=bWUp[q], nbytes=nb)
                dma("gpsimd", WD[:, JP[q]:JP[q + 1], :], wd_v[:, JP[q]:JP[q + 1], :], reads=[bWOUT], writes=[bWDp[q]], sembuf=bWDp[q], nbytes=nb)

            NT = NSEQ * NT2

            def tile_info(i2):
                s, j = i2 // NT2, i2 % NT2
                return s, j, i2 % 2, j * T2, s * SEQ + j * T2

            def front_pieces(i2):
                s, j, slot, t0_, r0 = tile_info(i2)
                pieces = []

                def loads():
                    dma("sync", X2[slot][:], x_d[r0:r0 + T2, :].rearrange("(b p) f -> p b f", p=128), writes=bX2[slot], sembuf=bX2[slot][0], nbytes=1 << 20)
                    dma("sync", At[:], sa_d[s, :, :, t0_:t0_ + T2].rearrange("c p t -> p c t"), writes=[bAt], sembuf=bAt, nbytes=1 << 19)
                    dma("sync", Bt[:], sb_d[s, :, :, t0_:t0_ + T2].rearrange("c p t -> p c t"), writes=[bBt], sembuf=bBt, nbytes=1 << 19)
                    dma("sync", Gt[:], y_d[r0:r0 + 128, :].rearrange("p (c t) -> p c t", c=4), writes=[bGt], sembuf=bGt, nbytes=1 << 19)
                    dma("sync", YT[:, 4:8, :], sy_d[s, :, :, t0_:t0_ + T2].rearrange("c p t -> p c t"), writes=[bYTp], sembuf=bYTp, nbytes=1 << 18)
                pieces.append(loads)

                def ylru():
                    kst = j // 2
                    for c in range(4):
                        segs = [(0, T2, kst + 1)] if j % 2 == 0 else [(0, T2 - 8, kst + 1), (T2 - 8, T2, kst + 2)]
                        for (a_, b_, kh) in segs:
                            hi = hidx(s, kh)
                            op("vector", lambda e, c=c, a_=a_, b_=b_, hi=hi: e.scalar_tensor_tensor(
                                out=At[:, c, a_:b_], in0=Bt[:, c, a_:b_], scalar=HH[:, hi, c:c + 1], in1=At[:, c, a_:b_], op0=ALU.mult, op1=ALU.add),
                               reads=[bAt, bBt, bHH[s][kh]], writes=[bAt], n=1.2 * (b_ - a_))
                    op("vector", lambda e: e.tensor_tensor(out=YT[:, 0:4, :], in0=At[:], in1=Gt[:], op=ALU.mult), reads=[bAt, bGt], writes=[bYTl], n=4 * T2)
                pieces.append(ylru)

                state = {}

                def wout_half(tb, hf_):
                    def f():
                        if hf_ == 0:
                            state[tb] = (stat_slot(), TMP.next(), XN2.next())
                        (c, bst), (tmp, btmp), (xn, bxn) = state[tb]
                        for kc in range(8):
                            op("tensor", lambda e, kc=kc: e.matmul(out=AUX[:], lhsT=YT[:, kc, tb * 128:(tb + 1) * 128],
                                                                   rhs=WOUT[:, kc, hf_ * 512:(hf_ + 1) * 512], start=(kc == 0), stop=(kc == 7)),
                               reads=[bYTl, bYTp, bWOUT], writes=[bAUX], n=512)
                        op("scalar", lambda e: e.activation(out=xn[:, hf_ * 512:(hf_ + 1) * 512], in_=AUX[:], func=AF.Square, accum_out=c[:, hf_:hf_ + 1]),
                           reads=[bAUX], writes=[bst, bxn], n=512)
                        op("vector", lambda e: e.tensor_tensor(out=tmp[:, hf_ * 512:(hf_ + 1) * 512], in0=AUX[:], in1=G2[:, hf_ * 512:(hf_ + 1) * 512], op=ALU.mult),
                           reads=[bAUX, bG2], writes=[btmp], n=512)
                    return f

                def norm_block(tb):
                    def f():
                        (c, bst), (tmp, btmp), (xn, bxn) = state[tb]
                        xrow, bx = X2[slot][:, tb, :], bX2[slot][tb]
                        rstd = rstd_from(c, bst, 2, D)
                        op("vector", lambda e: e.scalar_tensor_tensor(out=xrow, in0=tmp[:], scalar=rstd, in1=xrow, op0=ALU.mult, op1=ALU.add),
                           reads=[btmp, bst, bx], writes=[bx], n=1.2 * D)
                        c2, bst2 = stat_slot()
                        op("scalar", lambda e: e.activation(out=xn[:], in_=xrow, func=AF.Square, accum_out=c2[:, 0:1]), reads=[bx], writes=[bst2, bxn], n=D)
                        rstd2 = rstd_from(c2, bst2, 1, D)
                        op("scalar", lambda e: e.activation(out=xn[:], in_=xrow, func=AF.Copy, scale=rstd2), reads=[bx, bst2], writes=[bxn], n=D)
                    return f

                for tb in range(2):
                    pieces.append(wout_half(tb, 0))
                    pieces.append(wout_half(tb, 1))
                    pieces.append(norm_block(tb))

                def transposes():
                    for tb in range(2):
                        (c, bst), (tmp, btmp), (xn, bxn) = state[tb]
                        if tb == 0:
                            tpv, btpv = AUXT, bAUX
                        else:
                            gu, btpv = GUp.next()
                            tpv = gu[:].rearrange("p a t -> p (a t)").bitcast(BF16).rearrange("p (c t) -> p c t", c=8)
                        for cc in range(8):
                            op("tensor", lambda e, cc=cc, xn=xn, tpv=tpv: e.transpose(out=tpv[:, cc, :], in_=xn[:, cc * 128:(cc + 1) * 128], identity=IDN[:]),
                               reads=[bxn, bIDN], writes=[btpv], n=128)
                        op("vector", lambda e, tb=tb, tpv=tpv: e.tensor_tensor(out=H2T[:, :, tb * 128:(tb + 1) * 128], in0=tpv,
                                                                               in1=V[:, VC_G3:VC_G3 + 8].unsqueeze(2).to_broadcast([128, 8, 128]), op=ALU.mult),
                           reads=[btpv, bV], writes=[bH2T[tb]], n=D)
                return pieces, transposes

            def down(i2, jf, ac, bac):
                for tb in range(2):
                    for hf_ in range(2):
                        op("tensor", lambda e, tb=tb, hf_=hf_: e.matmul(
                            out=MF[tb][:, hf_ * 512:(hf_ + 1) * 512], lhsT=ac[:, tb * 128:(tb + 1) * 128],
                            rhs=WD[:, jf, hf_ * 512:(hf_ + 1) * 512], start=(jf == 0), stop=(jf == NFF - 1)),
                           reads=[bac, bWDp[piece_of(jf)]], writes=[bMF[tb]], n=512)

            def final_block(i2, tb):
                s, j, slot, t0_, r0 = tile_info(i2)
                xrow, bx = X2[slot][:, tb, :], bX2[slot][tb]
                c, bst = stat_slot()
                tmp, btmp = TMP.next()
                op("scalar", lambda e: e.activation(out=tmp[:].bitcast(BF16)[:, 0:D], in_=MF[tb][:], func=AF.Square, accum_out=c[:, 0:1]),
                   reads=[bMF[tb]], writes=[bst, btmp], n=D)
                op("vector", lambda e: e.tensor_tensor(out=tmp[:], in0=MF[tb][:], in1=G4[:], op=ALU.mult), reads=[bMF[tb], bG4], writes=[btmp], n=D)
                rstd = rstd_from(c, bst, 1, D)
                op("vector", lambda e: e.scalar_tensor_tensor(out=xrow, in0=tmp[:], scalar=rstd, in1=xrow, op0=ALU.mult, op1=ALU.add),
                   reads=[btmp, bst, bx], writes=[bx], n=1.2 * D)

            pieces, transposes = front_pieces(0)
            for p in pieces:
                p()
            transposes()
            for i2 in range(NT):
                s, j, slot, t0_, r0 = tile_info(i2)
                if i2 + 1 < NT:
                    nxt_pieces, nxt_transposes = front_pieces(i2 + 1)
                else:
                    nxt_pieces, nxt_transposes = [], None
                at = {0: 0, 1: 1, 3: 2, 5: 3, 7: 4, 10: 5, 12: 6, 14: 7}
                prevq = []
                for jf in range(NFF):
                    gu, bgu = GUp.next()
                    for kc in range(8):
                        op("tensor", lambda e, kc=kc, gu=gu, jf=jf: e.matmul(out=gu[:, 0, :], lhsT=WG[:, kc, jf * 128:(jf + 1) * 128], rhs=H2T[:, kc, :],
                                                                            start=(kc == 0), stop=(kc == 7)),
                           reads=[bWGp[piece_of(jf)]] + bH2T, writes=[bgu], n=T2)
                    for kc in range(8):
                        op("tensor", lambda e, kc=kc, gu=gu, jf=jf: e.matmul(out=gu[:, 1, :], lhsT=WU[:, kc, jf * 128:(jf + 1) * 128], rhs=H2T[:, kc, :],
                                                                            start=(kc == 0), stop=(kc == 7)),
                           reads=[bWUp[piece_of(jf)]] + bH2T, writes=[bgu], n=T2)
                    if len(prevq) >= 3:
                        down(i2, *prevq.pop(0))
                    if jf in at and at[jf] < len(nxt_pieces):
                        nxt_pieces[at[jf]]()
                    sg, bsg = SG.next()
                    op("scalar", lambda e, sg=sg, gu=gu: e.activation(out=sg[:], in_=gu[:, 0, :], func=AF.Silu), reads=[bgu], writes=[bsg], n=T2, tab="silu")
                    ac, bac = ACr.next()
                    op("vector", lambda e, sg=sg, gu=gu, ac=ac: e.tensor_tensor(out=ac[:], in0=sg[:], in1=gu[:, 1, :], op=ALU.mult), reads=[bsg, bgu], writes=[bac], n=T2)
                    prevq.append((jf, ac, bac))
                down(i2, *prevq.pop(0))
                if nxt_transposes is not None:
                    nxt_transposes()
                down(i2, *prevq.pop(0))
                down(i2, *prevq.pop(0))
                for tb in range(2):
                    final_block(i2, tb)
                dma("sync", y_d[r0:r0 + T2, :].rearrange("(b p) f -> p b f", p=128), X2[slot][:], reads=bX2[slot], sembuf=bX2[slot][1], nbytes=1 << 20)
            S.flush(final_engines=["sync"])
        print("[kernel] ninst=%d nwaits=%d nsems=%d sim_us=%.0f" % (S.ninst, S.nwaits, len(S.sems), S.sim_time / 1e3))
    return nc


_NC_CACHE = {}


def _cols(v, n):
    return np.ascontiguousarray(np.asarray(v, np.float32).reshape(n, 128).T)


def kernel(x_prompt, x_sample, pre_norm_mix, post_norm_mix, w_in, conv_w, conv_b, rg_wa, rg_ba, rg_wx, rg_bx,
           rg_lam, pool_w, pool_b, pool_scale, w_out, pre_norm_ffn, post_norm_ffn, w_gate, w_up, w_down):
    f = lambda a: np.ascontiguousarray(np.asarray(a, dtype=np.float32))
    x_prompt, x_sample = f(x_prompt), f(x_sample)
    vecs = np.concatenate([
        _cols(conv_w[0], 16), _cols(conv_b[0], 4), _cols(rg_ba[0], 8), _cols(rg_bx[0], 8), _cols(rg_lam[0], 8),
        _cols(pool_b[0], 4), _cols(pool_scale[0], 4), _cols(pre_norm_mix[0], 8), _cols(pre_norm_ffn[0], 8)], axis=1)
    assert vecs.shape == (128, NV)
    ident = np.eye(128, dtype=np.float32).astype(ml_dtypes.bfloat16)
    ec = np.zeros((128, 4, 16), np.float32)
    for g, w in enumerate(WINS):
        half = w // 2
        for e in range(16):
            t = e if e < 8 else SEQ - 16 + e
            cnt = min(t + half, SEQ) - max(t - half, 0)
            ec[:, g, e] = 1.0 / cnt
    common = {
        "vecs": np.ascontiguousarray(vecs), "g2": f(post_norm_mix[0]).reshape(1, D), "g4": f(post_norm_ffn[0]).reshape(1, D),
        "ident": ident, "ec": ec, "w_in": f(w_in[0]), "rg_wa": f(rg_wa[0]), "rg_wx": f(rg_wx[0]), "pool_w": f(pool_w[0]),
        "w_out": f(w_out[0]), "w_gate": f(w_gate[0]), "w_up": f(w_up[0]), "w_down": f(w_down[0]),
    }
    in_maps = []
    for c in range(NCORES):
        xs = np.concatenate([x_prompt[2 * c].reshape(SEQ, D), x_prompt[2 * c + 1].reshape(SEQ, D), x_sample[c].reshape(SEQ, D)], axis=0)
        m = dict(common)
        m["x"] = np.ascontiguousarray(xs)
        in_maps.append(m)
    if "nc" not in _NC_CACHE:
        _NC_CACHE["nc"] = build_program()
    nc = _NC_CACHE["nc"]
    res = run_bass_kernel_spmd(nc, in_maps, core_ids=list(range(NCORES)))
    y_prompt = np.empty_like(x_prompt)
    y_sample = np.empty_like(x_sample)
    for c in range(NCORES):
        y = np.asarray(res.results[c]["y"], dtype=np.float32).reshape(NSEQ, SEQ, D)
        y_prompt[2 * c] = y[0]
        y_prompt[2 * c + 1] = y[1]
        y_sample[c] = y[2]
    return (y_prompt, y_sample)
```

```python
import numpy as np
import ml_dtypes
from contextlib import ExitStack
import concourse.bass as bass
import concourse.mybir as mybir
from concourse.bass_utils import run_bass_kernel_spmd

F32 = mybir.dt.float32
BF16 = mybir.dt.bfloat16
F32R = mybir.dt.float32r
AF = mybir.ActivationFunctionType
ALU = mybir.AluOpType

NCORES = 8
D = 1024
SEQ = 4096
NSEQ = 3
NTOK = NSEQ * SEQ
DL = 512
DFF = 2816
NFF = DFF // 128
T1 = 512
HALO = 16
RW = T1 + HALO
NT1 = SEQ // T1
NSTG = NT1 + 1
T2 = 256
NT2 = SEQ // T2
EPS = 1e-6
WINS = (2, 4, 8, 16)

VC_CW, VC_CB, VC_BA, VC_BX, VC_LAM, VC_PB, VC_PS, VC_G1, VC_G3, NV = 0, 16, 20, 28, 36, 44, 48, 52, 60, 68


VERBOSE = False


class Buf:
    __slots__ = ("name", "w", "r", "dsem", "grp", "psum", "gbase")

    def __init__(self, name="", psum=False):
        self.name = name
        self.psum = psum
        self.w = None
        self.r = []
        self.grp = []
        self.gbase = set()
        self.dsem = None


class Node:
    __slots__ = ("id", "eng", "kind", "fn", "out", "in_", "sem", "deps", "cost", "lat", "start", "fin", "val", "clock", "tab")


class Sched:
    ENGS = ("tensor", "vector", "scalar", "gpsimd", "sync")

    def __init__(self, nc, es):
        self.nc = nc
        self.es = es
        self.sems = []
        self.semcount = []
        self.engsem = {}
        self.known = {}
        for n in self.ENGS:
            h = es.enter_context(nc.semaphore("e_" + n))
            self.sems.append(h)
            self.semcount.append(0)
            self.engsem[n] = len(self.sems) - 1
            self.known[n] = {}
        self.nodes = []
        self.done = {}
        self.nid = 0
        self.nwaits = 0
        self.ninst = 0
        self.sim_time = 0.0

    @staticmethod
    def _deps(reads, writes):
        deps = set()
        for b in reads:
            if b.w is not None:
                deps.add(b.w)
            deps.update(b.grp)
            if b.psum:
                deps.update(b.r)
        for b in writes:
            if b.w is not None:
                deps.add(b.w)
            deps.update(b.grp)
            deps.update(b.r)
        return deps

    def _node(self, eng, kind, deps, cost, lat):
        n = Node()
        n.id = self.nid
        self.nid += 1
        n.eng, n.kind, n.deps, n.cost, n.lat = eng, kind, deps, cost, lat
        n.fn = n.out = n.in_ = n.sem = None
        n.tab = None
        self.nodes.append(n)
        self.ninst += 1
        return n

    def op(self, eng, fn, reads=(), writes=(), n=512, two=False, tab=None):
        if eng == "tensor":
            cost = 8.0 + n / 2.35
        elif eng == "scalar":
            cost = 200.0 + 0.78 * n
        elif eng == "vector":
            cost = 100.0 + 1.25 * n
        else:
            cost = 400.0 + (2.3 if two else 0.95) * n
        n = self._node(eng, "op", self._deps(reads, writes), cost, cost)
        n.fn = fn
        n.tab = tab
        n.sem = self.engsem[eng]
        for b in writes:
            b.w = n.id
            b.r = []
            b.grp = []
        for b in reads:
            if b.w != n.id:
                b.r.append(n.id)
        return n.id

    def dma(self, q, out, in_, reads=(), writes=(), sembuf=None, group=False, nbytes=65536):
        if sembuf.dsem is None:
            h = self.es.enter_context(self.nc.semaphore("d%d" % len(self.sems)))
            self.sems.append(h)
            self.semcount.append(0)
            sembuf.dsem = len(self.sems) - 1
        deps = self._deps(reads, writes)
        if group:
            for b in writes:
                if not b.grp:
                    b.gbase = set(deps)
                deps -= set(b.grp)
                deps |= b.gbase
        issue = 1000.0 if q == "gpsimd" else 60.0
        n = self._node(q, "dma", deps, issue, 2000.0 + nbytes / 150.0)
        n.out, n.in_, n.sem = out, in_, sembuf.dsem
        for b in writes:
            if group:
                b.grp.append(n.id)
            else:
                b.grp = []
                b.r = []
            b.w = n.id
        for b in reads:
            b.r.append(n.id)
        return n.id

    def flush(self, final_engines=None):
        import heapq
        nodes = self.nodes
        self.nodes = []
        byid = {n.id: n for n in nodes}
        ndep = {}
        users = {}
        ready_t = {}
        for n in nodes:
            cnt = 0
            for d in n.deps:
                if d in byid:
                    cnt += 1
                    users.setdefault(d, []).append(n.id)
            ndep[n.id] = cnt
            ready_t[n.id] = 0.0
        free = {e: 0.0 for e in self.ENGS}
        pend = {e: [] for e in self.ENGS}
        avail = {e: [] for e in self.ENGS}
        for n in nodes:
            if ndep[n.id] == 0:
                heapq.heappush(pend[n.eng], (0.0, n.id))
        order = []
        remaining = len(nodes)
        SEMLAT = 150.0
        cur_tab = [None]
        while remaining:
            best = None
            for e in self.ENGS:
                if avail[e]:
                    t = free[e]
                elif pend[e]:
                    t = max(free[e], pend[e][0][0])
                else:
                    continue
                if best is None or t < best[0]:
                    best = (t, e)
            t, e = best
            while pend[e] and pend[e][0][0] <= t:
                rt, i = heapq.heappop(pend[e])
                heapq.heappush(avail[e], i)
            extra = 0.0
            if e == "scalar":
                lst = avail[e]
                same = [i for i in lst if byid[i].tab is None or byid[i].tab == cur_tab[0]]
                nsq = sum(1 for i in lst if byid[i].tab == "sqrt")
                if cur_tab[0] != "sqrt" and 0 < nsq < 8:
                    held = [i for i in lst if byid[i].tab == "sqrt" and t - ready_t[i] < 15000.0]
                    rest = [i for i in lst if i not in held]
                    if rest:
                        lst2 = rest
                    else:
                        lst2 = lst
                else:
                    lst2 = lst
                oldest = min(lst2)
                if same and not (byid[oldest].tab not in (None, cur_tab[0]) and t - ready_t[oldest] > 25000.0):
                    i = min(same)
                else:
                    i = oldest
                lst.remove(i)
                heapq.heapify(lst)
                if byid[i].tab is not None and byid[i].tab != cur_tab[0]:
                    cur_tab[0] = byid[i].tab
                    extra = 1300.0
            else:
                i = heapq.heappop(avail[e])
            n = byid[i]
            n.start = t + extra
            t = t + extra
            free[e] = t + n.cost
            n.fin = t + n.lat
            order.append(n)
            remaining -= 1
            for u in users.get(i, ()):
                ndep[u] -= 1
                if ready_t[u] < n.fin + SEMLAT:
                    ready_t[u] = n.fin + SEMLAT
                if ndep[u] == 0:
                    un = byid[u]
                    heapq.heappush(pend[un.eng], (ready_t[u], u))
        self.sim_time += max([n.fin for n in order] + [0.0])
        if VERBOSE:
            mk = max([n.fin for n in order] + [0.0])
            busy = {e: sum(n.cost for n in order if n.eng == e) for e in self.ENGS}
            print("[sched] phase makespan %.0f us; busy " % (mk / 1e3) + " ".join("%s=%.0f%%" % (e[:4], 100 * busy[e] / mk) for e in self.ENGS))
        prog = {e: [] for e in self.ENGS}
        for n in order:
            E = n.eng
            known = self.known[E]
            need = {}
            for d in n.deps:
                if d in byid:
                    dn = byid[d]
                    ev = (dn.sem, dn.val, dn.clock)
                else:
                    ev = self.done[d]
                sem, val, clock = ev
                if E == "tensor" and sem == self.engsem["tensor"]:
                    continue
                if known.get(sem, 0) >= val:
                    continue
                if need.get(sem, (0, None))[0] < val:
                    need[sem] = (val, clock)
            for sem, (val, clock) in need.items():
                if known.get(sem, 0) >= val:
                    continue
                prog[E].append(("wait", sem, val))
                self.nwaits += 1
                for s2, v2 in clock.items():
                    if known.get(s2, 0) < v2:
                        known[s2] = v2
                known[sem] = val
            if n.kind == "op":
                self.semcount[n.sem] += 1
                prog[E].append(("op", n.fn))
            else:
                self.semcount[n.sem] += 16
                prog[E].append(("dma", n.out, n.in_, n.sem))
            n.val = self.semcount[n.sem]
            n.clock = dict(known)
        for n in order:
            self.done[n.id] = (n.sem, n.val, n.clock)
            n.fn = n.out = n.in_ = None
        for e in (final_engines or self.ENGS):
            known = self.known[e]
            for sem, cnt in enumerate(self.semcount):
                if cnt > 0 and known.get(sem, 0) < cnt and not (e == "tensor" and sem == self.engsem["tensor"]):
                    prog[e].append(("wait", sem, cnt))
                    known[sem] = cnt
        sems = self.sems
        with self.nc.Block() as block:
            def runner(e):
                def run(h):
                    mysem = sems[self.engsem[e]]
                    for item in prog[e]:
                        if item[0] == "wait":
                            h.wait_ge(sems[item[1]], item[2])
                        elif item[0] == "op":
                            item[1](h).then_inc(mysem, 1)
                        else:
                            h.dma_start(out=item[1], in_=item[2]).then_inc(sems[item[3]], 16)
                return run
            block.tensor(runner("tensor"))
            block.vector(runner("vector"))
            block.scalar(runner("scalar"))
            block.gpsimd(runner("gpsimd"))
            block.sync(runner("sync"))


class Ring:
    def __init__(self, tiles, psum=False):
        self.tiles = tiles
        self.bufs = [Buf(psum=psum) for _ in tiles]
        self.i = 0

    def next(self):
        j = self.i % len(self.tiles)
        self.i += 1
        return self.tiles[j], self.bufs[j]


def build_program():
    nc = bass.Bass("TRN2", target_bir_lowering=False)
    dt = lambda name, shape, dtype, kind: nc.dram_tensor(name, shape, dtype, kind=kind).ap()
    x_d = dt("x", [NTOK, D], F32, "ExternalInput")
    vecs_d = dt("vecs", [128, NV], F32, "ExternalInput")
    g2_d = dt("g2", [1, D], F32, "ExternalInput")
    g4_d = dt("g4", [1, D], F32, "ExternalInput")
    ident_d = dt("ident", [128, 128], BF16, "ExternalInput")
    ec_d = dt("ec", [128, 4, 16], F32, "ExternalInput")
    w_in_d = dt("w_in", [D, 3 * DL], F32, "ExternalInput")
    rg_wa_d = dt("rg_wa", [2, 8, 64, 64], F32, "ExternalInput")
    rg_wx_d = dt("rg_wx", [2, 8, 64, 64], F32, "ExternalInput")
    pool_w_d = dt("pool_w", [4, 128, 128], F32, "ExternalInput")
    w_out_d = dt("w_out", [D, D], F32, "ExternalInput")
    w_gate_d = dt("w_gate", [D, DFF], F32, "ExternalInput")
    w_up_d = dt("w_up", [D, DFF], F32, "ExternalInput")
    w_down_d = dt("w_down", [DFF, D], F32, "ExternalInput")
    y_d = dt("y", [NTOK, D], F32, "ExternalOutput")
    sa_d = dt("stash_a", [NSEQ, 4, 128, SEQ], F32, "Internal")
    sb_d = dt("stash_b", [NSEQ, 4, 128, SEQ], F32, "Internal")
    sy_d = dt("stash_y", [NSEQ, 4, 128, SEQ], BF16, "Internal")

    with ExitStack() as es:
        S = Sched(nc, es)
        op, dma = S.op, S.dma

        def sb(ctx, name, shape, dtype):
            return ctx.enter_context(nc.sbuf_tensor(name, shape, dtype))

        def ps(ctx, name, shape, dtype):
            return ctx.enter_context(nc.psum_tensor(name, shape, dtype))

        V = sb(es, "V", [128, NV], F32); bV = Buf()
        CC = sb(es, "CC", [128, 44], F32); bCC = Buf()
        IDN = sb(es, "IDN", [128, 128], BF16); bIDN = Buf()
        NEGH = sb(es, "NEGH", [128, 1], F32); bNEGH = Buf()
        HL = sb(es, "HL", [128, NSEQ * (NSTG + 1), 4], F32)
        PPt = sb(es, "PPt", [128, NSEQ * (NSTG + 1), 4], F32)
        HH = sb(es, "HH", [128, NSEQ * (NSTG + 1), 4], F32)
        bHL = [[Buf() for _ in range(NSTG + 1)] for _ in range(NSEQ)]
        bHH = [[Buf() for _ in range(NSTG + 1)] for _ in range(NSEQ)]
        ST = sb(es, "ST", [128, 64], F32)
        bST = [Buf() for _ in range(16)]
        st_i = [0]

        def hidx(s, k):
            return s * (NSTG + 1) + k

        dma("sync", V[:], vecs_d, writes=[bV], sembuf=bV)
        dma("sync", IDN[:], ident_d, writes=[bIDN], sembuf=bIDN)
        op("gpsimd", lambda e: e.memset(NEGH[:], -0.5), writes=[bNEGH], n=1)

        lam = V[:, VC_LAM:VC_LAM + 8]
        t0, t1, t2 = CC[:, 32:40], CC[:, 0:8], CC[:, 8:16]
        op("vector", lambda e: e.tensor_scalar(out=t0, in0=lam, scalar1=-1.0, scalar2=None, op0=ALU.mult), reads=[bV], writes=[bCC], n=8)
        op("scalar", lambda e: e.activation(out=t1, in_=t0, func=AF.Abs), reads=[bCC], writes=[bCC], n=8)
        op("scalar", lambda e: e.activation(out=t1, in_=t1, func=AF.Exp, scale=-1.0), reads=[bCC], writes=[bCC], n=8, tab="ln")
        op("scalar", lambda e: e.activation(out=t1, in_=t1, func=AF.Ln, bias=1.0), reads=[bCC], writes=[bCC], n=8, tab="ln")
        op("vector", lambda e: e.tensor_scalar(out=t0, in0=t0, scalar1=0.0, scalar2=None, op0=ALU.max), reads=[bCC], writes=[bCC], n=8)
        op("vector", lambda e: e.tensor_tensor(out=t0, in0=t0, in1=t1, op=ALU.add), reads=[bCC], writes=[bCC], n=8)
        op("vector", lambda e: e.tensor_scalar(out=t1, in0=t0, scalar1=-8.0, scalar2=None, op0=ALU.mult), reads=[bCC], writes=[bCC], n=8)
        op("vector", lambda e: e.tensor_scalar(out=t2, in0=t0, scalar1=-4.0, scalar2=None, op0=ALU.mult), reads=[bCC], writes=[bCC], n=8)
        op("vector", lambda e: e.tensor_scalar(out=CC[:, 16:32], in0=V[:, VC_BA:VC_BA + 16], scalar1=0.5, scalar2=None, op0=ALU.mult),
           reads=[bV, bCC], writes=[bCC], n=16)
        op("vector", lambda e: e.tensor_tensor(out=CC[:, 40:44], in0=V[:, VC_PB:VC_PB + 4], in1=V[:, VC_PS:VC_PS + 4], op=ALU.mult),
           reads=[bV, bCC], writes=[bCC], n=4)

        def stat_slot():
            j = st_i[0] % 16
            st_i[0] += 1
            return ST[:, 4 * j:4 * j + 4], bST[j]

        def rstd_from(c, b, cols, width):
            if cols == 2:
                op("gpsimd", lambda e: e.tensor_tensor(out=c[:, 0:1], in0=c[:, 0:1], in1=c[:, 1:2], op=ALU.add), reads=[b], writes=[b], n=1, two=True)
            op("gpsimd", lambda e: e.tensor_scalar(out=c[:, 2:3], in0=c[:, 0:1], scalar1=1.0 / width, scalar2=EPS, op0=ALU.mult, op1=ALU.add),
               reads=[b], writes=[b], n=1)
            op("gpsimd", lambda e: e.tensor_tensor(out=c[:, 3:4], in0=c[:, 2:3], in1=NEGH[:], op=ALU.pow), reads=[b, bNEGH], writes=[b], n=1, two=True)
            return c[:, 3:4]

        with ExitStack() as p1:
            WIN = sb(p1, "WIN", [128, 8, 3 * DL], BF16); bWIN = Buf()
            BD = sb(p1, "BD", [128, 16, 128], BF16); bBD1 = Buf()
            PWt = sb(p1, "PWt", [128, 4, 128], BF16); bPWt = Buf()
            IDF = sb(p1, "IDF", [128, 128], F32); bIDF = Buf()
            DG = sb(p1, "DG", [128, 16, 128], F32); bDG = Buf()
            EC = sb(p1, "EC", [128, 4, 16], F32); bEC = Buf()
            ZER = sb(p1, "ZER", [128, T1], F32); bZER = Buf()
            HCt = sb(p1, "HCt", [128, 4], F32); bHC = [Buf() for _ in range(4)]
            XB = Ring([sb(p1, "XB%d" % i, [128, D], F32) for i in range(3)])
            XN = Ring([sb(p1, "XN%d" % i, [128, D], BF16) for i in range(2)])
            HT = [sb(p1, "HT%d" % i, [128, 8, T1], BF16) for i in range(2)]
            bHT = [[Buf() for _ in range(4)] for _ in range(2)]
            RU = [sb(p1, "RU%d" % i, [128, 4, RW], F32) for i in range(2)]
            RG = [sb(p1, "RG%d" % i, [128, 4, RW], F32) for i in range(2)]
            RP = [sb(p1, "RP%d" % i, [128, 4, RW], F32) for i in range(2)]
            bRU = [[Buf() for _ in range(4)] for _ in range(2)]
            bRG = [[Buf() for _ in range(4)] for _ in range(2)]
            bRP = [[Buf() for _ in range(4)] for _ in range(2)]
            bHALO = [[Buf() for _ in range(2)] for _ in range(3)]
            FU = sb(p1, "FU", [128, 4, 32], F32); FG = sb(p1, "FG", [128, 4, 32], F32); FP = sb(p1, "FP", [128, 4, 32], F32)
            bFU = [Buf() for _ in range(4)]; bFG = [Buf() for _ in range(4)]; bFP = [Buf() for _ in range(4)]
            UU = Ring([sb(p1, "UU%d" % i, [128, T1], F32) for i in range(2)])
            UB = Ring([sb(p1, "UB%d" % i, [128, T1], BF16) for i in range(2)])
            THR = Ring([sb(p1, "THR%d" % i, [128, T1], F32) for i in range(2)])
            THI = [Ring([sb(p1, "THI%d_%d" % (d, i), [128, T1], F32) for i in range(4)]) for d in range(2)]
            AAr = [Ring([sb(p1, "AA%d_%d" % (d, i), [128, T1], F32) for i in range(4)]) for d in range(2)]
            A2r = [Ring([sb(p1, "A2%d_%d" % (d, i), [128, T1], F32) for i in range(4)]) for d in range(2)]
            HFr = Ring([sb(p1, "HF%d" % i, [128, T1], F32) for i in range(2)])
            HLr = Ring([sb(p1, "HLr%d" % i, [128, T1], F32) for i in range(2)])
            PCr = Ring([sb(p1, "PC%d" % i, [128, T1], F32) for i in range(2)])
            AOr = Ring([sb(p1, "AO%d" % i, [128, T1], F32) for i in range(2)])
            SGB = {}
            PS1 = sb(p1, "PS1", [128, RW], F32); bPS1 = Buf()
            PS2 = sb(p1, "PS2", [128, RW], F32); bPS2 = Buf()
            PGr = Ring([sb(p1, "PG%d" % i, [128, T1], BF16) for i in range(2)])
            YPr = Ring([sb(p1, "YP%d" % i, [128, T1], BF16) for i in range(2)])
            TPp = Ring([ps(p1, "TP%d" % i, [128, 8, 128], BF16) for i in range(2)], psum=True)
            WPp = Ring([ps(p1, "WP%d" % i, [128, T1], F32) for i in range(2)], psum=True)
            CVp = Ring([ps(p1, "CV%d" % i, [128, T1], F32) for i in range(1)], psum=True)
            GPp = Ring([ps(p1, "GP%d" % i, [128, T1], F32) for i in range(2)], psum=True)
            PMp = Ring([ps(p1, "PM%d" % i, [128, T1], F32) for i in range(1)], psum=True)

            dma("gpsimd", WIN[:], w_in_d.rearrange("(kc p) f -> p kc f", p=128), writes=[bWIN], sembuf=bWIN, nbytes=6 << 20)
            op("gpsimd", lambda e: e.memset(BD[:], 0.0), writes=[bBD1], n=2048)
            for m, wsrc in enumerate((rg_wa_d, rg_wx_d)):
                for d in range(2):
                    for h in range(8):
                        c, hf_ = h // 2, h % 2
                        idx = m * 8 + d * 4 + c
                        dma("gpsimd", BD[hf_ * 64:(hf_ + 1) * 64, idx, hf_ * 64:(hf_ + 1) * 64], wsrc[d, h],
                            writes=[bBD1], sembuf=bBD1, group=True, nbytes=16384)
            dma("gpsimd", PWt[:], pool_w_d.rearrange("g i j -> i g j"), writes=[bPWt], sembuf=bPWt, nbytes=1 << 18)
            dma("sync", EC[:], ec_d, writes=[bEC], sembuf=bEC)
            op("vector", lambda e: e.tensor_copy(out=IDF[:], in_=IDN[:]), reads=[bIDN], writes=[bIDF], n=128)
            for k in range(4):
                for c in range(4):
                    op("vector", lambda e, k=k, c=c: e.tensor_scalar(out=DG[:, 4 * k + c, :], in0=IDF[:], scalar1=V[:, VC_CW + 4 * k + c:VC_CW + 4 * k + c + 1],
                                                                     scalar2=None, op0=ALU.mult),
                       reads=[bIDF, bV, bDG], writes=[bDG], n=128)
            op("gpsimd", lambda e: e.memset(ZER[:], 0.0), writes=[bZER])
            op("gpsimd", lambda e: e.memset(FU[:], 0.0), writes=bFU, n=128)
            op("gpsimd", lambda e: e.memset(FG[:], 0.0), writes=bFG, n=128)
            op("gpsimd", lambda e: e.memset(FP[:], 0.0), writes=bFP, n=128)
            op("gpsimd", lambda e: e.memset(HH[:], 0.0), writes=[b for row in bHH for b in row], n=120)

            def x_block(i, tb):
                slot = i % 2
                r0 = i * T1 + tb * 128
                xb, bxb = XB.next()
                dma("sync", xb[:], x_d[r0:r0 + 128, :], writes=[bxb], sembuf=bxb, nbytes=1 << 19)
                xn, bxn = XN.next()
                c, bst = stat_slot()
                op("scalar", lambda e: e.activation(out=xn[:], in_=xb[:], func=AF.Square, accum_out=c[:, 0:1]), reads=[bxb], writes=[bst, bxn], n=D)
                rstd = rstd_from(c, bst, 1, D)
                op("vector", lambda e: e.tensor_scalar(out=xn[:], in0=xb[:], scalar1=rstd, scalar2=None, op0=ALU.mult), reads=[bxb, bst], writes=[bxn], n=D)
                tp, btp = TPp.next()
                for cc in range(8):
                    op("tensor", lambda e, cc=cc: e.transpose(out=tp[:, cc, :], in_=xn[:, cc * 128:(cc + 1) * 128], identity=IDN[:]),
                       reads=[bxn, bIDN], writes=[btp], n=128)
                op("vector", lambda e: e.tensor_tensor(out=HT[slot][:, :, tb * 128:(tb + 1) * 128], in0=tp[:],
                                                       in1=V[:, VC_G1:VC_G1 + 8].unsqueeze(2).to_broadcast([128, 8, 128]), op=ALU.mult),
                   reads=[btp, bV], writes=[bHT[slot][tb]], n=D)

            def w_chunk(i, oc):
                slot = i % 2
                k = i % NT1
                kind, c = oc // 4, oc % 4
                R, bR = ((RU, bRU), (RG, bRG), (RP, bRP))[kind]
                if c == 0:
                    if k == 0:
                        op("gpsimd", lambda e: e.memset(R[slot][:, :, 0:HALO], 0.0), writes=bR[slot], n=64)
                    else:
                        dma("sync", R[slot][:, :, 0:HALO], R[1 - slot][:, :, T1:RW], reads=bR[1 - slot], writes=bR[slot],
                            sembuf=bHALO[kind][slot], nbytes=32768)
                wp, bwp = WPp.next()
                for kc in range(8):
                    op("tensor", lambda e, kc=kc: e.matmul(out=wp[:], lhsT=WIN[:, kc, oc * 128:(oc + 1) * 128], rhs=HT[slot][:, kc, :],
                                                           start=(kc == 0), stop=(kc == 7)),
                       reads=[bWIN] + bHT[slot], writes=[bwp], n=T1)
                op("scalar", lambda e: e.activation(out=R[slot][:, c, HALO:RW], in_=wp[:], func=AF.Copy), reads=[bwp], writes=[bR[slot][c]], n=T1)

            def w_gelu(i):
                slot = i % 2
                op("scalar", lambda e: e.activation(out=RG[slot][:, :, HALO:RW], in_=RG[slot][:, :, HALO:RW], func=AF.Gelu_apprx_tanh),
                   reads=bRG[slot], writes=bRG[slot], n=4 * T1, tab="gelu")

            def lru_front(c, Ru, bRu, c0, n):
                u, bu = UU.next()
                cv, bcv = CVp.next()
                for k in range(4):
                    op("tensor", lambda e, k=k: e.matmul(out=cv[:, :n], lhsT=DG[:, 4 * k + c, :], rhs=Ru[:, c, c0 - 2 + k:c0 - 2 + k + n],
                                                         start=(k == 0), stop=(k == 3)),
                       reads=[bDG, bRu[c]], writes=[bcv], n=4 * n)
                op("scalar", lambda e: e.activation(out=u[:, :n], in_=cv[:, :n], func=AF.Identity, bias=V[:, VC_CB + c:VC_CB + c + 1], scale=1.0),
                   reads=[bcv, bV], writes=[bu], n=n)
                ub, bub = UB.next()
                op("vector", lambda e: e.tensor_scalar(out=ub[:, :n], in0=cv[:, :n], scalar1=V[:, VC_CB + c:VC_CB + c + 1], scalar2=None, op0=ALU.add),
                   reads=[bcv, bV], writes=[bub], n=n)
                res = {"u": (u, bu)}

                def per_dir(d):
                    j = d * 4 + c
                    rp, brp = GPp.next()
                    op("tensor", lambda e: e.matmul(out=rp[:, :n], lhsT=BD[:, j, :], rhs=ub[:, :n], start=True, stop=True),
                       reads=[bBD1, bub], writes=[brp], n=n)
                    thr, bthr = THR.next()
                    op("scalar", lambda e: e.activation(out=thr[:, :n], in_=rp[:, :n], func=AF.Tanh, scale=0.5, bias=CC[:, 16 + j:17 + j]),
                       reads=[brp, bCC], writes=[bthr], n=n, tab="exp")
                    ip, bip = GPp.next()
                    op("tensor", lambda e: e.matmul(out=ip[:, :n], lhsT=BD[:, 8 + j, :], rhs=ub[:, :n], start=True, stop=True),
                       reads=[bBD1, bub], writes=[bip], n=n)
                    thi, bthi = THI[d].next()
                    op("scalar", lambda e: e.activation(out=thi[:, :n], in_=ip[:, :n], func=AF.Tanh, scale=0.5, bias=CC[:, 24 + j:25 + j]),
                       reads=[bip, bCC], writes=[bthi], n=n, tab="exp")
                    aa, baa = AAr[d].next()
                    op("scalar", lambda e: e.activation(out=aa[:, :n], in_=thr[:, :n], func=AF.Exp, scale=CC[:, 8 + j:9 + j], bias=CC[:, 8 + j:9 + j]),
                       reads=[bthr, bCC], writes=[baa], n=n, tab="exp")
                    a2, ba2 = A2r[d].next()
                    if d == 0:
                        op("scalar", lambda e: e.activation(out=a2[:, :n], in_=thr[:, :n], func=AF.Exp, scale=CC[:, j:j + 1], bias=CC[:, j:j + 1]),
                           reads=[bthr, bCC], writes=[ba2], n=n, tab="exp")
                    else:
                        op("gpsimd", lambda e: e.tensor_tensor(out=a2[:, :n], in0=aa[:, :n], in1=aa[:, :n], op=ALU.mult),
                           reads=[baa], writes=[ba2], n=n, two=True)
                    op("vector", lambda e: e.scalar_tensor_tensor(out=thi[:, :n], in0=thi[:, :n], scalar=1.0, in1=u[:, :n], op0=ALU.add, op1=ALU.mult),
                       reads=[bthi, bu], writes=[bthi], n=1.2 * n)
                    res[d] = ((thi, bthi), (aa, baa), (a2, ba2))
                per_dir(0)
                per_dir(1)
                return res

            def lru_back(c, res, Rg, bRg, c0, n, s, kst, tok0):
                def sq_dir(d):
                    (thi, bthi), (aa, baa), (a2, ba2) = res[d]
                    op("scalar", lambda e: e.activation(out=a2[:, :n], in_=a2[:, :n], func=AF.Sqrt, scale=-0.25, bias=0.25),
                       reads=[ba2], writes=[ba2], n=n, tab="sqrt")
                    op("gpsimd", lambda e: e.tensor_tensor(out=thi[:, :n], in0=thi[:, :n], in1=a2[:, :n], op=ALU.mult),
                       reads=[bthi, ba2], writes=[bthi], n=n, two=True)
                sq_dir(0)
                sq_dir(1)
                (bt0, bbt0), (a0, ba0), _ = res[0]
                (bt1, bbt1), (a1, ba1), _ = res[1]
                hf, bhf = HFr.next()
                op("vector", lambda e: e.tensor_tensor_scan(out=hf[:, :n], data0=a0[:, :n], data1=bt0[:, :n], initial=HCt[:, c:c + 1],
                                                            op0=ALU.mult, op1=ALU.add),
                   reads=[ba0, bbt0, bHC[c]], writes=[bhf], n=2 * n)
                dma("sync", HCt[:, c:c + 1], hf[:, n - 1:n], reads=[bhf], writes=[bHC[c]], sembuf=bHC[c], nbytes=512)
                hl, bhl = HLr.next()
                op("vector", lambda e: e.tensor_tensor_scan(out=hl[:, 0:n][:, ::-1], data0=a1[:, 0:n][:, ::-1], data1=bt1[:, 0:n][:, ::-1],
                                                            initial=0.0, op0=ALU.mult, op1=ALU.add),
                   reads=[ba1, bbt1], writes=[bhl], n=2 * n)
                pc, bpc = PCr.next()
                op("vector", lambda e: e.tensor_tensor_scan(out=pc[:, 0:n][:, ::-1], data0=a1[:, 0:n][:, ::-1], data1=ZER[:, :n],
                                                            initial=1.0, op0=ALU.mult, op1=ALU.add),
                   reads=[ba1, bZER], writes=[bpc], n=2 * n)
                hi = hidx(s, kst)
                ao, bao = AOr.next()
                op("gpsimd", lambda e: e.tensor_tensor(out=ao[:, :n], in0=hf[:, :n], in1=hl[:, :n], op=ALU.add), reads=[bhf, bhl], writes=[bao], n=n, two=True)
                dma("sync", sa_d[s, c, :, tok0:tok0 + n], ao[:, :n], reads=[bao], sembuf=bao, nbytes=n * 512)
                dma("sync", sb_d[s, c, :, tok0:tok0 + n], pc[:, :n], reads=[bpc], sembuf=bpc, nbytes=n * 512)
                dma("sync", HL[:, hi, c:c + 1], hl[:, 0:1], reads=[bhl], writes=[bHL[s][kst]], sembuf=SGB.setdefault(("hl", id(hl)), Buf()), group=True, nbytes=512)
                dma("sync", PPt[:, hi, c:c + 1], pc[:, 0:1], reads=[bpc], writes=[bHL[s][kst]], sembuf=SGB.setdefault(("pp", id(pc)), Buf()), group=True, nbytes=512)
                sgb = SGB.setdefault(id(Rg), [Buf() for _ in range(4)])[c]
                ta = tok0
                while ta < tok0 + n:
                    k2 = ta // T2
                    tb_ = min(tok0 + n, (k2 + 1) * T2)
                    row0 = (s * NT2 + k2) * T2
                    dma("sync", y_d[row0:row0 + 128, c * T2 + (ta - k2 * T2):c * T2 + (tb_ - k2 * T2)],
                        Rg[:, c, c0 + (ta - tok0):c0 + (tb_ - tok0)], reads=[bRg[c]], sembuf=sgb, group=False, nbytes=(tb_ - ta) * 512)
                    ta = tb_

            def pool_group(g, Rp, bRp, c0, n, s, tok0, first, last):
                w = WINS[g]
                half = w // 2
                src = Rp[:, g, :]
                width = c0 + n + 8
                cur, bcur = src, bRp[g]
                tmps = [(PS1, bPS1), (PS2, bPS2)]
                step = 1
                ti = 0
                while step < half:
                    (dst, bdst) = tmps[ti % 2]
                    ti += 1
                    L = width - (2 * step - 1)
                    op("gpsimd", lambda e, cur=cur, dst=dst, step=step, L=L: e.tensor_tensor(out=dst[:, 0:L], in0=cur[:, 0:L], in1=cur[:, step:step + L], op=ALU.add),
                       reads=[bcur], writes=[bdst], n=L, two=True)
                    cur, bcur = dst, bdst
                    step *= 2
                (dst, bdst) = tmps[ti % 2]
                op("gpsimd", lambda e: e.tensor_tensor(out=dst[:, 0:n], in0=cur[:, c0 - half:c0 - half + n], in1=cur[:, c0:c0 + n], op=ALU.add),
                   reads=[bcur], writes=[bdst], n=n, two=True)
                pg, bpg = PGr.next()
                op("vector", lambda e: e.scalar_tensor_tensor(out=pg[:, :n], in0=dst[:, 0:n], scalar=1.0 / w, in1=src[:, c0:c0 + n],
                                                              op0=ALU.mult, op1=ALU.subtract),
                   reads=[bdst, bRp[g]], writes=[bpg], n=1.5 * n)
                if first or last:
                    e0, off = (0, 0) if first else (8, n - 8)
                    op("vector", lambda e: e.tensor_tensor(out=dst[:, off:off + 8], in0=dst[:, off:off + 8], in1=EC[:, g, e0:e0 + 8], op=ALU.mult),
                       reads=[bdst, bEC], writes=[bdst], n=8)
                    op("vector", lambda e: e.tensor_tensor(out=pg[:, off:off + 8], in0=dst[:, off:off + 8], in1=src[:, c0 + off:c0 + off + 8], op=ALU.subtract),
                       reads=[bdst, bRp[g]], writes=[bpg], n=8)
                pm, bpm = PMp.next()
                op("tensor", lambda e: e.matmul(out=pm[:, :n], lhsT=PWt[:, g, :], rhs=pg[:, :n], start=True, stop=True), reads=[bPWt, bpg], writes=[bpm], n=n)
                yp, byp = YPr.next()
                op("scalar", lambda e: e.activation(out=yp[:, :n], in_=pm[:, :n], func=AF.Identity, scale=V[:, VC_PS + g:VC_PS + g + 1],
                                                    bias=CC[:, 40 + g:41 + g]),
                   reads=[bpm, bV, bCC], writes=[byp], n=n)
                dma("sync", sy_d[s, g, :, tok0:tok0 + n], yp[:, :n], reads=[byp], sembuf=byp, nbytes=n * 256)

            def mix_stage(s, kst, Ru, bRu, Rg, bRg, Rp, bRp, c0, n, tok0):
                first = (kst == 0)
                last = (kst == NT1)
                fr = {}
                for c in range(4):
                    fr[c] = lru_front(c, Ru, bRu, c0, n)
                for c in range(4):
                    lru_back(c, fr[c], Rg, bRg, c0, n, s, kst, tok0)
                    pool_group(c, Rp, bRp, c0, n, s, tok0, first, last)

            NTILES = NSEQ * NT1
            for tb in range(4):
                x_block(0, tb)
            for oc in range(12):
                w_chunk(0, oc)
            w_gelu(0)
            for tb in range(4):
                x_block(1, tb)
            for i in range(NTILES):
                s, k = i // NT1, i % NT1
                slot = i % 2
                if k == 0:
                    for c in range(4):
                        op("gpsimd", lambda e, c=c: e.memset(HCt[:, c:c + 1], 0.0), writes=[bHC[c]], n=1)
                if i + 1 < NTILES:
                    for oc in range(12):
                        w_chunk(i + 1, oc)
                    w_gelu(i + 1)
                if i + 2 < NTILES:
                    for tb in range(4):
                        x_block(i + 2, tb)
                if k == 0:
                    mix_stage(s, 0, RU[slot], bRU[slot], RG[slot], bRG[slot], RP[slot], bRP[slot], HALO, T1 - 8, 0)
                else:
                    mix_stage(s, k, RU[slot], bRU[slot], RG[slot], bRG[slot], RP[slot], bRP[slot], 8, T1, k * T1 - 8)
                if k == NT1 - 1:
                    for c in range(4):
                        op("gpsimd", lambda e, c=c, slot=slot: e.tensor_copy(out=FU[:, c, 0:HALO], in_=RU[slot][:, c, T1:RW]), reads=[bRU[slot][c]], writes=[bFU[c]], n=16)
                        op("gpsimd", lambda e, c=c, slot=slot: e.tensor_copy(out=FG[:, c, 0:HALO], in_=RG[slot][:, c, T1:RW]), reads=[bRG[slot][c]], writes=[bFG[c]], n=16)
                        op("gpsimd", lambda e, c=c, slot=slot: e.tensor_copy(out=FP[:, c, 0:HALO], in_=RP[slot][:, c, T1:RW]), reads=[bRP[slot][c]], writes=[bFP[c]], n=16)
                    mix_stage(s, NT1, FU, bFU, FG, bFG, FP, bFP, 8, 8, SEQ - 8)
                    for kk in range(NT1, -1, -1):
                        hi, hn = hidx(s, kk), hidx(s, kk + 1)
                        op("vector", lambda e, hi=hi, hn=hn: e.tensor_tensor(out=HH[:, hi, :], in0=PPt[:, hi, :], in1=HH[:, hn, :], op=ALU.mult),
                           reads=[bHL[s][kk], bHH[s][kk + 1]], writes=[bHH[s][kk]], n=4)
                        op("vector", lambda e, hi=hi: e.tensor_tensor(out=HH[:, hi, :], in0=HH[:, hi, :], in1=HL[:, hi, :], op=ALU.add),
                           reads=[bHL[s][kk], bHH[s][kk]], writes=[bHH[s][kk]], n=4)
            S.flush()

        with ExitStack() as p2:
            WOUT = sb(p2, "WOUT", [128, 8, D], BF16); bWOUT = Buf()
            dma("gpsimd", WOUT[:], w_out_d.rearrange("(kc p) f -> p kc f", p=128), writes=[bWOUT], sembuf=bWOUT, nbytes=4 << 20)
            WG = sb(p2, "WG", [128, 8, DFF], BF16); bWGp = [Buf() for _ in range(4)]
            WU = sb(p2, "WU", [128, 8, DFF], BF16); bWUp = [Buf() for _ in range(4)]
            WD = sb(p2, "WD", [128, NFF, D], BF16); bWDp = [Buf() for _ in range(4)]
            JP = (0, 4, 10, 16, NFF)

            def piece_of(jf):
                return max(q for q in range(4) if JP[q] <= jf)
            G2 = sb(p2, "G2", [128, D], F32); bG2 = Buf()
            G4 = sb(p2, "G4", [128, D], F32); bG4 = Buf()
            X2 = [sb(p2, "X2_%d" % i, [128, 2, D], F32) for i in range(2)]
            bX2 = [[Buf() for _ in range(2)] for _ in range(2)]
            At = sb(p2, "At", [128, 4, T2], F32); bAt = Buf()
            Bt = sb(p2, "Bt", [128, 4, T2], F32); bBt = Buf()
            Gt = sb(p2, "Gt", [128, 4, T2], F32); bGt = Buf()
            YT = sb(p2, "YT", [128, 8, T2], BF16); bYTl = Buf(); bYTp = Buf()
            H2T = sb(p2, "H2T", [128, 8, T2], BF16); bH2T = [Buf() for _ in range(2)]
            XN2 = Ring([sb(p2, "XN2_%d" % i, [128, D], BF16) for i in range(2)])
            TMP = Ring([sb(p2, "TMP%d" % i, [128, D], F32) for i in range(1)])
            SG = Ring([sb(p2, "SG%d" % i, [128, T2], F32) for i in range(2)])
            ACr = Ring([sb(p2, "AC%d" % i, [128, T2], BF16) for i in range(5)])
            MF = [ps(p2, "MF%d" % i, [128, D], F32) for i in range(2)]; bMF = [Buf(psum=True) for _ in range(2)]
            GUp = Ring([ps(p2, "GU%d" % i, [128, 2, T2], F32) for i in range(3)], psum=True)
            AUX = ps(p2, "AUX", [128, 512], F32); bAUX = Buf(psum=True)
            AUXT = AUX[:].bitcast(BF16).rearrange("p (c t) -> p c t", c=8)

            dma("sync", G2[:], g2_d.partition_broadcast(128), writes=[bG2], sembuf=bG2, nbytes=1 << 19)
            dma("sync", G4[:], g4_d.partition_broadcast(128), writes=[bG4], sembuf=bG4, nbytes=1 << 19)
            wg_v = w_gate_d.rearrange("(kc p) f -> p kc f", p=128)
            wu_v = w_up_d.rearrange("(kc p) f -> p kc f", p=128)
            wd_v = w_down_d.rearrange("(j p) f -> p j f", p=128)
            for q in range(4):
                f0, f1 = JP[q] * 128, JP[q + 1] * 128
                nb = (f1 - f0) * 4096
                dma("gpsimd", WG[:, :, f0:f1], wg_v[:, :, f0:f1], reads=[bWOUT], writes=[bWGp[q]], sembuf=bWGp[q], nbytes=nb)
                dma("gpsimd", WU[:, :, f0:f1], wu_v[:, :, f0:f1], reads=[bWOUT], writes=[bWUp[q]], sembuf=bWUp[q], nbytes=nb)
                dma("gpsimd", WD[:, JP[q]:JP[q + 1], :], wd_v[:, JP[q]:JP[q + 1], :], reads=[bWOUT], writes=[bWDp[q]], sembuf=bWDp[q], nbytes=nb)

            NT = NSEQ * NT2

            def tile_info(i2):
                s, j = i2 // NT2, i2 % NT2
                return s, j, i2 % 2, j * T2, s * SEQ + j * T2

            def front_pieces(i2):
                s, j, slot, t0_, r0 = tile_info(i2)
                pieces = []

                def loads():
                    dma("sync", X2[slot][:], x_d[r0:r0 + T2, :].rearrange("(b p) f -> p b f", p=128), writes=bX2[slot], sembuf=bX2[slot][0], nbytes=1 << 20)
                    dma("sync", At[:], sa_d[s, :, :, t0_:t0_ + T2].rearrange("c p t -> p c t"), writes=[bAt], sembuf=bAt, nbytes=1 << 19)
                    dma("sync", Bt[:], sb_d[s, :, :, t0_:t0_ + T2].rearrange("c p t -> p c t"), writes=[bBt], sembuf=bBt, nbytes=1 << 19)
                    dma("sync", Gt[:], y_d[r0:r0 + 128, :].rearrange("p (c t) -> p c t", c=4), writes=[bGt], sembuf=bGt, nbytes=1 << 19)
                    dma("sync", YT[:, 4:8, :], sy_d[s, :, :, t0_:t0_ + T2].rearrange("c p t -> p c t"), writes=[bYTp], sembuf=bYTp, nbytes=1 << 18)
                pieces.append(loads)

                def ylru():
                    kst = j // 2
                    for c in range(4):
                        segs = [(0, T2, kst + 1)] if j % 2 == 0 else [(0, T2 - 8, kst + 1), (T2 - 8, T2, kst + 2)]
                        for (a_, b_, kh) in segs:
                            hi = hidx(s, kh)
                            op("vector", lambda e, c=c, a_=a_, b_=b_, hi=hi: e.scalar_tensor_tensor(
                                out=At[:, c, a_:b_], in0=Bt[:, c, a_:b_], scalar=HH[:, hi, c:c + 1], in1=At[:, c, a_:b_], op0=ALU.mult, op1=ALU.add),
                               reads=[bAt, bBt, bHH[s][kh]], writes=[bAt], n=1.2 * (b_ - a_))
                    op("vector", lambda e: e.tensor_tensor(out=YT[:, 0:4, :], in0=At[:], in1=Gt[:], op=ALU.mult), reads=[bAt, bGt], writes=[bYTl], n=4 * T2)
                pieces.append(ylru)

                state = {}

                def wout_half(tb, hf_):
                    def f():
                        if hf_ == 0:
                            state[tb] = (stat_slot(), TMP.next(), XN2.next())
                        (c, bst), (tmp, btmp), (xn, bxn) = state[tb]
                        for kc in range(8):
                            op("tensor", lambda e, kc=kc: e.matmul(out=AUX[:], lhsT=YT[:, kc, tb * 128:(tb + 1) * 128],
                                                                   rhs=WOUT[:, kc, hf_ * 512:(hf_ + 1) * 512], start=(kc == 0), stop=(kc == 7)),
                               reads=[bYTl, bYTp, bWOUT], writes=[bAUX], n=512)
                        op("scalar", lambda e: e.activation(out=xn[:, hf_ * 512:(hf_ + 1) * 512], in_=AUX[:], func=AF.Square, accum_out=c[:, hf_:hf_ + 1]),
                           reads=[bAUX], writes=[bst, bxn], n=512)
                        op("vector", lambda e: e.tensor_tensor(out=tmp[:, hf_ * 512:(hf_ + 1) * 512], in0=AUX[:], in1=G2[:, hf_ * 512:(hf_ + 1) * 512], op=ALU.mult),
                           reads=[bAUX, bG2], writes=[btmp], n=512)
                    return f

                def norm_block(tb):
                    def f():
                        (c, bst), (tmp, btmp), (xn, bxn) = state[tb]
                        xrow, bx = X2[slot][:, tb, :], bX2[slot][tb]
                        rstd = rstd_from(c, bst, 2, D)
                        op("vector", lambda e: e.scalar_tensor_tensor(out=xrow, in0=tmp[:], scalar=rstd, in1=xrow, op0=ALU.mult, op1=ALU.add),
                           reads=[btmp, bst, bx], writes=[bx], n=1.2 * D)
                        c2, bst2 = stat_slot()
                        op("scalar", lambda e: e.activation(out=xn[:], in_=xrow, func=AF.Square, accum_out=c2[:, 0:1]), reads=[bx], writes=[bst2, bxn], n=D)
                        rstd2 = rstd_from(c2, bst2, 1, D)
                        op("scalar", lambda e: e.activation(out=xn[:], in_=xrow, func=AF.Copy, scale=rstd2), reads=[bx, bst2], writes=[bxn], n=D)
                    return f

                for tb in range(2):
                    pieces.append(wout_half(tb, 0))
                    pieces.append(wout_half(tb, 1))
                    pieces.append(norm_block(tb))

                def transposes():
                    for tb in range(2):
                        (c, bst), (tmp, btmp), (xn, bxn) = state[tb]
                        if tb == 0:
                            tpv, btpv = AUXT, bAUX
                        else:
                            gu, btpv = GUp.next()
                            tpv = gu[:].rearrange("p a t -> p (a t)").bitcast(BF16).rearrange("p (c t) -> p c t", c=8)
                        for cc in range(8):
                            op("tensor", lambda e, cc=cc, xn=xn, tpv=tpv: e.transpose(out=tpv[:, cc, :], in_=xn[:, cc * 128:(cc + 1) * 128], identity=IDN[:]),
                               reads=[bxn, bIDN], writes=[btpv], n=128)
                        op("vector", lambda e, tb=tb, tpv=tpv: e.tensor_tensor(out=H2T[:, :, tb * 128:(tb + 1) * 128], in0=tpv,
                                                                               in1=V[:, VC_G3:VC_G3 + 8].unsqueeze(2).to_broadcast([128, 8, 128]), op=ALU.mult),
                           reads=[btpv, bV], writes=[bH2T[tb]], n=D)
                return pieces, transposes

            def down(i2, jf, ac, bac):
                for tb in range(2):
                    for hf_ in range(2):
                        op("tensor", lambda e, tb=tb, hf_=hf_: e.matmul(
                            out=MF[tb][:, hf_ * 512:(hf_ + 1) * 512], lhsT=ac[:, tb * 128:(tb + 1) * 128],
                            rhs=WD[:, jf, hf_ * 512:(hf_ + 1) * 512], start=(jf == 0), stop=(jf == NFF - 1)),
                           reads=[bac, bWDp[piece_of(jf)]], writes=[bMF[tb]], n=512)

            def final_block(i2, tb):
                s, j, slot, t0_, r0 = tile_info(i2)
                xrow, bx = X2[slot][:, tb, :], bX2[slot][tb]
                c, bst = stat_slot()
                tmp, btmp = TMP.next()
                op("scalar", lambda e: e.activation(out=tmp[:].bitcast(BF16)[:, 0:D], in_=MF[tb][:], func=AF.Square, accum_out=c[:, 0:1]),
                   reads=[bMF[tb]], writes=[bst, btmp], n=D)
                op("vector", lambda e: e.tensor_tensor(out=tmp[:], in0=MF[tb][:], in1=G4[:], op=ALU.mult), reads=[bMF[tb], bG4], writes=[btmp], n=D)
                rstd = rstd_from(c, bst, 1, D)
                op("vector", lambda e: e.scalar_tensor_tensor(out=xrow, in0=tmp[:], scalar=rstd, in1=xrow, op0=ALU.mult, op1=ALU.add),
                   reads=[btmp, bst, bx], writes=[bx], n=1.2 * D)

            pieces, transposes = front_pieces(0)
            for p in pieces:
                p()
            transposes()
            for i2 in range(NT):
                s, j, slot, t0_, r0 = tile_info(i2)
                if i2 + 1 < NT:
                    nxt_pieces, nxt_transposes = front_pieces(i2 + 1)
                else:
                    nxt_pieces, nxt_transposes = [], None
                at = {0: 0, 1: 1, 3: 2, 5: 3, 7: 4, 10: 5, 12: 6, 14: 7}
                prevq = []
                for jf in range(NFF):
                    gu, bgu = GUp.next()
                    for kc in range(8):
                        op("tensor", lambda e, kc=kc, gu=gu, jf=jf: e.matmul(out=gu[:, 0, :], lhsT=WG[:, kc, jf * 128:(jf + 1) * 128], rhs=H2T[:, kc, :],
                                                                            start=(kc == 0), stop=(kc == 7)),
                           reads=[bWGp[piece_of(jf)]] + bH2T, writes=[bgu], n=T2)
                    for kc in range(8):
                        op("tensor", lambda e, kc=kc, gu=gu, jf=jf: e.matmul(out=gu[:, 1, :], lhsT=WU[:, kc, jf * 128:(jf + 1) * 128], rhs=H2T[:, kc, :],
                                                                            start=(kc == 0), stop=(kc == 7)),
                           reads=[bWUp[piece_of(jf)]] + bH2T, writes=[bgu], n=T2)
                    if len(prevq) >= 3:
                        down(i2, *prevq.pop(0))
                    if jf in at and at[jf] < len(nxt_pieces):
                        nxt_pieces[at[jf]]()
                    sg, bsg = SG.next()
                    op("scalar", lambda e, sg=sg, gu=gu: e.activation(out=sg[:], in_=gu[:, 0, :], func=AF.Silu), reads=[bgu], writes=[bsg], n=T2, tab="silu")
                    ac, bac = ACr.next()
                    op("vector", lambda e, sg=sg, gu=gu, ac=ac: e.tensor_tensor(out=ac[:], in0=sg[:], in1=gu[:, 1, :], op=ALU.mult), reads=[bsg, bgu], writes=[bac], n=T2)
                    prevq.append((jf, ac, bac))
                down(i2, *prevq.pop(0))
                if nxt_transposes is not None:
                    nxt_transposes()
                down(i2, *prevq.pop(0))
                down(i2, *prevq.pop(0))
                for tb in range(2):
                    final_block(i2, tb)
                dma("sync", y_d[r0:r0 + T2, :].rearrange("(b p) f -> p b f", p=128), X2[slot][:], reads=bX2[slot], sembuf=bX2[slot][1], nbytes=1 << 20)
            S.flush(final_engines=["sync"])
        print("[kernel] ninst=%d nwaits=%d nsems=%d sim_us=%.0f" % (S.ninst, S.nwaits, len(S.sems), S.sim_time / 1e3))
    return nc


_NC_CACHE = {}


def _cols(v, n):
    return np.ascontiguousarray(np.asarray(v, np.float32).reshape(n, 128).T)


def kernel(x_prompt, x_sample, pre_norm_mix, post_norm_mix, w_in, conv_w, conv_b, rg_wa, rg_ba, rg_wx, rg_bx,
           rg_lam, pool_w, pool_b, pool_scale, w_out, pre_norm_ffn, post_norm_ffn, w_gate, w_up, w_down):
    f = lambda a: np.ascontiguousarray(np.asarray(a, dtype=np.float32))
    x_prompt, x_sample = f(x_prompt), f(x_sample)
    vecs = np.concatenate([
        _cols(conv_w[0], 16), _cols(conv_b[0], 4), _cols(rg_ba[0], 8), _cols(rg_bx[0], 8), _cols(rg_lam[0], 8),
        _cols(pool_b[0], 4), _cols(pool_scale[0], 4), _cols(pre_norm_mix[0], 8), _cols(pre_norm_ffn[0], 8)], axis=1)
    assert vecs.shape == (128, NV)
    ident = np.eye(128, dtype=np.float32).astype(ml_dtypes.bfloat16)
    ec = np.zeros((128, 4, 16), np.float32)
    for g, w in enumerate(WINS):
        half = w // 2
        for e in range(16):
            t = e if e < 8 else SEQ - 16 + e
            cnt = min(t + half, SEQ) - max(t - half, 0)
            ec[:, g, e] = 1.0 / cnt
    common = {
        "vecs": np.ascontiguousarray(vecs), "g2": f(post_norm_mix[0]).reshape(1, D), "g4": f(post_norm_ffn[0]).reshape(1, D),
        "ident": ident, "ec": ec, "w_in": f(w_in[0]), "rg_wa": f(rg_wa[0]), "rg_wx": f(rg_wx[0]), "pool_w": f(pool_w[0]),
        "w_out": f(w_out[0]), "w_gate": f(w_gate[0]), "w_up": f(w_up[0]), "w_down": f(w_down[0]),
    }
    in_maps = []
    for c in range(NCORES):
        xs = np.concatenate([x_prompt[2 * c].reshape(SEQ, D), x_prompt[2 * c + 1].reshape(SEQ, D), x_sample[c].reshape(SEQ, D)], axis=0)
        m = dict(common)
        m["x"] = np.ascontiguousarray(xs)
        in_maps.append(m)
    if "nc" not in _NC_CACHE:
        _NC_CACHE["nc"] = build_program()
    nc = _NC_CACHE["nc"]
    res = run_bass_kernel_spmd(nc, in_maps, core_ids=list(range(NCORES)))
    y_prompt = np.empty_like(x_prompt)
    y_sample = np.empty_like(x_sample)
    for c in range(NCORES):
        y = np.asarray(res.results[c]["y"], dtype=np.float32).reshape(NSEQ, SEQ, D)
        y_prompt[2 * c] = y[0]
        y_prompt[2 * c + 1] = y[1]
        y_sample[c] = y[2]
    return (y_prompt, y_sample)
```

```python
import numpy as np
import ml_dtypes
from contextlib import ExitStack
import concourse.bass as bass
import concourse.mybir as mybir
from concourse.bass_utils import run_bass_kernel_spmd

F32 = mybir.dt.float32
BF16 = mybir.dt.bfloat16
F32R = mybir.dt.float32r
AF = mybir.ActivationFunctionType
ALU = mybir.AluOpType

NCORES = 8
D = 1024
SEQ = 4096
NSEQ = 3
NTOK = NSEQ * SEQ
DL = 512
DFF = 2816
NFF = DFF // 128
T1 = 512
HALO = 16
RW = T1 + HALO
NT1 = SEQ // T1
NSTG = NT1 + 1
T2 = 256
NT2 = SEQ // T2
EPS = 1e-6
WINS = (2, 4, 8, 16)

VC_CW, VC_CB, VC_BA, VC_BX, VC_LAM, VC_PB, VC_PS, VC_G1, VC_G3, NV = 0, 16, 20, 28, 36, 44, 48, 52, 60, 68


VERBOSE = False


class Buf:
    __slots__ = ("name", "w", "r", "dsem", "grp", "psum", "gbase")

    def __init__(self, name="", psum=False):
        self.name = name
        self.psum = psum
        self.w = None
        self.r = []
        self.grp = []
        self.gbase = set()
        self.dsem = None


class Node:
    __slots__ = ("id", "eng", "kind", "fn", "out", "in_", "sem", "deps", "cost", "lat", "start", "fin", "val", "clock", "tab")


class Sched:
    ENGS = ("tensor", "vector", "scalar", "gpsimd", "sync")

    def __init__(self, nc, es):
        self.nc = nc
        self.es = es
        self.sems = []
        self.semcount = []
        self.engsem = {}
        self.known = {}
        for n in self.ENGS:
            h = es.enter_context(nc.semaphore("e_" + n))
            self.sems.append(h)
            self.semcount.append(0)
            self.engsem[n] = len(self.sems) - 1
            self.known[n] = {}
        self.nodes = []
        self.done = {}
        self.nid = 0
        self.nwaits = 0
        self.ninst = 0
        self.sim_time = 0.0

    @staticmethod
    def _deps(reads, writes):
        deps = set()
        for b in reads:
            if b.w is not None:
                deps.add(b.w)
            deps.update(b.grp)
            if b.psum:
                deps.update(b.r)
        for b in writes:
            if b.w is not None:
                deps.add(b.w)
            deps.update(b.grp)
            deps.update(b.r)
        return deps

    def _node(self, eng, kind, deps, cost, lat):
        n = Node()
        n.id = self.nid
        self.nid += 1
        n.eng, n.kind, n.deps, n.cost, n.lat = eng, kind, deps, cost, lat
        n.fn = n.out = n.in_ = n.sem = None
        n.tab = None
        self.nodes.append(n)
        self.ninst += 1
        return n

    def op(self, eng, fn, reads=(), writes=(), n=512, two=False, tab=None):
        if eng == "tensor":
            cost = 8.0 + n / 2.35
        elif eng == "scalar":
            cost = 200.0 + 0.78 * n
        elif eng == "vector":
            cost = 100.0 + 1.25 * n
        else:
            cost = 400.0 + (2.3 if two else 0.95) * n
        n = self._node(eng, "op", self._deps(reads, writes), cost, cost)
        n.fn = fn
        n.tab = tab
        n.sem = self.engsem[eng]
        for b in writes:
            b.w = n.id
            b.r = []
            b.grp = []
        for b in reads:
            if b.w != n.id:
                b.r.append(n.id)
        return n.id

    def dma(self, q, out, in_, reads=(), writes=(), sembuf=None, group=False, nbytes=65536):
        if sembuf.dsem is None:
            h = self.es.enter_context(self.nc.semaphore("d%d" % len(self.sems)))
            self.sems.append(h)
            self.semcount.append(0)
            sembuf.dsem = len(self.sems) - 1
        deps = self._deps(reads, writes)
        if group:
            for b in writes:
                if not b.grp:
                    b.gbase = set(deps)
                deps -= set(b.grp)
                deps |= b.gbase
        issue = 1000.0 if q == "gpsimd" else 60.0
        n = self._node(q, "dma", deps, issue, 2000.0 + nbytes / 150.0)
        n.out, n.in_, n.sem = out, in_, sembuf.dsem
        for b in writes:
            if group:
                b.grp.append(n.id)
            else:
                b.grp = []
                b.r = []
            b.w = n.id
        for b in reads:
            b.r.append(n.id)
        return n.id

    def flush(self, final_engines=None):
        import heapq
        nodes = self.nodes
        self.nodes = []
        byid = {n.id: n for n in nodes}
        ndep = {}
        users = {}
        ready_t = {}
        for n in nodes:
            cnt = 0
            for d in n.deps:
                if d in byid:
                    cnt += 1
                    users.setdefault(d, []).append(n.id)
            ndep[n.id] = cnt
            ready_t[n.id] = 0.0
        free = {e: 0.0 for e in self.ENGS}
        pend = {e: [] for e in self.ENGS}
        avail = {e: [] for e in self.ENGS}
        for n in nodes:
            if ndep[n.id] == 0:
                heapq.heappush(pend[n.eng], (0.0, n.id))
        order = []
        remaining = len(nodes)
        SEMLAT = 150.0
        cur_tab = [None]
        while remaining:
            best = None
            for e in self.ENGS:
                if avail[e]:
                    t = free[e]
                elif pend[e]:
                    t = max(free[e], pend[e][0][0])
                else:
                    continue
                if best is None or t < best[0]:
                    best = (t, e)
            t, e = best
            while pend[e] and pend[e][0][0] <= t:
                rt, i = heapq.heappop(pend[e])
                heapq.heappush(avail[e], i)
            extra = 0.0
            if e == "scalar":
                lst = avail[e]
                same = [i for i in lst if byid[i].tab is None or byid[i].tab == cur_tab[0]]
                nsq = sum(1 for i in lst if byid[i].tab == "sqrt")
                if cur_tab[0] != "sqrt" and 0 < nsq < 4:
                    held = [i for i in lst if byid[i].tab == "sqrt" and t - ready_t[i] < 8000.0]
                    rest = [i for i in lst if i not in held]
                    if rest:
                        lst2 = rest
                    else:
                        lst2 = lst
                else:
                    lst2 = lst
                oldest = min(lst2)
                if same and not (byid[oldest].tab not in (None, cur_tab[0]) and t - ready_t[oldest] > 25000.0):
                    i = min(same)
                else:
                    i = oldest
                lst.remove(i)
                heapq.heapify(lst)
                if byid[i].tab is not None and byid[i].tab != cur_tab[0]:
                    cur_tab[0] = byid[i].tab
                    extra = 1300.0
            else:
                i = heapq.heappop(avail[e])
            n = byid[i]
            n.start = t + extra
            t = t + extra
            free[e] = t + n.cost
            n.fin = t + n.lat
            order.append(n)
            remaining -= 1
            for u in users.get(i, ()):
                ndep[u] -= 1
                if ready_t[u] < n.fin + SEMLAT:
                    ready_t[u] = n.fin + SEMLAT
                if ndep[u] == 0:
                    un = byid[u]
                    heapq.heappush(pend[un.eng], (ready_t[u], u))
        self.sim_time += max([n.fin for n in order] + [0.0])
        if VERBOSE:
            mk = max([n.fin for n in order] + [0.0])
            busy = {e: sum(n.cost for n in order if n.eng == e) for e in self.ENGS}
            print("[sched] phase makespan %.0f us; busy " % (mk / 1e3) + " ".join("%s=%.0f%%" % (e[:4], 100 * busy[e] / mk) for e in self.ENGS))
        prog = {e: [] for e in self.ENGS}
        for n in order:
            E = n.eng
            known = self.known[E]
            need = {}
            for d in n.deps:
                if d in byid:
                    dn = byid[d]
                    ev = (dn.sem, dn.val, dn.clock)
                else:
                    ev = self.done[d]
                sem, val, clock = ev
                if E == "tensor" and sem == self.engsem["tensor"]:
                    continue
                if known.get(sem, 0) >= val:
                    continue
                if need.get(sem, (0, None))[0] < val:
                    need[sem] = (val, clock)
            for sem, (val, clock) in need.items():
                if known.get(sem, 0) >= val:
                    continue
                prog[E].append(("wait", sem, val))
                self.nwaits += 1
                for s2, v2 in clock.items():
                    if known.get(s2, 0) < v2:
                        known[s2] = v2
                known[sem] = val
            if n.kind == "op":
                self.semcount[n.sem] += 1
                prog[E].append(("op", n.fn))
            else:
                self.semcount[n.sem] += 16
                prog[E].append(("dma", n.out, n.in_, n.sem))
            n.val = self.semcount[n.sem]
            n.clock = dict(known)
        for n in order:
            self.done[n.id] = (n.sem, n.val, n.clock)
            n.fn = n.out = n.in_ = None
        for e in (final_engines or self.ENGS):
            known = self.known[e]
            for sem, cnt in enumerate(self.semcount):
                if cnt > 0 and known.get(sem, 0) < cnt and not (e == "tensor" and sem == self.engsem["tensor"]):
                    prog[e].append(("wait", sem, cnt))
                    known[sem] = cnt
        sems = self.sems
        with self.nc.Block() as block:
            def runner(e):
                def run(h):
                    mysem = sems[self.engsem[e]]
                    for item in prog[e]:
                        if item[0] == "wait":
                            h.wait_ge(sems[item[1]], item[2])
                        elif item[0] == "op":
                            item[1](h).then_inc(mysem, 1)
                        else:
                            h.dma_start(out=item[1], in_=item[2]).then_inc(sems[item[3]], 16)
                return run
            block.tensor(runner("tensor"))
            block.vector(runner("vector"))
            block.scalar(runner("scalar"))
            block.gpsimd(runner("gpsimd"))
            block.sync(runner("sync"))


class Ring:
    def __init__(self, tiles, psum=False):
        self.tiles = tiles
        self.bufs = [Buf(psum=psum) for _ in tiles]
        self.i = 0

    def next(self):
        j = self.i % len(self.tiles)
        self.i += 1
        return self.tiles[j], self.bufs[j]


def build_program():
    nc = bass.Bass("TRN2", target_bir_lowering=False)
    dt = lambda name, shape, dtype, kind: nc.dram_tensor(name, shape, dtype, kind=kind).ap()
    x_d = dt("x", [NTOK, D], F32, "ExternalInput")
    vecs_d = dt("vecs", [128, NV], F32, "ExternalInput")
    g2_d = dt("g2", [1, D], F32, "ExternalInput")
    g4_d = dt("g4", [1, D], F32, "ExternalInput")
    ident_d = dt("ident", [128, 128], BF16, "ExternalInput")
    ec_d = dt("ec", [128, 4, 16], F32, "ExternalInput")
    w_in_d = dt("w_in", [D, 3 * DL], F32, "ExternalInput")
    rg_wa_d = dt("rg_wa", [2, 8, 64, 64], F32, "ExternalInput")
    rg_wx_d = dt("rg_wx", [2, 8, 64, 64], F32, "ExternalInput")
    pool_w_d = dt("pool_w", [4, 128, 128], F32, "ExternalInput")
    w_out_d = dt("w_out", [D, D], F32, "ExternalInput")
    w_gate_d = dt("w_gate", [D, DFF], F32, "ExternalInput")
    w_up_d = dt("w_up", [D, DFF], F32, "ExternalInput")
    w_down_d = dt("w_down", [DFF, D], F32, "ExternalInput")
    y_d = dt("y", [NTOK, D], F32, "ExternalOutput")
    sa_d = dt("stash_a", [NSEQ, 4, 128, SEQ], F32, "Internal")
    sb_d = dt("stash_b", [NSEQ, 4, 128, SEQ], F32, "Internal")
    sy_d = dt("stash_y", [NSEQ, 4, 128, SEQ], BF16, "Internal")

    with ExitStack() as es:
        S = Sched(nc, es)
        op, dma = S.op, S.dma

        def sb(ctx, name, shape, dtype):
            return ctx.enter_context(nc.sbuf_tensor(name, shape, dtype))

        def ps(ctx, name, shape, dtype):
            return ctx.enter_context(nc.psum_tensor(name, shape, dtype))

        V = sb(es, "V", [128, NV], F32); bV = Buf()
        CC = sb(es, "CC", [128, 44], F32); bCC = Buf()
        IDN = sb(es, "IDN", [128, 128], BF16); bIDN = Buf()
        NEGH = sb(es, "NEGH", [128, 1], F32); bNEGH = Buf()
        HL = sb(es, "HL", [128, NSEQ * (NSTG + 1), 4], F32)
        PPt = sb(es, "PPt", [128, NSEQ * (NSTG + 1), 4], F32)
        HH = sb(es, "HH", [128, NSEQ * (NSTG + 1), 4], F32)
        bHL = [[Buf() for _ in range(NSTG + 1)] for _ in range(NSEQ)]
        bHH = [[Buf() for _ in range(NSTG + 1)] for _ in range(NSEQ)]
        ST = sb(es, "ST", [128, 64], F32)
        bST = [Buf() for _ in range(16)]
        st_i = [0]

        def hidx(s, k):
            return s * (NSTG + 1) + k

        dma("sync", V[:], vecs_d, writes=[bV], sembuf=bV)
        dma("sync", IDN[:], ident_d, writes=[bIDN], sembuf=bIDN)
        op("gpsimd", lambda e: e.memset(NEGH[:], -0.5), writes=[bNEGH], n=1)

        lam = V[:, VC_LAM:VC_LAM + 8]
        t0, t1, t2 = CC[:, 32:40], CC[:, 0:8], CC[:, 8:16]
        op("vector", lambda e: e.tensor_scalar(out=t0, in0=lam, scalar1=-1.0, scalar2=None, op0=ALU.mult), reads=[bV], writes=[bCC], n=8)
        op("scalar", lambda e: e.activation(out=t1, in_=t0, func=AF.Abs), reads=[bCC], writes=[bCC], n=8)
        op("scalar", lambda e: e.activation(out=t1, in_=t1, func=AF.Exp, scale=-1.0), reads=[bCC], writes=[bCC], n=8, tab="ln")
        op("scalar", lambda e: e.activation(out=t1, in_=t1, func=AF.Ln, bias=1.0), reads=[bCC], writes=[bCC], n=8, tab="ln")
        op("vector", lambda e: e.tensor_scalar(out=t0, in0=t0, scalar1=0.0, scalar2=None, op0=ALU.max), reads=[bCC], writes=[bCC], n=8)
        op("vector", lambda e: e.tensor_tensor(out=t0, in0=t0, in1=t1, op=ALU.add), reads=[bCC], writes=[bCC], n=8)
        op("vector", lambda e: e.tensor_scalar(out=t1, in0=t0, scalar1=-8.0, scalar2=None, op0=ALU.mult), reads=[bCC], writes=[bCC], n=8)
        op("vector", lambda e: e.tensor_scalar(out=t2, in0=t0, scalar1=-4.0, scalar2=None, op0=ALU.mult), reads=[bCC], writes=[bCC], n=8)
        op("vector", lambda e: e.tensor_scalar(out=CC[:, 16:32], in0=V[:, VC_BA:VC_BA + 16], scalar1=0.5, scalar2=None, op0=ALU.mult),
           reads=[bV, bCC], writes=[bCC], n=16)
        op("vector", lambda e: e.tensor_tensor(out=CC[:, 40:44], in0=V[:, VC_PB:VC_PB + 4], in1=V[:, VC_PS:VC_PS + 4], op=ALU.mult),
           reads=[bV, bCC], writes=[bCC], n=4)

        def stat_slot():
            j = st_i[0] % 16
            st_i[0] += 1
            return ST[:, 4 * j:4 * j + 4], bST[j]

        def rstd_from(c, b, cols, width):
            if cols == 2:
                op("gpsimd", lambda e: e.tensor_tensor(out=c[:, 0:1], in0=c[:, 0:1], in1=c[:, 1:2], op=ALU.add), reads=[b], writes=[b], n=1, two=True)
            op("gpsimd", lambda e: e.tensor_scalar(out=c[:, 2:3], in0=c[:, 0:1], scalar1=1.0 / width, scalar2=EPS, op0=ALU.mult, op1=ALU.add),
               reads=[b], writes=[b], n=1)
            op("gpsimd", lambda e: e.tensor_tensor(out=c[:, 3:4], in0=c[:, 2:3], in1=NEGH[:], op=ALU.pow), reads=[b, bNEGH], writes=[b], n=1, two=True)
            return c[:, 3:4]

        with ExitStack() as p1:
            WIN = sb(p1, "WIN", [128, 8, 3 * DL], BF16); bWIN = Buf()
            BD = sb(p1, "BD", [128, 16, 128], BF16); bBD1 = Buf()
            PWt = sb(p1, "PWt", [128, 4, 128], BF16); bPWt = Buf()
            IDF = sb(p1, "IDF", [128, 128], F32); bIDF = Buf()
            DG = sb(p1, "DG", [128, 16, 128], F32); bDG = Buf()
            EC = sb(p1, "EC", [128, 4, 16], F32); bEC = Buf()
            ZER = sb(p1, "ZER", [128, T1], F32); bZER = Buf()
            HCt = sb(p1, "HCt", [128, 4], F32); bHC = [Buf() for _ in range(4)]
            XB = Ring([sb(p1, "XB%d" % i, [128, D], F32) for i in range(4)])
            XN = Ring([sb(p1, "XN%d" % i, [128, D], BF16) for i in range(2)])
            HT = [sb(p1, "HT%d" % i, [128, 8, T1], BF16) for i in range(2)]
            bHT = [[Buf() for _ in range(4)] for _ in range(2)]
            RU = [sb(p1, "RU%d" % i, [128, 4, RW], F32) for i in range(2)]
            RG = [sb(p1, "RG%d" % i, [128, 4, RW], F32) for i in range(2)]
            RP = [sb(p1, "RP%d" % i, [128, 4, RW], F32) for i in range(2)]
            bRU = [[Buf() for _ in range(4)] for _ in range(2)]
            bRG = [[Buf() for _ in range(4)] for _ in range(2)]
            bRP = [[Buf() for _ in range(4)] for _ in range(2)]
            bHALO = [[Buf() for _ in range(2)] for _ in range(3)]
            FU = sb(p1, "FU", [128, 4, 32], F32); FG = sb(p1, "FG", [128, 4, 32], F32); FP = sb(p1, "FP", [128, 4, 32], F32)
            bFU = [Buf() for _ in range(4)]; bFG = [Buf() for _ in range(4)]; bFP = [Buf() for _ in range(4)]
            UU = Ring([sb(p1, "UU%d" % i, [128, T1], F32) for i in range(3)])
            UB = Ring([sb(p1, "UB%d" % i, [128, T1], BF16) for i in range(2)])
            THR = Ring([sb(p1, "THR%d" % i, [128, T1], F32) for i in range(2)])
            THI = [Ring([sb(p1, "THI%d_%d" % (d, i), [128, T1], F32) for i in range(2)]) for d in range(2)]
            AAr = [Ring([sb(p1, "AA%d_%d" % (d, i), [128, T1], F32) for i in range(2)]) for d in range(2)]
            A2r = [Ring([sb(p1, "A2%d_%d" % (d, i), [128, T1], F32) for i in range(2)]) for d in range(2)]
            HFr = Ring([sb(p1, "HF%d" % i, [128, T1], F32) for i in range(2)])
            HLr = Ring([sb(p1, "HLr%d" % i, [128, T1], F32) for i in range(2)])
            PCr = Ring([sb(p1, "PC%d" % i, [128, T1], F32) for i in range(2)])
            AOr = Ring([sb(p1, "AO%d" % i, [128, T1], F32) for i in range(2)])
            SGB = {}
            PS1 = sb(p1, "PS1", [128, RW], F32); bPS1 = Buf()
            PS2 = sb(p1, "PS2", [128, RW], F32); bPS2 = Buf()
            PGr = Ring([sb(p1, "PG%d" % i, [128, T1], BF16) for i in range(2)])
            YPr = Ring([sb(p1, "YP%d" % i, [128, T1], BF16) for i in range(2)])
            TPp = Ring([ps(p1, "TP%d" % i, [128, 8, 128], BF16) for i in range(2)], psum=True)
            WPp = Ring([ps(p1, "WP%d" % i, [128, T1], F32) for i in range(2)], psum=True)
            CVp = Ring([ps(p1, "CV%d" % i, [128, T1], F32) for i in range(1)], psum=True)
            GPp = Ring([ps(p1, "GP%d" % i, [128, T1], F32) for i in range(2)], psum=True)
            PMp = Ring([ps(p1, "PM%d" % i, [128, T1], F32) for i in range(1)], psum=True)

            dma("gpsimd", WIN[:], w_in_d.rearrange("(kc p) f -> p kc f", p=128), writes=[bWIN], sembuf=bWIN, nbytes=6 << 20)
            op("gpsimd", lambda e: e.memset(BD[:], 0.0), writes=[bBD1], n=2048)
            for m, wsrc in enumerate((rg_wa_d, rg_wx_d)):
                for d in range(2):
                    for h in range(8):
                        c, hf_ = h // 2, h % 2
                        idx = m * 8 + d * 4 + c
                        dma("gpsimd", BD[hf_ * 64:(hf_ + 1) * 64, idx, hf_ * 64:(hf_ + 1) * 64], wsrc[d, h],
                            writes=[bBD1], sembuf=bBD1, group=True, nbytes=16384)
            dma("gpsimd", PWt[:], pool_w_d.rearrange("g i j -> i g j"), writes=[bPWt], sembuf=bPWt, nbytes=1 << 18)
            dma("sync", EC[:], ec_d, writes=[bEC], sembuf=bEC)
            op("vector", lambda e: e.tensor_copy(out=IDF[:], in_=IDN[:]), reads=[bIDN], writes=[bIDF], n=128)
            for k in range(4):
                for c in range(4):
                    op("vector", lambda e, k=k, c=c: e.tensor_scalar(out=DG[:, 4 * k + c, :], in0=IDF[:], scalar1=V[:, VC_CW + 4 * k + c:VC_CW + 4 * k + c + 1],
                                                                     scalar2=None, op0=ALU.mult),
                       reads=[bIDF, bV, bDG], writes=[bDG], n=128)
            op("gpsimd", lambda e: e.memset(ZER[:], 0.0), writes=[bZER])
            op("gpsimd", lambda e: e.memset(FU[:], 0.0), writes=bFU, n=128)
            op("gpsimd", lambda e: e.memset(FG[:], 0.0), writes=bFG, n=128)
            op("gpsimd", lambda e: e.memset(FP[:], 0.0), writes=bFP, n=128)
            op("gpsimd", lambda e: e.memset(HH[:], 0.0), writes=[b for row in bHH for b in row], n=120)

            def x_block(i, tb):
                slot = i % 2
                r0 = i * T1 + tb * 128
                xb, bxb = XB.next()
                dma("sync", xb[:], x_d[r0:r0 + 128, :], writes=[bxb], sembuf=bxb, nbytes=1 << 19)
                xn, bxn = XN.next()
                c, bst = stat_slot()
                op("scalar", lambda e: e.activation(out=xn[:], in_=xb[:], func=AF.Square, accum_out=c[:, 0:1]), reads=[bxb], writes=[bst, bxn], n=D)
                rstd = rstd_from(c, bst, 1, D)
                op("vector", lambda e: e.tensor_scalar(out=xn[:], in0=xb[:], scalar1=rstd, scalar2=None, op0=ALU.mult), reads=[bxb, bst], writes=[bxn], n=D)
                tp, btp = TPp.next()
                for cc in range(8):
                    op("tensor", lambda e, cc=cc: e.transpose(out=tp[:, cc, :], in_=xn[:, cc * 128:(cc + 1) * 128], identity=IDN[:]),
                       reads=[bxn, bIDN], writes=[btp], n=128)
                op("vector", lambda e: e.tensor_tensor(out=HT[slot][:, :, tb * 128:(tb + 1) * 128], in0=tp[:],
                                                       in1=V[:, VC_G1:VC_G1 + 8].unsqueeze(2).to_broadcast([128, 8, 128]), op=ALU.mult),
                   reads=[btp, bV], writes=[bHT[slot][tb]], n=D)

            def w_chunk(i, oc):
                slot = i % 2
                k = i % NT1
                kind, c = oc // 4, oc % 4
                R, bR = ((RU, bRU), (RG, bRG), (RP, bRP))[kind]
                if c == 0:
                    if k == 0:
                        op("gpsimd", lambda e: e.memset(R[slot][:, :, 0:HALO], 0.0), writes=bR[slot], n=64)
                    else:
                        dma("sync", R[slot][:, :, 0:HALO], R[1 - slot][:, :, T1:RW], reads=bR[1 - slot], writes=bR[slot],
                            sembuf=bHALO[kind][slot], nbytes=32768)
                wp, bwp = WPp.next()
                for kc in range(8):
                    op("tensor", lambda e, kc=kc: e.matmul(out=wp[:], lhsT=WIN[:, kc, oc * 128:(oc + 1) * 128], rhs=HT[slot][:, kc, :],
                                                           start=(kc == 0), stop=(kc == 7)),
                       reads=[bWIN] + bHT[slot], writes=[bwp], n=T1)
                op("scalar", lambda e: e.activation(out=R[slot][:, c, HALO:RW], in_=wp[:], func=AF.Copy), reads=[bwp], writes=[bR[slot][c]], n=T1)

            def w_gelu(i):
                slot = i % 2
                op("scalar", lambda e: e.activation(out=RG[slot][:, :, HALO:RW], in_=RG[slot][:, :, HALO:RW], func=AF.Gelu_apprx_tanh),
                   reads=bRG[slot], writes=bRG[slot], n=4 * T1, tab="gelu")

            def lru_front(c, Ru, bRu, c0, n):
                u, bu = UU.next()
                cv, bcv = CVp.next()
                for k in range(4):
                    op("tensor", lambda e, k=k: e.matmul(out=cv[:, :n], lhsT=DG[:, 4 * k + c, :], rhs=Ru[:, c, c0 - 2 + k:c0 - 2 + k + n],
                                                         start=(k == 0), stop=(k == 3)),
                       reads=[bDG, bRu[c]], writes=[bcv], n=4 * n)
                op("scalar", lambda e: e.activation(out=u[:, :n], in_=cv[:, :n], func=AF.Identity, bias=V[:, VC_CB + c:VC_CB + c + 1], scale=1.0),
                   reads=[bcv, bV], writes=[bu], n=n)
                ub, bub = UB.next()
                op("vector", lambda e: e.tensor_scalar(out=ub[:, :n], in0=cv[:, :n], scalar1=V[:, VC_CB + c:VC_CB + c + 1], scalar2=None, op0=ALU.add),
                   reads=[bcv, bV], writes=[bub], n=n)
                res = {"u": (u, bu)}

                def per_dir(d):
                    j = d * 4 + c
                    rp, brp = GPp.next()
                    op("tensor", lambda e: e.matmul(out=rp[:, :n], lhsT=BD[:, j, :], rhs=ub[:, :n], start=True, stop=True),
                       reads=[bBD1, bub], writes=[brp], n=n)
                    thr, bthr = THR.next()
                    op("scalar", lambda e: e.activation(out=thr[:, :n], in_=rp[:, :n], func=AF.Tanh, scale=0.5, bias=CC[:, 16 + j:17 + j]),
                       reads=[brp, bCC], writes=[bthr], n=n, tab="exp")
                    ip, bip = GPp.next()
                    op("tensor", lambda e: e.matmul(out=ip[:, :n], lhsT=BD[:, 8 + j, :], rhs=ub[:, :n], start=True, stop=True),
                       reads=[bBD1, bub], writes=[bip], n=n)
                    thi, bthi = THI[d].next()
                    op("scalar", lambda e: e.activation(out=thi[:, :n], in_=ip[:, :n], func=AF.Tanh, scale=0.5, bias=CC[:, 24 + j:25 + j]),
                       reads=[bip, bCC], writes=[bthi], n=n, tab="exp")
                    aa, baa = AAr[d].next()
                    op("scalar", lambda e: e.activation(out=aa[:, :n], in_=thr[:, :n], func=AF.Exp, scale=CC[:, 8 + j:9 + j], bias=CC[:, 8 + j:9 + j]),
                       reads=[bthr, bCC], writes=[baa], n=n, tab="exp")
                    a2, ba2 = A2r[d].next()
                    if d == 0:
                        op("scalar", lambda e: e.activation(out=a2[:, :n], in_=thr[:, :n], func=AF.Exp, scale=CC[:, j:j + 1], bias=CC[:, j:j + 1]),
                           reads=[bthr, bCC], writes=[ba2], n=n, tab="exp")
                    else:
                        op("gpsimd", lambda e: e.tensor_tensor(out=a2[:, :n], in0=aa[:, :n], in1=aa[:, :n], op=ALU.mult),
                           reads=[baa], writes=[ba2], n=n, two=True)
                    op("vector", lambda e: e.scalar_tensor_tensor(out=thi[:, :n], in0=thi[:, :n], scalar=1.0, in1=u[:, :n], op0=ALU.add, op1=ALU.mult),
                       reads=[bthi, bu], writes=[bthi], n=1.2 * n)
                    res[d] = ((thi, bthi), (aa, baa), (a2, ba2))
                per_dir(0)
                per_dir(1)
                return res

            def lru_back(c, res, Rg, bRg, c0, n, s, kst, tok0):
                def sq_dir(d):
                    (thi, bthi), (aa, baa), (a2, ba2) = res[d]
                    op("scalar", lambda e: e.activation(out=a2[:, :n], in_=a2[:, :n], func=AF.Sqrt, scale=-0.25, bias=0.25),
                       reads=[ba2], writes=[ba2], n=n, tab="sqrt")
                    op("gpsimd", lambda e: e.tensor_tensor(out=thi[:, :n], in0=thi[:, :n], in1=a2[:, :n], op=ALU.mult),
                       reads=[bthi, ba2], writes=[bthi], n=n, two=True)
                sq_dir(0)
                sq_dir(1)
                (bt0, bbt0), (a0, ba0), _ = res[0]
                (bt1, bbt1), (a1, ba1), _ = res[1]
                hf, bhf = HFr.next()
                op("vector", lambda e: e.tensor_tensor_scan(out=hf[:, :n], data0=a0[:, :n], data1=bt0[:, :n], initial=HCt[:, c:c + 1],
                                                            op0=ALU.mult, op1=ALU.add),
                   reads=[ba0, bbt0, bHC[c]], writes=[bhf], n=2 * n)
                dma("sync", HCt[:, c:c + 1], hf[:, n - 1:n], reads=[bhf], writes=[bHC[c]], sembuf=bHC[c], nbytes=512)
                hl, bhl = HLr.next()
                op("vector", lambda e: e.tensor_tensor_scan(out=hl[:, 0:n][:, ::-1], data0=a1[:, 0:n][:, ::-1], data1=bt1[:, 0:n][:, ::-1],
                                                            initial=0.0, op0=ALU.mult, op1=ALU.add),
                   reads=[ba1, bbt1], writes=[bhl], n=2 * n)
                pc, bpc = PCr.next()
                op("vector", lambda e: e.tensor_tensor_scan(out=pc[:, 0:n][:, ::-1], data0=a1[:, 0:n][:, ::-1], data1=ZER[:, :n],
                                                            initial=1.0, op0=ALU.mult, op1=ALU.add),
                   reads=[ba1, bZER], writes=[bpc], n=2 * n)
                hi = hidx(s, kst)
                ao, bao = AOr.next()
                op("gpsimd", lambda e: e.tensor_tensor(out=ao[:, :n], in0=hf[:, :n], in1=hl[:, :n], op=ALU.add), reads=[bhf, bhl], writes=[bao], n=n, two=True)
                dma("sync", sa_d[s, c, :, tok0:tok0 + n], ao[:, :n], reads=[bao], sembuf=bao, nbytes=n * 512)
                dma("sync", sb_d[s, c, :, tok0:tok0 + n], pc[:, :n], reads=[bpc], sembuf=bpc, nbytes=n * 512)
                dma("sync", HL[:, hi, c:c + 1], hl[:, 0:1], reads=[bhl], writes=[bHL[s][kst]], sembuf=SGB.setdefault(("hl", id(hl)), Buf()), group=True, nbytes=512)
                dma("sync", PPt[:, hi, c:c + 1], pc[:, 0:1], reads=[bpc], writes=[bHL[s][kst]], sembuf=SGB.setdefault(("pp", id(pc)), Buf()), group=True, nbytes=512)
                sgb = SGB.setdefault(id(Rg), [Buf() for _ in range(4)])[c]
                ta = tok0
                while ta < tok0 + n:
                    k2 = ta // T2
                    tb_ = min(tok0 + n, (k2 + 1) * T2)
                    row0 = (s * NT2 + k2) * T2
                    dma("sync", y_d[row0:row0 + 128, c * T2 + (ta - k2 * T2):c * T2 + (tb_ - k2 * T2)],
                        Rg[:, c, c0 + (ta - tok0):c0 + (tb_ - tok0)], reads=[bRg[c]], sembuf=sgb, group=False, nbytes=(tb_ - ta) * 512)
                    ta = tb_

            def pool_group(g, Rp, bRp, c0, n, s, tok0, first, last):
                w = WINS[g]
                half = w // 2
                src = Rp[:, g, :]
                width = c0 + n + 8
                cur, bcur = src, bRp[g]
                tmps = [(PS1, bPS1), (PS2, bPS2)]
                step = 1
                ti = 0
                while step < half:
                    (dst, bdst) = tmps[ti % 2]
                    ti += 1
                    L = width - (2 * step - 1)
                    op("gpsimd", lambda e, cur=cur, dst=dst, step=step, L=L: e.tensor_tensor(out=dst[:, 0:L], in0=cur[:, 0:L], in1=cur[:, step:step + L], op=ALU.add),
                       reads=[bcur], writes=[bdst], n=L, two=True)
                    cur, bcur = dst, bdst
                    step *= 2
                (dst, bdst) = tmps[ti % 2]
                op("gpsimd", lambda e: e.tensor_tensor(out=dst[:, 0:n], in0=cur[:, c0 - half:c0 - half + n], in1=cur[:, c0:c0 + n], op=ALU.add),
                   reads=[bcur], writes=[bdst], n=n, two=True)
                pg, bpg = PGr.next()
                op("vector", lambda e: e.scalar_tensor_tensor(out=pg[:, :n], in0=dst[:, 0:n], scalar=1.0 / w, in1=src[:, c0:c0 + n],
                                                              op0=ALU.mult, op1=ALU.subtract),
                   reads=[bdst, bRp[g]], writes=[bpg], n=1.5 * n)
                if first or last:
                    e0, off = (0, 0) if first else (8, n - 8)
                    op("vector", lambda e: e.tensor_tensor(out=dst[:, off:off + 8], in0=dst[:, off:off + 8], in1=EC[:, g, e0:e0 + 8], op=ALU.mult),
                       reads=[bdst, bEC], writes=[bdst], n=8)
                    op("vector", lambda e: e.tensor_tensor(out=pg[:, off:off + 8], in0=dst[:, off:off + 8], in1=src[:, c0 + off:c0 + off + 8], op=ALU.subtract),
                       reads=[bdst, bRp[g]], writes=[bpg], n=8)
                pm, bpm = PMp.next()
                op("tensor", lambda e: e.matmul(out=pm[:, :n], lhsT=PWt[:, g, :], rhs=pg[:, :n], start=True, stop=True), reads=[bPWt, bpg], writes=[bpm], n=n)
                yp, byp = YPr.next()
                op("scalar", lambda e: e.activation(out=yp[:, :n], in_=pm[:, :n], func=AF.Identity, scale=V[:, VC_PS + g:VC_PS + g + 1],
                                                    bias=CC[:, 40 + g:41 + g]),
                   reads=[bpm, bV, bCC], writes=[byp], n=n)
                dma("sync", sy_d[s, g, :, tok0:tok0 + n], yp[:, :n], reads=[byp], sembuf=byp, nbytes=n * 256)

            def mix_stage(s, kst, Ru, bRu, Rg, bRg, Rp, bRp, c0, n, tok0):
                first = (kst == 0)
                last = (kst == NT1)
                fr = {}
                fr[0] = lru_front(0, Ru, bRu, c0, n)
                fr[1] = lru_front(1, Ru, bRu, c0, n)
                for c in range(4):
                    lru_back(c, fr[c], Rg, bRg, c0, n, s, kst, tok0)
                    if c + 2 < 4:
                        fr[c + 2] = lru_front(c + 2, Ru, bRu, c0, n)
                    pool_group(c, Rp, bRp, c0, n, s, tok0, first, last)

            NTILES = NSEQ * NT1
            for tb in range(4):
                x_block(0, tb)
            for oc in range(12):
                w_chunk(0, oc)
            w_gelu(0)
            for tb in range(4):
                x_block(1, tb)
            for i in range(NTILES):
                s, k = i // NT1, i % NT1
                slot = i % 2
                if k == 0:
                    for c in range(4):
                        op("gpsimd", lambda e, c=c: e.memset(HCt[:, c:c + 1], 0.0), writes=[bHC[c]], n=1)
                if i + 1 < NTILES:
                    for oc in range(12):
                        w_chunk(i + 1, oc)
                    w_gelu(i + 1)
                if i + 2 < NTILES:
                    for tb in range(4):
                        x_block(i + 2, tb)
                if k == 0:
                    mix_stage(s, 0, RU[slot], bRU[slot], RG[slot], bRG[slot], RP[slot], bRP[slot], HALO, T1 - 8, 0)
                else:
                    mix_stage(s, k, RU[slot], bRU[slot], RG[slot], bRG[slot], RP[slot], bRP[slot], 8, T1, k * T1 - 8)
                if k == NT1 - 1:
                    for c in range(4):
                        op("gpsimd", lambda e, c=c, slot=slot: e.tensor_copy(out=FU[:, c, 0:HALO], in_=RU[slot][:, c, T1:RW]), reads=[bRU[slot][c]], writes=[bFU[c]], n=16)
                        op("gpsimd", lambda e, c=c, slot=slot: e.tensor_copy(out=FG[:, c, 0:HALO], in_=RG[slot][:, c, T1:RW]), reads=[bRG[slot][c]], writes=[bFG[c]], n=16)
                        op("gpsimd", lambda e, c=c, slot=slot: e.tensor_copy(out=FP[:, c, 0:HALO], in_=RP[slot][:, c, T1:RW]), reads=[bRP[slot][c]], writes=[bFP[c]], n=16)
                    mix_stage(s, NT1, FU, bFU, FG, bFG, FP, bFP, 8, 8, SEQ - 8)
                    for kk in range(NT1, -1, -1):
                        hi, hn = hidx(s, kk), hidx(s, kk + 1)
                        op("vector", lambda e, hi=hi, hn=hn: e.tensor_tensor(out=HH[:, hi, :], in0=PPt[:, hi, :], in1=HH[:, hn, :], op=ALU.mult),
                           reads=[bHL[s][kk], bHH[s][kk + 1]], writes=[bHH[s][kk]], n=4)
                        op("vector", lambda e, hi=hi: e.tensor_tensor(out=HH[:, hi, :], in0=HH[:, hi, :], in1=HL[:, hi, :], op=ALU.add),
                           reads=[bHL[s][kk], bHH[s][kk]], writes=[bHH[s][kk]], n=4)
            S.flush()

        with ExitStack() as p2:
            WOUT = sb(p2, "WOUT", [128, 8, D], BF16); bWOUT = Buf()
            dma("gpsimd", WOUT[:], w_out_d.rearrange("(kc p) f -> p kc f", p=128), writes=[bWOUT], sembuf=bWOUT, nbytes=4 << 20)
            WG = sb(p2, "WG", [128, 8, DFF], BF16); bWGp = [Buf() for _ in range(4)]
            WU = sb(p2, "WU", [128, 8, DFF], BF16); bWUp = [Buf() for _ in range(4)]
            WD = sb(p2, "WD", [128, NFF, D], BF16); bWDp = [Buf() for _ in range(4)]
            JP = (0, 4, 10, 16, NFF)

            def piece_of(jf):
                return max(q for q in range(4) if JP[q] <= jf)
            G2 = sb(p2, "G2", [128, D], F32); bG2 = Buf()
            G4 = sb(p2, "G4", [128, D], F32); bG4 = Buf()
            X2 = [sb(p2, "X2_%d" % i, [128, 2, D], F32) for i in range(2)]
            bX2 = [[Buf() for _ in range(2)] for _ in range(2)]
            At = sb(p2, "At", [128, 4, T2], F32); bAt = Buf()
            Bt = sb(p2, "Bt", [128, 4, T2], F32); bBt = Buf()
            Gt = sb(p2, "Gt", [128, 4, T2], F32); bGt = Buf()
            YT = sb(p2, "YT", [128, 8, T2], BF16); bYTl = Buf(); bYTp = Buf()
            H2T = sb(p2, "H2T", [128, 8, T2], BF16); bH2T = [Buf() for _ in range(2)]
            XN2 = Ring([sb(p2, "XN2_%d" % i, [128, D], BF16) for i in range(2)])
            TMP = Ring([sb(p2, "TMP%d" % i, [128, D], F32) for i in range(1)])
            SG = Ring([sb(p2, "SG%d" % i, [128, T2], F32) for i in range(2)])
            ACr = Ring([sb(p2, "AC%d" % i, [128, T2], BF16) for i in range(5)])
            MF = [ps(p2, "MF%d" % i, [128, D], F32) for i in range(2)]; bMF = [Buf(psum=True) for _ in range(2)]
            GUp = Ring([ps(p2, "GU%d" % i, [128, 2, T2], F32) for i in range(3)], psum=True)
            AUX = ps(p2, "AUX", [128, 512], F32); bAUX = Buf(psum=True)
            AUXT = AUX[:].bitcast(BF16).rearrange("p (c t) -> p c t", c=8)

            dma("sync", G2[:], g2_d.partition_broadcast(128), writes=[bG2], sembuf=bG2, nbytes=1 << 19)
            dma("sync", G4[:], g4_d.partition_broadcast(128), writes=[bG4], sembuf=bG4, nbytes=1 << 19)
            wg_v = w_gate_d.rearrange("(kc p) f -> p kc f", p=128)
            wu_v = w_up_d.rearrange("(kc p) f -> p kc f", p=128)
            wd_v = w_down_d.rearrange("(j p) f -> p j f", p=128)
            for q in range(4):
                f0, f1 = JP[q] * 128, JP[q + 1] * 128
                nb = (f1 - f0) * 4096
                dma("gpsimd", WG[:, :, f0:f1], wg_v[:, :, f0:f1], reads=[bWOUT], writes=[bWGp[q]], sembuf=bWGp[q], nbytes=nb)
                dma("gpsimd", WU[:, :, f0:f1], wu_v[:, :, f0:f1], reads=[bWOUT], writes=[bWUp[q]], sembuf=bWUp[q], nbytes=nb)
                dma("gpsimd", WD[:, JP[q]:JP[q + 1], :], wd_v[:, JP[q]:JP[q + 1], :], reads=[bWOUT], writes=[bWDp[q]], sembuf=bWDp[q], nbytes=nb)

            NT = NSEQ * NT2

            def tile_info(i2):
                s, j = i2 // NT2, i2 % NT2
                return s, j, i2 % 2, j * T2, s * SEQ + j * T2

            def front_pieces(i2):
                s, j, slot, t0_, r0 = tile_info(i2)
                pieces = []

                def loads():
                    dma("sync", X2[slot][:], x_d[r0:r0 + T2, :].rearrange("(b p) f -> p b f", p=128), writes=bX2[slot], sembuf=bX2[slot][0], nbytes=1 << 20)
                    dma("sync", At[:], sa_d[s, :, :, t0_:t0_ + T2].rearrange("c p t -> p c t"), writes=[bAt], sembuf=bAt, nbytes=1 << 19)
                    dma("sync", Bt[:], sb_d[s, :, :, t0_:t0_ + T2].rearrange("c p t -> p c t"), writes=[bBt], sembuf=bBt, nbytes=1 << 19)
                    dma("sync", Gt[:], y_d[r0:r0 + 128, :].rearrange("p (c t) -> p c t", c=4), writes=[bGt], sembuf=bGt, nbytes=1 << 19)
                    dma("sync", YT[:, 4:8, :], sy_d[s, :, :, t0_:t0_ + T2].rearrange("c p t -> p c t"), writes=[bYTp], sembuf=bYTp, nbytes=1 << 18)
                pieces.append(loads)

                def ylru():
                    kst = j // 2
                    for c in range(4):
                        segs = [(0, T2, kst + 1)] if j % 2 == 0 else [(0, T2 - 8, kst + 1), (T2 - 8, T2, kst + 2)]
                        for (a_, b_, kh) in segs:
                            hi = hidx(s, kh)
                            op("vector", lambda e, c=c, a_=a_, b_=b_, hi=hi: e.scalar_tensor_tensor(
                                out=At[:, c, a_:b_], in0=Bt[:, c, a_:b_], scalar=HH[:, hi, c:c + 1], in1=At[:, c, a_:b_], op0=ALU.mult, op1=ALU.add),
                               reads=[bAt, bBt, bHH[s][kh]], writes=[bAt], n=1.2 * (b_ - a_))
                    op("vector", lambda e: e.tensor_tensor(out=YT[:, 0:4, :], in0=At[:], in1=Gt[:], op=ALU.mult), reads=[bAt, bGt], writes=[bYTl], n=4 * T2)
                pieces.append(ylru)

                state = {}

                def wout_half(tb, hf_):
                    def f():
                        if hf_ == 0:
                            state[tb] = (stat_slot(), TMP.next(), XN2.next())
                        (c, bst), (tmp, btmp), (xn, bxn) = state[tb]
                        for kc in range(8):
                            op("tensor", lambda e, kc=kc: e.matmul(out=AUX[:], lhsT=YT[:, kc, tb * 128:(tb + 1) * 128],
                                                                   rhs=WOUT[:, kc, hf_ * 512:(hf_ + 1) * 512], start=(kc == 0), stop=(kc == 7)),
                               reads=[bYTl, bYTp, bWOUT], writes=[bAUX], n=512)
                        op("scalar", lambda e: e.activation(out=xn[:, hf_ * 512:(hf_ + 1) * 512], in_=AUX[:], func=AF.Square, accum_out=c[:, hf_:hf_ + 1]),
                           reads=[bAUX], writes=[bst, bxn], n=512)
                        op("vector", lambda e: e.tensor_tensor(out=tmp[:, hf_ * 512:(hf_ + 1) * 512], in0=AUX[:], in1=G2[:, hf_ * 512:(hf_ + 1) * 512], op=ALU.mult),
                           reads=[bAUX, bG2], writes=[btmp], n=512)
                    return f

                def norm_block(tb):
                    def f():
                        (c, bst), (tmp, btmp), (xn, bxn) = state[tb]
                        xrow, bx = X2[slot][:, tb, :], bX2[slot][tb]
                        rstd = rstd_from(c, bst, 2, D)
                        op("vector", lambda e: e.scalar_tensor_tensor(out=xrow, in0=tmp[:], scalar=rstd, in1=xrow, op0=ALU.mult, op1=ALU.add),
                           reads=[btmp, bst, bx], writes=[bx], n=1.2 * D)
                        c2, bst2 = stat_slot()
                        op("scalar", lambda e: e.activation(out=xn[:], in_=xrow, func=AF.Square, accum_out=c2[:, 0:1]), reads=[bx], writes=[bst2, bxn], n=D)
                        rstd2 = rstd_from(c2, bst2, 1, D)
                        op("scalar", lambda e: e.activation(out=xn[:], in_=xrow, func=AF.Copy, scale=rstd2), reads=[bx, bst2], writes=[bxn], n=D)
                    return f

                for tb in range(2):
                    pieces.append(wout_half(tb, 0))
                    pieces.append(wout_half(tb, 1))
                    pieces.append(norm_block(tb))

                def transposes():
                    for tb in range(2):
                        (c, bst), (tmp, btmp), (xn, bxn) = state[tb]
                        if tb == 0:
                            tpv, btpv = AUXT, bAUX
                        else:
                            gu, btpv = GUp.next()
                            tpv = gu[:].rearrange("p a t -> p (a t)").bitcast(BF16).rearrange("p (c t) -> p c t", c=8)
                        for cc in range(8):
                            op("tensor", lambda e, cc=cc, xn=xn, tpv=tpv: e.transpose(out=tpv[:, cc, :], in_=xn[:, cc * 128:(cc + 1) * 128], identity=IDN[:]),
                               reads=[bxn, bIDN], writes=[btpv], n=128)
                        op("vector", lambda e, tb=tb, tpv=tpv: e.tensor_tensor(out=H2T[:, :, tb * 128:(tb + 1) * 128], in0=tpv,
                                                                               in1=V[:, VC_G3:VC_G3 + 8].unsqueeze(2).to_broadcast([128, 8, 128]), op=ALU.mult),
                           reads=[btpv, bV], writes=[bH2T[tb]], n=D)
                return pieces, transposes

            def down(i2, jf, ac, bac):
                for tb in range(2):
                    for hf_ in range(2):
                        op("tensor", lambda e, tb=tb, hf_=hf_: e.matmul(
                            out=MF[tb][:, hf_ * 512:(hf_ + 1) * 512], lhsT=ac[:, tb * 128:(tb + 1) * 128],
                            rhs=WD[:, jf, hf_ * 512:(hf_ + 1) * 512], start=(jf == 0), stop=(jf == NFF - 1)),
                           reads=[bac, bWDp[piece_of(jf)]], writes=[bMF[tb]], n=512)

            def final_block(i2, tb):
                s, j, slot, t0_, r0 = tile_info(i2)
                xrow, bx = X2[slot][:, tb, :], bX2[slot][tb]
                c, bst = stat_slot()
                tmp, btmp = TMP.next()
                op("scalar", lambda e: e.activation(out=tmp[:].bitcast(BF16)[:, 0:D], in_=MF[tb][:], func=AF.Square, accum_out=c[:, 0:1]),
                   reads=[bMF[tb]], writes=[bst, btmp], n=D)
                op("vector", lambda e: e.tensor_tensor(out=tmp[:], in0=MF[tb][:], in1=G4[:], op=ALU.mult), reads=[bMF[tb], bG4], writes=[btmp], n=D)
                rstd = rstd_from(c, bst, 1, D)
                op("vector", lambda e: e.scalar_tensor_tensor(out=xrow, in0=tmp[:], scalar=rstd, in1=xrow, op0=ALU.mult, op1=ALU.add),
                   reads=[btmp, bst, bx], writes=[bx], n=1.2 * D)

            pieces, transposes = front_pieces(0)
            for p in pieces:
                p()
            transposes()
            for i2 in range(NT):
                s, j, slot, t0_, r0 = tile_info(i2)
                if i2 + 1 < NT:
                    nxt_pieces, nxt_transposes = front_pieces(i2 + 1)
                else:
                    nxt_pieces, nxt_transposes = [], None
                at = {0: 0, 1: 1, 3: 2, 5: 3, 7: 4, 10: 5, 12: 6, 14: 7}
                prevq = []
                for jf in range(NFF):
                    gu, bgu = GUp.next()
                    for kc in range(8):
                        op("tensor", lambda e, kc=kc, gu=gu, jf=jf: e.matmul(out=gu[:, 0, :], lhsT=WG[:, kc, jf * 128:(jf + 1) * 128], rhs=H2T[:, kc, :],
                                                                            start=(kc == 0), stop=(kc == 7)),
                           reads=[bWGp[piece_of(jf)]] + bH2T, writes=[bgu], n=T2)
                    for kc in range(8):
                        op("tensor", lambda e, kc=kc, gu=gu, jf=jf: e.matmul(out=gu[:, 1, :], lhsT=WU[:, kc, jf * 128:(jf + 1) * 128], rhs=H2T[:, kc, :],
                                                                            start=(kc == 0), stop=(kc == 7)),
                           reads=[bWUp[piece_of(jf)]] + bH2T, writes=[bgu], n=T2)
                    if len(prevq) >= 3:
                        down(i2, *prevq.pop(0))
                    if jf in at and at[jf] < len(nxt_pieces):
                        nxt_pieces[at[jf]]()
                    sg, bsg = SG.next()
                    op("scalar", lambda e, sg=sg, gu=gu: e.activation(out=sg[:], in_=gu[:, 0, :], func=AF.Silu), reads=[bgu], writes=[bsg], n=T2, tab="silu")
                    ac, bac = ACr.next()
                    op("vector", lambda e, sg=sg, gu=gu, ac=ac: e.tensor_tensor(out=ac[:], in0=sg[:], in1=gu[:, 1, :], op=ALU.mult), reads=[bsg, bgu], writes=[bac], n=T2)
                    prevq.append((jf, ac, bac))
                down(i2, *prevq.pop(0))
                if nxt_transposes is not None:
                    nxt_transposes()
                down(i2, *prevq.pop(0))
                down(i2, *prevq.pop(0))
                for tb in range(2):
                    final_block(i2, tb)
                dma("sync", y_d[r0:r0 + T2, :].rearrange("(b p) f -> p b f", p=128), X2[slot][:], reads=bX2[slot], sembuf=bX2[slot][1], nbytes=1 << 20)
            S.flush(final_engines=["sync"])
        print("[kernel] ninst=%d nwaits=%d nsems=%d sim_us=%.0f" % (S.ninst, S.nwaits, len(S.sems), S.sim_time / 1e3))
    return nc


_NC_CACHE = {}


def _cols(v, n):
    return np.ascontiguousarray(np.asarray(v, np.float32).reshape(n, 128).T)


def kernel(x_prompt, x_sample, pre_norm_mix, post_norm_mix, w_in, conv_w, conv_b, rg_wa, rg_ba, rg_wx, rg_bx,
           rg_lam, pool_w, pool_b, pool_scale, w_out, pre_norm_ffn, post_norm_ffn, w_gate, w_up, w_down):
    f = lambda a: np.ascontiguousarray(np.asarray(a, dtype=np.float32))
    x_prompt, x_sample = f(x_prompt), f(x_sample)
    vecs = np.concatenate([
        _cols(conv_w[0], 16), _cols(conv_b[0], 4), _cols(rg_ba[0], 8), _cols(rg_bx[0], 8), _cols(rg_lam[0], 8),
        _cols(pool_b[0], 4), _cols(pool_scale[0], 4), _cols(pre_norm_mix[0], 8), _cols(pre_norm_ffn[0], 8)], axis=1)
    assert vecs.shape == (128, NV)
    ident = np.eye(128, dtype=np.float32).astype(ml_dtypes.bfloat16)
    ec = np.zeros((128, 4, 16), np.float32)
    for g, w in enumerate(WINS):
        half = w // 2
        for e in range(16):
            t = e if e < 8 else SEQ - 16 + e
            cnt = min(t + half, SEQ) - max(t - half, 0)
            ec[:, g, e] = 1.0 / cnt
    common = {
        "vecs": np.ascontiguousarray(vecs), "g2": f(post_norm_mix[0]).reshape(1, D), "g4": f(post_norm_ffn[0]).reshape(1, D),
        "ident": ident, "ec": ec, "w_in": f(w_in[0]), "rg_wa": f(rg_wa[0]), "rg_wx": f(rg_wx[0]), "pool_w": f(pool_w[0]),
        "w_out": f(w_out[0]), "w_gate": f(w_gate[0]), "w_up": f(w_up[0]), "w_down": f(w_down[0]),
    }
    in_maps = []
    for c in range(NCORES):
        xs = np.concatenate([x_prompt[2 * c].reshape(SEQ, D), x_prompt[2 * c + 1].reshape(SEQ, D), x_sample[c].reshape(SEQ, D)], axis=0)
        m = dict(common)
        m["x"] = np.ascontiguousarray(xs)
        in_maps.append(m)
    if "nc" not in _NC_CACHE:
        _NC_CACHE["nc"] = build_program()
    nc = _NC_CACHE["nc"]
    res = run_bass_kernel_spmd(nc, in_maps, core_ids=list(range(NCORES)))
    y_prompt = np.empty_like(x_prompt)
    y_sample = np.empty_like(x_sample)
    for c in range(NCORES):
        y = np.asarray(res.results[c]["y"], dtype=np.float32).reshape(NSEQ, SEQ, D)
        y_prompt[2 * c] = y[0]
        y_prompt[2 * c + 1] = y[1]
        y_sample[c] = y[2]
    return (y_prompt, y_sample)
```

```python
import numpy as np
import ml_dtypes
from contextlib import ExitStack
import concourse.bass as bass
import concourse.mybir as mybir
from concourse.bass_utils import run_bass_kernel_spmd

F32 = mybir.dt.float32
BF16 = mybir.dt.bfloat16
F32R = mybir.dt.float32r
AF = mybir.ActivationFunctionType
ALU = mybir.AluOpType

NCORES = 8
D = 1024
SEQ = 4096
NSEQ = 3
NTOK = NSEQ * SEQ
DL = 512
DFF = 2816
NFF = DFF // 128
T1 = 512
HALO = 16
RW = T1 + HALO
NT1 = SEQ // T1
NSTG = NT1 + 1
T2 = 256
NT2 = SEQ // T2
EPS = 1e-6
WINS = (2, 4, 8, 16)

VC_CW, VC_CB, VC_BA, VC_BX, VC_LAM, VC_PB, VC_PS, VC_G1, VC_G3, NV = 0, 16, 20, 28, 36, 44, 48, 52, 60, 68


VERBOSE = False


class Buf:
    __slots__ = ("name", "w", "r", "dsem", "grp", "psum", "gbase")

    def __init__(self, name="", psum=False):
        self.name = name
        self.psum = psum
        self.w = None
        self.r = []
        self.grp = []
        self.gbase = set()
        self.dsem = None


class Node:
    __slots__ = ("id", "eng", "kind", "fn", "out", "in_", "sem", "deps", "cost", "lat", "start", "fin", "val", "clock", "tab")


class Sched:
    ENGS = ("tensor", "vector", "scalar", "gpsimd", "sync")

    def __init__(self, nc, es):
        self.nc = nc
        self.es = es
        self.sems = []
        self.semcount = []
        self.engsem = {}
        self.known = {}
        for n in self.ENGS:
            h = es.enter_context(nc.semaphore("e_" + n))
            self.sems.append(h)
            self.semcount.append(0)
            self.engsem[n] = len(self.sems) - 1
            self.known[n] = {}
        self.nodes = []
        self.done = {}
        self.nid = 0
        self.nwaits = 0
        self.ninst = 0
        self.sim_time = 0.0

    @staticmethod
    def _deps(reads, writes):
        deps = set()
        for b in reads:
            if b.w is not None:
                deps.add(b.w)
            deps.update(b.grp)
            if b.psum:
                deps.update(b.r)
        for b in writes:
            if b.w is not None:
                deps.add(b.w)
            deps.update(b.grp)
            deps.update(b.r)
        return deps

    def _node(self, eng, kind, deps, cost, lat):
        n = Node()
        n.id = self.nid
        self.nid += 1
        n.eng, n.kind, n.deps, n.cost, n.lat = eng, kind, deps, cost, lat
        n.fn = n.out = n.in_ = n.sem = None
        n.tab = None
        self.nodes.append(n)
        self.ninst += 1
        return n

    def op(self, eng, fn, reads=(), writes=(), n=512, two=False, tab=None):
        if eng == "tensor":
            cost = 8.0 + n / 2.35
        elif eng == "scalar":
            cost = 200.0 + 0.78 * n
        elif eng == "vector":
            cost = 100.0 + 1.25 * n
        else:
            cost = 400.0 + (2.3 if two else 0.95) * n
        n = self._node(eng, "op", self._deps(reads, writes), cost, cost)
        n.fn = fn
        n.tab = tab
        n.sem = self.engsem[eng]
        for b in writes:
            b.w = n.id
            b.r = []
            b.grp = []
        for b in reads:
            if b.w != n.id:
                b.r.append(n.id)
        return n.id

    def dma(self, q, out, in_, reads=(), writes=(), sembuf=None, group=False, nbytes=65536):
        if sembuf.dsem is None:
            h = self.es.enter_context(self.nc.semaphore("d%d" % len(self.sems)))
            self.sems.append(h)
            self.semcount.append(0)
            sembuf.dsem = len(self.sems) - 1
        deps = self._deps(reads, writes)
        if group:
            for b in writes:
                if not b.grp:
                    b.gbase = set(deps)
                deps -= set(b.grp)
                deps |= b.gbase
        issue = 1000.0 if q == "gpsimd" else 60.0
        n = self._node(q, "dma", deps, issue, 2000.0 + nbytes / 150.0)
        n.out, n.in_, n.sem = out, in_, sembuf.dsem
        for b in writes:
            if group:
                b.grp.append(n.id)
            else:
                b.grp = []
                b.r = []
            b.w = n.id
        for b in reads:
            b.r.append(n.id)
        return n.id

    def flush(self, final_engines=None):
        import heapq
        nodes = self.nodes
        self.nodes = []
        byid = {n.id: n for n in nodes}
        ndep = {}
        users = {}
        ready_t = {}
        for n in nodes:
            cnt = 0
            for d in n.deps:
                if d in byid:
                    cnt += 1
                    users.setdefault(d, []).append(n.id)
            ndep[n.id] = cnt
            ready_t[n.id] = 0.0
        free = {e: 0.0 for e in self.ENGS}
        pend = {e: [] for e in self.ENGS}
        avail = {e: [] for e in self.ENGS}
        for n in nodes:
            if ndep[n.id] == 0:
                heapq.heappush(pend[n.eng], (0.0, n.id))
        order = []
        remaining = len(nodes)
        SEMLAT = 150.0
        cur_tab = [None]
        while remaining:
            best = None
            for e in self.ENGS:
                if avail[e]:
                    t = free[e]
                elif pend[e]:
                    t = max(free[e], pend[e][0][0])
                else:
                    continue
                if best is None or t < best[0]:
                    best = (t, e)
            t, e = best
            while pend[e] and pend[e][0][0] <= t:
                rt, i = heapq.heappop(pend[e])
                heapq.heappush(avail[e], i)
            extra = 0.0
            if e == "scalar":
                lst = avail[e]
                same = [i for i in lst if byid[i].tab is None or byid[i].tab == cur_tab[0]]
                nsq = sum(1 for i in lst if byid[i].tab == "sqrt")
                if cur_tab[0] != "sqrt" and 0 < nsq < 4:
                    held = [i for i in lst if byid[i].tab == "sqrt" and t - ready_t[i] < 8000.0]
                    rest = [i for i in lst if i not in held]
                    if rest:
                        lst2 = rest
                    else:
                        lst2 = lst
                else:
                    lst2 = lst
                oldest = min(lst2)
                if same and not (byid[oldest].tab not in (None, cur_tab[0]) and t - ready_t[oldest] > 25000.0):
                    i = min(same)
                else:
                    i = oldest
                lst.remove(i)
                heapq.heapify(lst)
                if byid[i].tab is not None and byid[i].tab != cur_tab[0]:
                    cur_tab[0] = byid[i].tab
                    extra = 1300.0
            else:
                i = heapq.heappop(avail[e])
            n = byid[i]
            n.start = t + extra
            t = t + extra
            free[e] = t + n.cost
            n.fin = t + n.lat
            order.append(n)
            remaining -= 1
            for u in users.get(i, ()):
                ndep[u] -= 1
                if ready_t[u] < n.fin + SEMLAT:
                    ready_t[u] = n.fin + SEMLAT
                if ndep[u] == 0:
                    un = byid[u]
                    heapq.heappush(pend[un.eng], (ready_t[u], u))
        self.sim_time += max([n.fin for n in order] + [0.0])
        if VERBOSE:
            mk = max([n.fin for n in order] + [0.0])
            busy = {e: sum(n.cost for n in order if n.eng == e) for e in self.ENGS}
            print("[sched] phase makespan %.0f us; busy " % (mk / 1e3) + " ".join("%s=%.0f%%" % (e[:4], 100 * busy[e] / mk) for e in self.ENGS))
        prog = {e: [] for e in self.ENGS}
        for n in order:
            E = n.eng
            known = self.known[E]
            need = {}
            for d in n.deps:
                if d in byid:
                    dn = byid[d]
                    ev = (dn.sem, dn.val, dn.clock)
                else:
                    ev = self.done[d]
                sem, val, clock = ev
                if E == "tensor" and sem == self.engsem["tensor"]:
                    continue
                if known.get(sem, 0) >= val:
                    continue
                if need.get(sem, (0, None))[0] < val:
                    need[sem] = (val, clock)
            for sem, (val, clock) in need.items():
                if known.get(sem, 0) >= val:
                    continue
                prog[E].append(("wait", sem, val))
                self.nwaits += 1
                for s2, v2 in clock.items():
                    if known.get(s2, 0) < v2:
                        known[s2] = v2
                known[sem] = val
            if n.kind == "op":
                self.semcount[n.sem] += 1
                prog[E].append(("op", n.fn))
            else:
                self.semcount[n.sem] += 16
                prog[E].append(("dma", n.out, n.in_, n.sem))
            n.val = self.semcount[n.sem]
            n.clock = dict(known)
        for n in order:
            self.done[n.id] = (n.sem, n.val, n.clock)
            n.fn = n.out = n.in_ = None
        for e in (final_engines or self.ENGS):
            known = self.known[e]
            for sem, cnt in enumerate(self.semcount):
                if cnt > 0 and known.get(sem, 0) < cnt and not (e == "tensor" and sem == self.engsem["tensor"]):
                    prog[e].append(("wait", sem, cnt))
                    known[sem] = cnt
        sems = self.sems
        with self.nc.Block() as block:
            def runner(e):
                def run(h):
                    mysem = sems[self.engsem[e]]
                    for item in prog[e]:
                        if item[0] == "wait":
                            h.wait_ge(sems[item[1]], item[2])
                        elif item[0] == "op":
                            item[1](h).then_inc(mysem, 1)
                        else:
                            h.dma_start(out=item[1], in_=item[2]).then_inc(sems[item[3]], 16)
                return run
            block.tensor(runner("tensor"))
            block.vector(runner("vector"))
            block.scalar(runner("scalar"))
            block.gpsimd(runner("gpsimd"))
            block.sync(runner("sync"))


class Ring:
    def __init__(self, tiles, psum=False):
        self.tiles = tiles
        self.bufs = [Buf(psum=psum) for _ in tiles]
        self.i = 0

    def next(self):
        j = self.i % len(self.tiles)
        self.i += 1
        return self.tiles[j], self.bufs[j]


def build_program():
    nc = bass.Bass("TRN2", target_bir_lowering=False)
    dt = lambda name, shape, dtype, kind: nc.dram_tensor(name, shape, dtype, kind=kind).ap()
    x_d = dt("x", [NTOK, D], F32, "ExternalInput")
    vecs_d = dt("vecs", [128, NV], F32, "ExternalInput")
    g2_d = dt("g2", [1, D], F32, "ExternalInput")
    g4_d = dt("g4", [1, D], F32, "ExternalInput")
    ident_d = dt("ident", [128, 128], BF16, "ExternalInput")
    ec_d = dt("ec", [128, 4, 16], F32, "ExternalInput")
    w_in_d = dt("w_in", [D, 3 * DL], F32, "ExternalInput")
    rg_wa_d = dt("rg_wa", [2, 8, 64, 64], F32, "ExternalInput")
    rg_wx_d = dt("rg_wx", [2, 8, 64, 64], F32, "ExternalInput")
    pool_w_d = dt("pool_w", [4, 128, 128], F32, "ExternalInput")
    w_out_d = dt("w_out", [D, D], F32, "ExternalInput")
    w_gate_d = dt("w_gate", [D, DFF], F32, "ExternalInput")
    w_up_d = dt("w_up", [D, DFF], F32, "ExternalInput")
    w_down_d = dt("w_down", [DFF, D], F32, "ExternalInput")
    y_d = dt("y", [NTOK, D], F32, "ExternalOutput")
    sa_d = dt("stash_a", [NSEQ, 4, 128, SEQ], F32, "Internal")
    sb_d = dt("stash_b", [NSEQ, 4, 128, SEQ], F32, "Internal")
    sy_d = dt("stash_y", [NSEQ, 4, 128, SEQ], BF16, "Internal")

    with ExitStack() as es:
        S = Sched(nc, es)
        op, dma = S.op, S.dma

        def sb(ctx, name, shape, dtype):
            return ctx.enter_context(nc.sbuf_tensor(name, shape, dtype))

        def ps(ctx, name, shape, dtype):
            return ctx.enter_context(nc.psum_tensor(name, shape, dtype))

        V = sb(es, "V", [128, NV], F32); bV = Buf()
        CC = sb(es, "CC", [128, 44], F32); bCC = Buf()
        IDN = sb(es, "IDN", [128, 128], BF16); bIDN = Buf()
        NEGH = sb(es, "NEGH", [128, 1], F32); bNEGH = Buf()
        HL = sb(es, "HL", [128, NSEQ * (NSTG + 1), 4], F32)
        PPt = sb(es, "PPt", [128, NSEQ * (NSTG + 1), 4], F32)
        HH = sb(es, "HH", [128, NSEQ * (NSTG + 1), 4], F32)
        bHL = [[Buf() for _ in range(NSTG + 1)] for _ in range(NSEQ)]
        bHH = [[Buf() for _ in range(NSTG + 1)] for _ in range(NSEQ)]
        ST = sb(es, "ST", [128, 64], F32)
        bST = [Buf() for _ in range(16)]
        st_i = [0]

        def hidx(s, k):
            return s * (NSTG + 1) + k

        dma("sync", V[:], vecs_d, writes=[bV], sembuf=bV)
        dma("sync", IDN[:], ident_d, writes=[bIDN], sembuf=bIDN)
        op("gpsimd", lambda e: e.memset(NEGH[:], -0.5), writes=[bNEGH], n=1)

        lam = V[:, VC_LAM:VC_LAM + 8]
        t0, t1, t2 = CC[:, 32:40], CC[:, 0:8], CC[:, 8:16]
        op("vector", lambda e: e.tensor_scalar(out=t0, in0=lam, scalar1=-1.0, scalar2=None, op0=ALU.mult), reads=[bV], writes=[bCC], n=8)
        op("scalar", lambda e: e.activation(out=t1, in_=t0, func=AF.Abs), reads=[bCC], writes=[bCC], n=8)
        op("scalar", lambda e: e.activation(out=t1, in_=t1, func=AF.Exp, scale=-1.0), reads=[bCC], writes=[bCC], n=8, tab="ln")
        op("scalar", lambda e: e.activation(out=t1, in_=t1, func=AF.Ln, bias=1.0), reads=[bCC], writes=[bCC], n=8, tab="ln")
        op("vector", lambda e: e.tensor_scalar(out=t0, in0=t0, scalar1=0.0, scalar2=None, op0=ALU.max), reads=[bCC], writes=[bCC], n=8)
        op("vector", lambda e: e.tensor_tensor(out=t0, in0=t0, in1=t1, op=ALU.add), reads=[bCC], writes=[bCC], n=8)
        op("vector", lambda e: e.tensor_scalar(out=t1, in0=t0, scalar1=-8.0, scalar2=None, op0=ALU.mult), reads=[bCC], writes=[bCC], n=8)
        op("vector", lambda e: e.tensor_scalar(out=t2, in0=t0, scalar1=-4.0, scalar2=None, op0=ALU.mult), reads=[bCC], writes=[bCC], n=8)
        op("vector", lambda e: e.tensor_scalar(out=CC[:, 16:32], in0=V[:, VC_BA:VC_BA + 16], scalar1=0.5, scalar2=None, op0=ALU.mult),
           reads=[bV, bCC], writes=[bCC], n=16)
        op("vector", lambda e: e.tensor_tensor(out=CC[:, 40:44], in0=V[:, VC_PB:VC_PB + 4], in1=V[:, VC_PS:VC_PS + 4], op=ALU.mult),
           reads=[bV, bCC], writes=[bCC], n=4)

        def stat_slot():
            j = st_i[0] % 16
            st_i[0] += 1
            return ST[:, 4 * j:4 * j + 4], bST[j]

        def rstd_from(c, b, cols, width):
            if cols == 2:
                op("gpsimd", lambda e: e.tensor_tensor(out=c[:, 0:1], in0=c[:, 0:1], in1=c[:, 1:2], op=ALU.add), reads=[b], writes=[b], n=1, two=True)
            op("gpsimd", lambda e: e.tensor_scalar(out=c[:, 2:3], in0=c[:, 0:1], scalar1=1.0 / width, scalar2=EPS, op0=ALU.mult, op1=ALU.add),
               reads=[b], writes=[b], n=1)
            op("gpsimd", lambda e: e.tensor_tensor(out=c[:, 3:4], in0=c[:, 2:3], in1=NEGH[:], op=ALU.pow), reads=[b, bNEGH], writes=[b], n=1, two=True)
            return c[:, 3:4]

        with ExitStack() as p1:
            WIN = sb(p1, "WIN", [128, 8, 3 * DL], BF16); bWIN = Buf()
            BD = sb(p1, "BD", [128, 16, 128], BF16); bBD1 = Buf()
            PWt = sb(p1, "PWt", [128, 4, 128], BF16); bPWt = Buf()
            IDF = sb(p1, "IDF", [128, 128], F32); bIDF = Buf()
            DG = sb(p1, "DG", [128, 16, 128], F32); bDG = Buf()
            EC = sb(p1, "EC", [128, 4, 16], F32); bEC = Buf()
            ZER = sb(p1, "ZER", [128, T1], F32); bZER = Buf()
            HCt = sb(p1, "HCt", [128, 4], F32); bHC = [Buf() for _ in range(4)]
            XB = Ring([sb(p1, "XB%d" % i, [128, D], F32) for i in range(4)])
            XN = Ring([sb(p1, "XN%d" % i, [128, D], BF16) for i in range(2)])
            HT = [sb(p1, "HT%d" % i, [128, 8, T1], BF16) for i in range(2)]
            bHT = [[Buf() for _ in range(4)] for _ in range(2)]
            RU = [sb(p1, "RU%d" % i, [128, 4, RW], F32) for i in range(2)]
            RG = [sb(p1, "RG%d" % i, [128, 4, RW], F32) for i in range(2)]
            RP = [sb(p1, "RP%d" % i, [128, 4, RW], F32) for i in range(2)]
            bRU = [[Buf() for _ in range(4)] for _ in range(2)]
            bRG = [[Buf() for _ in range(4)] for _ in range(2)]
            bRP = [[Buf() for _ in range(4)] for _ in range(2)]
            bHALO = [[Buf() for _ in range(2)] for _ in range(3)]
            FU = sb(p1, "FU", [128, 4, 32], F32); FG = sb(p1, "FG", [128, 4, 32], F32); FP = sb(p1, "FP", [128, 4, 32], F32)
            bFU = [Buf() for _ in range(4)]; bFG = [Buf() for _ in range(4)]; bFP = [Buf() for _ in range(4)]
            UU = Ring([sb(p1, "UU%d" % i, [128, T1], F32) for i in range(3)])
            UB = Ring([sb(p1, "UB%d" % i, [128, T1], BF16) for i in range(2)])
            THR = Ring([sb(p1, "THR%d" % i, [128, T1], F32) for i in range(2)])
            THI = [Ring([sb(p1, "THI%d_%d" % (d, i), [128, T1], F32) for i in range(2)]) for d in range(2)]
            AAr = [Ring([sb(p1, "AA%d_%d" % (d, i), [128, T1], F32) for i in range(2)]) for d in range(2)]
            A2r = [Ring([sb(p1, "A2%d_%d" % (d, i), [128, T1], F32) for i in range(2)]) for d in range(2)]
            HFr = Ring([sb(p1, "HF%d" % i, [128, T1], F32) for i in range(2)])
            HLr = Ring([sb(p1, "HLr%d" % i, [128, T1], F32) for i in range(2)])
            PCr = Ring([sb(p1, "PC%d" % i, [128, T1], F32) for i in range(2)])
            AOr = Ring([sb(p1, "AO%d" % i, [128, T1], F32) for i in range(2)])
            SGB = {}
            PS1 = sb(p1, "PS1", [128, RW], F32); bPS1 = Buf()
            PS2 = sb(p1, "PS2", [128, RW], F32); bPS2 = Buf()
            PGr = Ring([sb(p1, "PG%d" % i, [128, T1], BF16) for i in range(2)])
            YPr = Ring([sb(p1, "YP%d" % i, [128, T1], BF16) for i in range(2)])
            TPp = Ring([ps(p1, "TP%d" % i, [128, 8, 128], BF16) for i in range(2)], psum=True)
            WPp = Ring([ps(p1, "WP%d" % i, [128, T1], F32) for i in range(2)], psum=True)
            CVp = Ring([ps(p1, "CV%d" % i, [128, T1], F32) for i in range(1)], psum=True)
            GPp = Ring([ps(p1, "GP%d" % i, [128, T1], F32) for i in range(2)], psum=True)
            PMp = Ring([ps(p1, "PM%d" % i, [128, T1], F32) for i in range(1)], psum=True)

            dma("gpsimd", WIN[:], w_in_d.rearrange("(kc p) f -> p kc f", p=128), writes=[bWIN], sembuf=bWIN, nbytes=6 << 20)
            op("gpsimd", lambda e: e.memset(BD[:], 0.0), writes=[bBD1], n=2048)
            for m, wsrc in enumerate((rg_wa_d, rg_wx_d)):
                for d in range(2):
                    for h in range(8):
                        c, hf_ = h // 2, h % 2
                        idx = m * 8 + d * 4 + c
                        dma("gpsimd", BD[hf_ * 64:(hf_ + 1) * 64, idx, hf_ * 64:(hf_ + 1) * 64], wsrc[d, h],
                            writes=[bBD1], sembuf=bBD1, group=True, nbytes=16384)
            dma("gpsimd", PWt[:], pool_w_d.rearrange("g i j -> i g j"), writes=[bPWt], sembuf=bPWt, nbytes=1 << 18)
            dma("sync", EC[:], ec_d, writes=[bEC], sembuf=bEC)
            op("vector", lambda e: e.tensor_copy(out=IDF[:], in_=IDN[:]), reads=[bIDN], writes=[bIDF], n=128)
            for k in range(4):
                for c in range(4):
                    op("vector", lambda e, k=k, c=c: e.tensor_scalar(out=DG[:, 4 * k + c, :], in0=IDF[:], scalar1=V[:, VC_CW + 4 * k + c:VC_CW + 4 * k + c + 1],
                                                                     scalar2=None, op0=ALU.mult),
                       reads=[bIDF, bV, bDG], writes=[bDG], n=128)
            op("gpsimd", lambda e: e.memset(ZER[:], 0.0), writes=[bZER])
            op("gpsimd", lambda e: e.memset(FU[:], 0.0), writes=bFU, n=128)
            op("gpsimd", lambda e: e.memset(FG[:], 0.0), writes=bFG, n=128)
            op("gpsimd", lambda e: e.memset(FP[:], 0.0), writes=bFP, n=128)
            op("gpsimd", lambda e: e.memset(HH[:], 0.0), writes=[b for row in bHH for b in row], n=120)

            def x_block(i, tb):
                slot = i % 2
                r0 = i * T1 + tb * 128
                xb, bxb = XB.next()
                dma("sync", xb[:], x_d[r0:r0 + 128, :], writes=[bxb], sembuf=bxb, nbytes=1 << 19)
                xn, bxn = XN.next()
                c, bst = stat_slot()
                op("scalar", lambda e: e.activation(out=xn[:], in_=xb[:], func=AF.Square, accum_out=c[:, 0:1]), reads=[bxb], writes=[bst, bxn], n=D)
                rstd = rstd_from(c, bst, 1, D)
                op("vector", lambda e: e.tensor_scalar(out=xn[:], in0=xb[:], scalar1=rstd, scalar2=None, op0=ALU.mult), reads=[bxb, bst], writes=[bxn], n=D)
                tp, btp = TPp.next()
                for cc in range(8):
                    op("tensor", lambda e, cc=cc: e.transpose(out=tp[:, cc, :], in_=xn[:, cc * 128:(cc + 1) * 128], identity=IDN[:]),
                       reads=[bxn, bIDN], writes=[btp], n=128)
                op("vector", lambda e: e.tensor_tensor(out=HT[slot][:, :, tb * 128:(tb + 1) * 128], in0=tp[:],
                                                       in1=V[:, VC_G1:VC_G1 + 8].unsqueeze(2).to_broadcast([128, 8, 128]), op=ALU.mult),
                   reads=[btp, bV], writes=[bHT[slot][tb]], n=D)

            def w_chunk(i, oc):
                slot = i % 2
                k = i % NT1
                kind, c = oc // 4, oc % 4
                R, bR = ((RU, bRU), (RG, bRG), (RP, bRP))[kind]
                if c == 0:
                    if k == 0:
                        op("gpsimd", lambda e: e.memset(R[slot][:, :, 0:HALO], 0.0), writes=bR[slot], n=64)
                    else:
                        dma("sync", R[slot][:, :, 0:HALO], R[1 - slot][:, :, T1:RW], reads=bR[1 - slot], writes=bR[slot],
                            sembuf=bHALO[kind][slot], nbytes=32768)
                wp, bwp = WPp.next()
                for kc in range(8):
                    op("tensor", lambda e, kc=kc: e.matmul(out=wp[:], lhsT=WIN[:, kc, oc * 128:(oc + 1) * 128], rhs=HT[slot][:, kc, :],
                                                           start=(kc == 0), stop=(kc == 7)),
                       reads=[bWIN] + bHT[slot], writes=[bwp], n=T1)
                op("scalar", lambda e: e.activation(out=R[slot][:, c, HALO:RW], in_=wp[:], func=AF.Copy), reads=[bwp], writes=[bR[slot][c]], n=T1)

            def w_gelu(i):
                slot = i % 2
                op("scalar", lambda e: e.activation(out=RG[slot][:, :, HALO:RW], in_=RG[slot][:, :, HALO:RW], func=AF.Gelu_apprx_tanh),
                   reads=bRG[slot], writes=bRG[slot], n=4 * T1, tab="gelu")

            def lru_front(c, Ru, bRu, c0, n):
                u, bu = UU.next()
                cv, bcv = CVp.next()
                for k in range(4):
                    op("tensor", lambda e, k=k: e.matmul(out=cv[:, :n], lhsT=DG[:, 4 * k + c, :], rhs=Ru[:, c, c0 - 2 + k:c0 - 2 + k + n],
                                                         start=(k == 0), stop=(k == 3)),
                       reads=[bDG, bRu[c]], writes=[bcv], n=4 * n)
                op("scalar", lambda e: e.activation(out=u[:, :n], in_=cv[:, :n], func=AF.Identity, bias=V[:, VC_CB + c:VC_CB + c + 1], scale=1.0),
                   reads=[bcv, bV], writes=[bu], n=n)
                ub, bub = UB.next()
                op("vector", lambda e: e.tensor_scalar(out=ub[:, :n], in0=cv[:, :n], scalar1=V[:, VC_CB + c:VC_CB + c + 1], scalar2=None, op0=ALU.add),
                   reads=[bcv, bV], writes=[bub], n=n)
                res = {"u": (u, bu)}

                def per_dir(d):
                    j = d * 4 + c
                    rp, brp = GPp.next()
                    op("tensor", lambda e: e.matmul(out=rp[:, :n], lhsT=BD[:, j, :], rhs=ub[:, :n], start=True, stop=True),
                       reads=[bBD1, bub], writes=[brp], n=n)
                    thr, bthr = THR.next()
                    op("scalar", lambda e: e.activation(out=thr[:, :n], in_=rp[:, :n], func=AF.Tanh, scale=0.5, bias=CC[:, 16 + j:17 + j]),
                       reads=[brp, bCC], writes=[bthr], n=n, tab="exp")
                    ip, bip = GPp.next()
                    op("tensor", lambda e: e.matmul(out=ip[:, :n], lhsT=BD[:, 8 + j, :], rhs=ub[:, :n], start=True, stop=True),
                       reads=[bBD1, bub], writes=[bip], n=n)
                    thi, bthi = THI[d].next()
                    op("scalar", lambda e: e.activation(out=thi[:, :n], in_=ip[:, :n], func=AF.Tanh, scale=0.5, bias=CC[:, 24 + j:25 + j]),
                       reads=[bip, bCC], writes=[bthi], n=n, tab="exp")
                    aa, baa = AAr[d].next()
                    op("scalar", lambda e: e.activation(out=aa[:, :n], in_=thr[:, :n], func=AF.Exp, scale=CC[:, 8 + j:9 + j], bias=CC[:, 8 + j:9 + j]),
                       reads=[bthr, bCC], writes=[baa], n=n, tab="exp")
                    a2, ba2 = A2r[d].next()
                    if d == 0:
                        op("scalar", lambda e: e.activation(out=a2[:, :n], in_=thr[:, :n], func=AF.Exp, scale=CC[:, j:j + 1], bias=CC[:, j:j + 1]),
                           reads=[bthr, bCC], writes=[ba2], n=n, tab="exp")
                    else:
                        op("gpsimd", lambda e: e.tensor_tensor(out=a2[:, :n], in0=aa[:, :n], in1=aa[:, :n], op=ALU.mult),
                           reads=[baa], writes=[ba2], n=n, two=True)
                    op("vector", lambda e: e.scalar_tensor_tensor(out=thi[:, :n], in0=thi[:, :n], scalar=1.0, in1=u[:, :n], op0=ALU.add, op1=ALU.mult),
                       reads=[bthi, bu], writes=[bthi], n=1.2 * n)
                    res[d] = ((thi, bthi), (aa, baa), (a2, ba2))
                per_dir(0)
                per_dir(1)
                return res

            def lru_back(c, res, Rg, bRg, c0, n, s, kst, tok0):
                def sq_dir(d):
                    (thi, bthi), (aa, baa), (a2, ba2) = res[d]
                    op("scalar", lambda e: e.activation(out=a2[:, :n], in_=a2[:, :n], func=AF.Sqrt, scale=-0.25, bias=0.25),
                       reads=[ba2], writes=[ba2], n=n, tab="sqrt")
                    op("gpsimd", lambda e: e.tensor_tensor(out=thi[:, :n], in0=thi[:, :n], in1=a2[:, :n], op=ALU.mult),
                       reads=[bthi, ba2], writes=[bthi], n=n, two=True)
                sq_dir(0)
                sq_dir(1)
                (bt0, bbt0), (a0, ba0), _ = res[0]
                (bt1, bbt1), (a1, ba1), _ = res[1]
                hf, bhf = HFr.next()
                op("vector", lambda e: e.tensor_tensor_scan(out=hf[:, :n], data0=a0[:, :n], data1=bt0[:, :n], initial=HCt[:, c:c + 1],
                                                            op0=ALU.mult, op1=ALU.add),
                   reads=[ba0, bbt0, bHC[c]], writes=[bhf], n=2 * n)
                dma("sync", HCt[:, c:c + 1], hf[:, n - 1:n], reads=[bhf], writes=[bHC[c]], sembuf=bHC[c], nbytes=512)
                hl, bhl = HLr.next()
                op("vector", lambda e: e.tensor_tensor_scan(out=hl[:, 0:n][:, ::-1], data0=a1[:, 0:n][:, ::-1], data1=bt1[:, 0:n][:, ::-1],
                                                            initial=0.0, op0=ALU.mult, op1=ALU.add),
                   reads=[ba1, bbt1], writes=[bhl], n=2 * n)
                pc, bpc = PCr.next()
                op("vector", lambda e: e.tensor_tensor_scan(out=pc[:, 0:n][:, ::-1], data0=a1[:, 0:n][:, ::-1], data1=ZER[:, :n],
                                                            initial=1.0, op0=ALU.mult, op1=ALU.add),
                   reads=[ba1, bZER], writes=[bpc], n=2 * n)
                hi = hidx(s, kst)
                ao, bao = AOr.next()
                op("gpsimd", lambda e: e.tensor_tensor(out=ao[:, :n], in0=hf[:, :n], in1=hl[:, :n], op=ALU.add), reads=[bhf, bhl], writes=[bao], n=n, two=True)
                dma("sync", sa_d[s, c, :, tok0:tok0 + n], ao[:, :n], reads=[bao], sembuf=bao, nbytes=n * 512)
                dma("sync", sb_d[s, c, :, tok0:tok0 + n], pc[:, :n], reads=[bpc], sembuf=bpc, nbytes=n * 512)
                dma("sync", HL[:, hi, c:c + 1], hl[:, 0:1], reads=[bhl], writes=[bHL[s][kst]], sembuf=SGB.setdefault(("hl", id(hl)), Buf()), group=True, nbytes=512)
                dma("sync", PPt[:, hi, c:c + 1], pc[:, 0:1], reads=[bpc], writes=[bHL[s][kst]], sembuf=SGB.setdefault(("pp", id(pc)), Buf()), group=True, nbytes=512)
                sgb = SGB.setdefault(id(Rg), [Buf() for _ in range(4)])[c]
                ta = tok0
                while ta < tok0 + n:
                    k2 = ta // T2
                    tb_ = min(tok0 + n, (k2 + 1) * T2)
                    row0 = (s * NT2 + k2) * T2
                    dma("sync", y_d[row0:row0 + 128, c * T2 + (ta - k2 * T2):c * T2 + (tb_ - k2 * T2)],
                        Rg[:, c, c0 + (ta - tok0):c0 + (tb_ - tok0)], reads=[bRg[c]], sembuf=sgb, group=False, nbytes=(tb_ - ta) * 512)
                    ta = tb_

            def pool_group(g, Rp, bRp, c0, n, s, tok0, first, last):
                w = WINS[g]
                half = w // 2
                src = Rp[:, g, :]
                width = c0 + n + 8
                cur, bcur = src, bRp[g]
                tmps = [(PS1, bPS1), (PS2, bPS2)]
                step = 1
                ti = 0
                while step < half:
                    (dst, bdst) = tmps[ti % 2]
                    ti += 1
                    L = width - (2 * step - 1)
                    op("gpsimd", lambda e, cur=cur, dst=dst, step=step, L=L: e.tensor_tensor(out=dst[:, 0:L], in0=cur[:, 0:L], in1=cur[:, step:step + L], op=ALU.add),
                       reads=[bcur], writes=[bdst], n=L, two=True)
                    cur, bcur = dst, bdst
                    step *= 2
                (dst, bdst) = tmps[ti % 2]
                op("gpsimd", lambda e: e.tensor_tensor(out=dst[:, 0:n], in0=cur[:, c0 - half:c0 - half + n], in1=cur[:, c0:c0 + n], op=ALU.add),
                   reads=[bcur], writes=[bdst], n=n, two=True)
                pg, bpg = PGr.next()
                op("vector", lambda e: e.scalar_tensor_tensor(out=pg[:, :n], in0=dst[:, 0:n], scalar=1.0 / w, in1=src[:, c0:c0 + n],
                                                              op0=ALU.mult, op1=ALU.subtract),
                   reads=[bdst, bRp[g]], writes=[bpg], n=1.5 * n)
                if first or last:
                    e0, off = (0, 0) if first else (8, n - 8)
                    op("vector", lambda e: e.tensor_tensor(out=dst[:, off:off + 8], in0=dst[:, off:off + 8], in1=EC[:, g, e0:e0 + 8], op=ALU.mult),
                       reads=[bdst, bEC], writes=[bdst], n=8)
                    op("vector", lambda e: e.tensor_tensor(out=pg[:, off:off + 8], in0=dst[:, off:off + 8], in1=src[:, c0 + off:c0 + off + 8], op=ALU.subtract),
                       reads=[bdst, bRp[g]], writes=[bpg], n=8)
                pm, bpm = PMp.next()
                op("tensor", lambda e: e.matmul(out=pm[:, :n], lhsT=PWt[:, g, :], rhs=pg[:, :n], start=True, stop=True), reads=[bPWt, bpg], writes=[bpm], n=n)
                yp, byp = YPr.next()
                op("scalar", lambda e: e.activation(out=yp[:, :n], in_=pm[:, :n], func=AF.Identity, scale=V[:, VC_PS + g:VC_PS + g + 1],
                                                    bias=CC[:, 40 + g:41 + g]),
                   reads=[bpm, bV, bCC], writes=[byp], n=n)
                dma("sync", sy_d[s, g, :, tok0:tok0 + n], yp[:, :n], reads=[byp], sembuf=byp, nbytes=n * 256)

            def mix_stage(s, kst, Ru, bRu, Rg, bRg, Rp, bRp, c0, n, tok0):
                first = (kst == 0)
                last = (kst == NT1)
                fr = {}
                fr[0] = lru_front(0, Ru, bRu, c0, n)
                fr[1] = lru_front(1, Ru, bRu, c0, n)
                for c in range(4):
                    lru_back(c, fr[c], Rg, bRg, c0, n, s, kst, tok0)
                    if c + 2 < 4:
                        fr[c + 2] = lru_front(c + 2, Ru, bRu, c0, n)
                    pool_group(c, Rp, bRp, c0, n, s, tok0, first, last)

            NTILES = NSEQ * NT1
            for tb in range(4):
                x_block(0, tb)
            for oc in range(12):
                w_chunk(0, oc)
            w_gelu(0)
            for tb in range(4):
                x_block(1, tb)
            for i in range(NTILES):
                s, k = i // NT1, i % NT1
                slot = i % 2
                if k == 0:
                    for c in range(4):
                        op("gpsimd", lambda e, c=c: e.memset(HCt[:, c:c + 1], 0.0), writes=[bHC[c]], n=1)
                if k == 0:
                    mix_stage(s, 0, RU[slot], bRU[slot], RG[slot], bRG[slot], RP[slot], bRP[slot], HALO, T1 - 8, 0)
                else:
                    mix_stage(s, k, RU[slot], bRU[slot], RG[slot], bRG[slot], RP[slot], bRP[slot], 8, T1, k * T1 - 8)
                if i + 1 < NTILES:
                    for oc in range(12):
                        w_chunk(i + 1, oc)
                    w_gelu(i + 1)
                if i + 2 < NTILES:
                    for tb in range(4):
                        x_block(i + 2, tb)
                if k == NT1 - 1:
                    for c in range(4):
                        op("gpsimd", lambda e, c=c, slot=slot: e.tensor_copy(out=FU[:, c, 0:HALO], in_=RU[slot][:, c, T1:RW]), reads=[bRU[slot][c]], writes=[bFU[c]], n=16)
                        op("gpsimd", lambda e, c=c, slot=slot: e.tensor_copy(out=FG[:, c, 0:HALO], in_=RG[slot][:, c, T1:RW]), reads=[bRG[slot][c]], writes=[bFG[c]], n=16)
                        op("gpsimd", lambda e, c=c, slot=slot: e.tensor_copy(out=FP[:, c, 0:HALO], in_=RP[slot][:, c, T1:RW]), reads=[bRP[slot][c]], writes=[bFP[c]], n=16)
                    mix_stage(s, NT1, FU, bFU, FG, bFG, FP, bFP, 8, 8, SEQ - 8)
                    for kk in range(NT1, -1, -1):
                        hi, hn = hidx(s, kk), hidx(s, kk + 1)
                        op("vector", lambda e, hi=hi, hn=hn: e.tensor_tensor(out=HH[:, hi, :], in0=PPt[:, hi, :], in1=HH[:, hn, :], op=ALU.mult),
                           reads=[bHL[s][kk], bHH[s][kk + 1]], writes=[bHH[s][kk]], n=4)
                        op("vector", lambda e, hi=hi: e.tensor_tensor(out=HH[:, hi, :], in0=HH[:, hi, :], in1=HL[:, hi, :], op=ALU.add),
                           reads=[bHL[s][kk], bHH[s][kk]], writes=[bHH[s][kk]], n=4)
            S.flush()

        with ExitStack() as p2:
            WOUT = sb(p2, "WOUT", [128, 8, D], BF16); bWOUT = Buf()
            dma("gpsimd", WOUT[:], w_out_d.rearrange("(kc p) f -> p kc f", p=128), writes=[bWOUT], sembuf=bWOUT, nbytes=4 << 20)
            WG = sb(p2, "WG", [128, 8, DFF], BF16); bWGp = [Buf() for _ in range(4)]
            WU = sb(p2, "WU", [128, 8, DFF], BF16); bWUp = [Buf() for _ in range(4)]
            WD = sb(p2, "WD", [128, NFF, D], BF16); bWDp = [Buf() for _ in range(4)]
            JP = (0, 4, 10, 16, NFF)

            def piece_of(jf):
                return max(q for q in range(4) if JP[q] <= jf)
            G2 = sb(p2, "G2", [128, D], F32); bG2 = Buf()
            G4 = sb(p2, "G4", [128, D], F32); bG4 = Buf()
            X2 = [sb(p2, "X2_%d" % i, [128, 2, D], F32) for i in range(2)]
            bX2 = [[Buf() for _ in range(2)] for _ in range(2)]
            At = sb(p2, "At", [128, 4, T2], F32); bAt = Buf()
            Bt = sb(p2, "Bt", [128, 4, T2], F32); bBt = Buf()
            Gt = sb(p2, "Gt", [128, 4, T2], F32); bGt = Buf()
            YT = sb(p2, "YT", [128, 8, T2], BF16); bYTl = Buf(); bYTp = Buf()
            H2T = sb(p2, "H2T", [128, 8, T2], BF16); bH2T = [Buf() for _ in range(2)]
            XN2 = Ring([sb(p2, "XN2_%d" % i, [128, D], BF16) for i in range(2)])
            TMP = Ring([sb(p2, "TMP%d" % i, [128, D], F32) for i in range(1)])
            SG = Ring([sb(p2, "SG%d" % i, [128, T2], F32) for i in range(2)])
            ACr = Ring([sb(p2, "AC%d" % i, [128, T2], BF16) for i in range(5)])
            MF = [ps(p2, "MF%d" % i, [128, D], F32) for i in range(2)]; bMF = [Buf(psum=True) for _ in range(2)]
            GUp = Ring([ps(p2, "GU%d" % i, [128, 2, T2], F32) for i in range(3)], psum=True)
            AUX = ps(p2, "AUX", [128, 512], F32); bAUX = Buf(psum=True)
            AUXT = AUX[:].bitcast(BF16).rearrange("p (c t) -> p c t", c=8)

            dma("sync", G2[:], g2_d.partition_broadcast(128), writes=[bG2], sembuf=bG2, nbytes=1 << 19)
            dma("sync", G4[:], g4_d.partition_broadcast(128), writes=[bG4], sembuf=bG4, nbytes=1 << 19)
            wg_v = w_gate_d.rearrange("(kc p) f -> p kc f", p=128)
            wu_v = w_up_d.rearrange("(kc p) f -> p kc f", p=128)
            wd_v = w_down_d.rearrange("(j p) f -> p j f", p=128)
            for q in range(4):
                f0, f1 = JP[q] * 128, JP[q + 1] * 128
                nb = (f1 - f0) * 4096
                dma("gpsimd", WG[:, :, f0:f1], wg_v[:, :, f0:f1], reads=[bWOUT], writes=[bWGp[q]], sembuf=bWGp[q], nbytes=nb)
                dma("gpsimd", WU[:, :, f0:f1], wu_v[:, :, f0:f1], reads=[bWOUT], writes=[bWUp[q]], sembuf=bWUp[q], nbytes=nb)
                dma("gpsimd", WD[:, JP[q]:JP[q + 1], :], wd_v[:, JP[q]:JP[q + 1], :], reads=[bWOUT], writes=[bWDp[q]], sembuf=bWDp[q], nbytes=nb)

            NT = NSEQ * NT2

            def tile_info(i2):
                s, j = i2 // NT2, i2 % NT2
                return s, j, i2 % 2, j * T2, s * SEQ + j * T2

            def front_pieces(i2):
                s, j, slot, t0_, r0 = tile_info(i2)
                pieces = []

                def loads():
                    dma("sync", X2[slot][:], x_d[r0:r0 + T2, :].rearrange("(b p) f -> p b f", p=128), writes=bX2[slot], sembuf=bX2[slot][0], nbytes=1 << 20)
                    dma("sync", At[:], sa_d[s, :, :, t0_:t0_ + T2].rearrange("c p t -> p c t"), writes=[bAt], sembuf=bAt, nbytes=1 << 19)
                    dma("sync", Bt[:], sb_d[s, :, :, t0_:t0_ + T2].rearrange("c p t -> p c t"), writes=[bBt], sembuf=bBt, nbytes=1 << 19)
                    dma("sync", Gt[:], y_d[r0:r0 + 128, :].rearrange("p (c t) -> p c t", c=4), writes=[bGt], sembuf=bGt, nbytes=1 << 19)
                    dma("sync", YT[:, 4:8, :], sy_d[s, :, :, t0_:t0_ + T2].rearrange("c p t -> p c t"), writes=[bYTp], sembuf=bYTp, nbytes=1 << 18)
                pieces.append(loads)

                def ylru():
                    kst = j // 2
                    for c in range(4):
                        segs = [(0, T2, kst + 1)] if j % 2 == 0 else [(0, T2 - 8, kst + 1), (T2 - 8, T2, kst + 2)]
                        for (a_, b_, kh) in segs:
                            hi = hidx(s, kh)
                            op("vector", lambda e, c=c, a_=a_, b_=b_, hi=hi: e.scalar_tensor_tensor(
                                out=At[:, c, a_:b_], in0=Bt[:, c, a_:b_], scalar=HH[:, hi, c:c + 1], in1=At[:, c, a_:b_], op0=ALU.mult, op1=ALU.add),
                               reads=[bAt, bBt, bHH[s][kh]], writes=[bAt], n=1.2 * (b_ - a_))
                    op("vector", lambda e: e.tensor_tensor(out=YT[:, 0:4, :], in0=At[:], in1=Gt[:], op=ALU.mult), reads=[bAt, bGt], writes=[bYTl], n=4 * T2)
                pieces.append(ylru)

                state = {}

                def wout_half(tb, hf_):
                    def f():
                        if hf_ == 0:
                            state[tb] = (stat_slot(), TMP.next(), XN2.next())
                        (c, bst), (tmp, btmp), (xn, bxn) = state[tb]
                        for kc in range(8):
                            op("tensor", lambda e, kc=kc: e.matmul(out=AUX[:], lhsT=YT[:, kc, tb * 128:(tb + 1) * 128],
                                                                   rhs=WOUT[:, kc, hf_ * 512:(hf_ + 1) * 512], start=(kc == 0), stop=(kc == 7)),
                               reads=[bYTl, bYTp, bWOUT], writes=[bAUX], n=512)
                        op("scalar", lambda e: e.activation(out=xn[:, hf_ * 512:(hf_ + 1) * 512], in_=AUX[:], func=AF.Square, accum_out=c[:, hf_:hf_ + 1]),
                           reads=[bAUX], writes=[bst, bxn], n=512)
                        op("vector", lambda e: e.tensor_tensor(out=tmp[:, hf_ * 512:(hf_ + 1) * 512], in0=AUX[:], in1=G2[:, hf_ * 512:(hf_ + 1) * 512], op=ALU.mult),
                           reads=[bAUX, bG2], writes=[btmp], n=512)
                    return f

                def norm_block(tb):
                    def f():
                        (c, bst), (tmp, btmp), (xn, bxn) = state[tb]
                        xrow, bx = X2[slot][:, tb, :], bX2[slot][tb]
                        rstd = rstd_from(c, bst, 2, D)
                        op("vector", lambda e: e.scalar_tensor_tensor(out=xrow, in0=tmp[:], scalar=rstd, in1=xrow, op0=ALU.mult, op1=ALU.add),
                           reads=[btmp, bst, bx], writes=[bx], n=1.2 * D)
                        c2, bst2 = stat_slot()
                        op("scalar", lambda e: e.activation(out=xn[:], in_=xrow, func=AF.Square, accum_out=c2[:, 0:1]), reads=[bx], writes=[bst2, bxn], n=D)
                        rstd2 = rstd_from(c2, bst2, 1, D)
                        op("scalar", lambda e: e.activation(out=xn[:], in_=xrow, func=AF.Copy, scale=rstd2), reads=[bx, bst2], writes=[bxn], n=D)
                    return f

                for tb in range(2):
                    pieces.append(wout_half(tb, 0))
                    pieces.append(wout_half(tb, 1))
                    pieces.append(norm_block(tb))

                def transposes():
                    for tb in range(2):
                        (c, bst), (tmp, btmp), (xn, bxn) = state[tb]
                        if tb == 0:
                            tpv, btpv = AUXT, bAUX
                        else:
                            gu, btpv = GUp.next()
                            tpv = gu[:].rearrange("p a t -> p (a t)").bitcast(BF16).rearrange("p (c t) -> p c t", c=8)
                        for cc in range(8):
                            op("tensor", lambda e, cc=cc, xn=xn, tpv=tpv: e.transpose(out=tpv[:, cc, :], in_=xn[:, cc * 128:(cc + 1) * 128], identity=IDN[:]),
                               reads=[bxn, bIDN], writes=[btpv], n=128)
                        op("vector", lambda e, tb=tb, tpv=tpv: e.tensor_tensor(out=H2T[:, :, tb * 128:(tb + 1) * 128], in0=tpv,
                                                                               in1=V[:, VC_G3:VC_G3 + 8].unsqueeze(2).to_broadcast([128, 8, 128]), op=ALU.mult),
                           reads=[btpv, bV], writes=[bH2T[tb]], n=D)
                return pieces, transposes

            def down(i2, jf, ac, bac):
                for tb in range(2):
                    for hf_ in range(2):
                        op("tensor", lambda e, tb=tb, hf_=hf_: e.matmul(
                            out=MF[tb][:, hf_ * 512:(hf_ + 1) * 512], lhsT=ac[:, tb * 128:(tb + 1) * 128],
                            rhs=WD[:, jf, hf_ * 512:(hf_ + 1) * 512], start=(jf == 0), stop=(jf == NFF - 1)),
                           reads=[bac, bWDp[piece_of(jf)]], writes=[bMF[tb]], n=512)

            def final_block(i2, tb):
                s, j, slot, t0_, r0 = tile_info(i2)
                xrow, bx = X2[slot][:, tb, :], bX2[slot][tb]
                c, bst = stat_slot()
                tmp, btmp = TMP.next()
                op("scalar", lambda e: e.activation(out=tmp[:].bitcast(BF16)[:, 0:D], in_=MF[tb][:], func=AF.Square, accum_out=c[:, 0:1]),
                   reads=[bMF[tb]], writes=[bst, btmp], n=D)
                op("vector", lambda e: e.tensor_tensor(out=tmp[:], in0=MF[tb][:], in1=G4[:], op=ALU.mult), reads=[bMF[tb], bG4], writes=[btmp], n=D)
                rstd = rstd_from(c, bst, 1, D)
                op("vector", lambda e: e.scalar_tensor_tensor(out=xrow, in0=tmp[:], scalar=rstd, in1=xrow, op0=ALU.mult, op1=ALU.add),
                   reads=[btmp, bst, bx], writes=[bx], n=1.2 * D)

            pieces, transposes = front_pieces(0)
            for p in pieces:
                p()
            transposes()
            for i2 in range(NT):
                s, j, slot, t0_, r0 = tile_info(i2)
                if i2 + 1 < NT:
                    nxt_pieces, nxt_transposes = front_pieces(i2 + 1)
                else:
                    nxt_pieces, nxt_transposes = [], None
                at = {0: 0, 1: 1, 3: 2, 5: 3, 7: 4, 10: 5, 12: 6, 14: 7}
                prevq = []
                for jf in range(NFF):
                    gu, bgu = GUp.next()
                    for kc in range(8):
                        op("tensor", lambda e, kc=kc, gu=gu, jf=jf: e.matmul(out=gu[:, 0, :], lhsT=WG[:, kc, jf * 128:(jf + 1) * 128], rhs=H2T[:, kc, :],
                                                                            start=(kc == 0), stop=(kc == 7)),
                           reads=[bWGp[piece_of(jf)]] + bH2T, writes=[bgu], n=T2)
                    for kc in range(8):
                        op("tensor", lambda e, kc=kc, gu=gu, jf=jf: e.matmul(out=gu[:, 1, :], lhsT=WU[:, kc, jf * 128:(jf + 1) * 128], rhs=H2T[:, kc, :],
                                                                            start=(kc == 0), stop=(kc == 7)),
                           reads=[bWUp[piece_of(jf)]] + bH2T, writes=[bgu], n=T2)
                    if len(prevq) >= 3:
                        down(i2, *prevq.pop(0))
                    if jf in at and at[jf] < len(nxt_pieces):
                        nxt_pieces[at[jf]]()
                    sg, bsg = SG.next()
                    op("scalar", lambda e, sg=sg, gu=gu: e.activation(out=sg[:], in_=gu[:, 0, :], func=AF.Silu), reads=[bgu], writes=[bsg], n=T2, tab="silu")
                    ac, bac = ACr.next()
                    op("vector", lambda e, sg=sg, gu=gu, ac=ac: e.tensor_tensor(out=ac[:], in0=sg[:], in1=gu[:, 1, :], op=ALU.mult), reads=[bsg, bgu], writes=[bac], n=T2)
                    prevq.append((jf, ac, bac))
                down(i2, *prevq.pop(0))
                if nxt_transposes is not None:
                    nxt_transposes()
                down(i2, *prevq.pop(0))
                down(i2, *prevq.pop(0))
                for tb in range(2):
                    final_block(i2, tb)
                dma("sync", y_d[r0:r0 + T2, :].rearrange("(b p) f -> p b f", p=128), X2[slot][:], reads=bX2[slot], sembuf=bX2[slot][1], nbytes=1 << 20)
            S.flush(final_engines=["sync"])
        print("[kernel] ninst=%d nwaits=%d nsems=%d sim_us=%.0f" % (S.ninst, S.nwaits, len(S.sems), S.sim_time / 1e3))
    return nc


_NC_CACHE = {}


def _cols(v, n):
    return np.ascontiguousarray(np.asarray(v, np.float32).reshape(n, 128).T)


def kernel(x_prompt, x_sample, pre_norm_mix, post_norm_mix, w_in, conv_w, conv_b, rg_wa, rg_ba, rg_wx, rg_bx,
           rg_lam, pool_w, pool_b, pool_scale, w_out, pre_norm_ffn, post_norm_ffn, w_gate, w_up, w_down):
    f = lambda a: np.ascontiguousarray(np.asarray(a, dtype=np.float32))
    x_prompt, x_sample = f(x_prompt), f(x_sample)
    vecs = np.concatenate([
        _cols(conv_w[0], 16), _cols(conv_b[0], 4), _cols(rg_ba[0], 8), _cols(rg_bx[0], 8), _cols(rg_lam[0], 8),
        _cols(pool_b[0], 4), _cols(pool_scale[0], 4), _cols(pre_norm_mix[0], 8), _cols(pre_norm_ffn[0], 8)], axis=1)
    assert vecs.shape == (128, NV)
    ident = np.eye(128, dtype=np.float32).astype(ml_dtypes.bfloat16)
    ec = np.zeros((128, 4, 16), np.float32)
    for g, w in enumerate(WINS):
        half = w // 2
        for e in range(16):
            t = e if e < 8 else SEQ - 16 + e
            cnt = min(t + half, SEQ) - max(t - half, 0)
            ec[:, g, e] = 1.0 / cnt
    common = {
        "vecs": np.ascontiguousarray(vecs), "g2": f(post_norm_mix[0]).reshape(1, D), "g4": f(post_norm_ffn[0]).reshape(1, D),
        "ident": ident, "ec": ec, "w_in": f(w_in[0]), "rg_wa": f(rg_wa[0]), "rg_wx": f(rg_wx[0]), "pool_w": f(pool_w[0]),
        "w_out": f(w_out[0]), "w_gate": f(w_gate[0]), "w_up": f(w_up[0]), "w_down": f(w_down[0]),
    }
    in_maps = []
    for c in range(NCORES):
        xs = np.concatenate([x_prompt[2 * c].reshape(SEQ, D), x_prompt[2 * c + 1].reshape(SEQ, D), x_sample[c].reshape(SEQ, D)], axis=0)
        m = dict(common)
        m["x"] = np.ascontiguousarray(xs)
        in_maps.append(m)
    if "nc" not in _NC_CACHE:
        _NC_CACHE["nc"] = build_program()
    nc = _NC_CACHE["nc"]
    res = run_bass_kernel_spmd(nc, in_maps, core_ids=list(range(NCORES)))
    y_prompt = np.empty_like(x_prompt)
    y_sample = np.empty_like(x_sample)
    for c in range(NCORES):
        y = np.asarray(res.results[c]["y"], dtype=np.float32).reshape(NSEQ, SEQ, D)
        y_prompt[2 * c] = y[0]
        y_prompt[2 * c + 1] = y[1]
        y_sample[c] = y[2]
    return (y_prompt, y_sample)
```

```python
import numpy as np
import ml_dtypes
from contextlib import ExitStack
import concourse.bass as bass
import concourse.mybir as mybir
from concourse.bass_utils import run_bass_kernel_spmd

F32 = mybir.dt.float32
BF16 = mybir.dt.bfloat16
F32R = mybir.dt.float32r
AF = mybir.ActivationFunctionType
ALU = mybir.AluOpType

NCORES = 8
D = 1024
SEQ = 4096
NSEQ = 3
NTOK = NSEQ * SEQ
DL = 512
DFF = 2816
NFF = DFF // 128
T1 = 512
HALO = 16
RW = T1 + HALO
NT1 = SEQ // T1
NSTG = NT1 + 1
T2 = 256
NT2 = SEQ // T2
EPS = 1e-6
WINS = (2, 4, 8, 16)

VC_CW, VC_CB, VC_BA, VC_BX, VC_LAM, VC_PB, VC_PS, VC_G1, VC_G3, NV = 0, 16, 20, 28, 36, 44, 48, 52, 60, 68


VERBOSE = False


class Buf:
    __slots__ = ("name", "w", "r", "dsem", "grp", "psum", "gbase")

    def __init__(self, name="", psum=False):
        self.name = name
        self.psum = psum
        self.w = None
        self.r = []
        self.grp = []
        self.gbase = set()
        self.dsem = None


class Node:
    __slots__ = ("id", "eng", "kind", "fn", "out", "in_", "sem", "deps", "cost", "lat", "start", "fin", "val", "clock", "tab")


class Sched:
    ENGS = ("tensor", "vector", "scalar", "gpsimd", "sync")

    def __init__(self, nc, es):
        self.nc = nc
        self.es = es
        self.sems = []
        self.semcount = []
        self.engsem = {}
        self.known = {}
        for n in self.ENGS:
            h = es.enter_context(nc.semaphore("e_" + n))
            self.sems.append(h)
            self.semcount.append(0)
            self.engsem[n] = len(self.sems) - 1
            self.known[n] = {}
        self.nodes = []
        self.done = {}
        self.nid = 0
        self.nwaits = 0
        self.ninst = 0
        self.sim_time = 0.0

    @staticmethod
    def _deps(reads, writes):
        deps = set()
        for b in reads:
            if b.w is not None:
                deps.add(b.w)
            deps.update(b.grp)
            if b.psum:
                deps.update(b.r)
        for b in writes:
            if b.w is not None:
                deps.add(b.w)
            deps.update(b.grp)
            deps.update(b.r)
        return deps

    def _node(self, eng, kind, deps, cost, lat):
        n = Node()
        n.id = self.nid
        self.nid += 1
        n.eng, n.kind, n.deps, n.cost, n.lat = eng, kind, deps, cost, lat
        n.fn = n.out = n.in_ = n.sem = None
        n.tab = None
        self.nodes.append(n)
        self.ninst += 1
        return n

    def op(self, eng, fn, reads=(), writes=(), n=512, two=False, tab=None):
        if eng == "tensor":
            cost = 8.0 + n / 2.35
        elif eng == "scalar":
            cost = 200.0 + 0.78 * n
        elif eng == "vector":
            cost = 100.0 + 1.25 * n
        else:
            cost = 400.0 + (2.3 if two else 0.95) * n
        n = self._node(eng, "op", self._deps(reads, writes), cost, cost)
        n.fn = fn
        n.tab = tab
        n.sem = self.engsem[eng]
        for b in writes:
            b.w = n.id
            b.r = []
            b.grp = []
        for b in reads:
            if b.w != n.id:
                b.r.append(n.id)
        return n.id

    def dma(self, q, out, in_, reads=(), writes=(), sembuf=None, group=False, nbytes=65536):
        if sembuf.dsem is None:
            h = self.es.enter_context(self.nc.semaphore("d%d" % len(self.sems)))
            self.sems.append(h)
            self.semcount.append(0)
            sembuf.dsem = len(self.sems) - 1
        deps = self._deps(reads, writes)
        if group:
            for b in writes:
                if not b.grp:
                    b.gbase = set(deps)
                deps -= set(b.grp)
                deps |= b.gbase
        issue = 1000.0 if q == "gpsimd" else 60.0
        n = self._node(q, "dma", deps, issue, 2000.0 + nbytes / 150.0)
        n.out, n.in_, n.sem = out, in_, sembuf.dsem
        for b in writes:
            if group:
                b.grp.append(n.id)
            else:
                b.grp = []
                b.r = []
            b.w = n.id
        for b in reads:
            b.r.append(n.id)
        return n.id

    def flush(self, final_engines=None):
        import heapq
        nodes = self.nodes
        self.nodes = []
        byid = {n.id: n for n in nodes}
        ndep = {}
        users = {}
        ready_t = {}
        for n in nodes:
            cnt = 0
            for d in n.deps:
                if d in byid:
                    cnt += 1
                    users.setdefault(d, []).append(n.id)
            ndep[n.id] = cnt
            ready_t[n.id] = 0.0
        free = {e: 0.0 for e in self.ENGS}
        pend = {e: [] for e in self.ENGS}
        avail = {e: [] for e in self.ENGS}
        for n in nodes:
            if ndep[n.id] == 0:
                heapq.heappush(pend[n.eng], (0.0, n.id))
        order = []
        remaining = len(nodes)
        SEMLAT = 150.0
        cur_tab = [None]
        while remaining:
            best = None
            for e in self.ENGS:
                if avail[e]:
                    t = free[e]
                elif pend[e]:
                    t = max(free[e], pend[e][0][0])
                else:
                    continue
                if best is None or t < best[0]:
                    best = (t, e)
            t, e = best
            while pend[e] and pend[e][0][0] <= t:
                rt, i = heapq.heappop(pend[e])
                heapq.heappush(avail[e], i)
            extra = 0.0
            if e == "scalar":
                lst = avail[e]
                same = [i for i in lst if byid[i].tab is None or byid[i].tab == cur_tab[0]]
                nsq = sum(1 for i in lst if byid[i].tab == "sqrt")
                if cur_tab[0] != "sqrt" and 0 < nsq < 4:
                    held = [i for i in lst if byid[i].tab == "sqrt" and t - ready_t[i] < 8000.0]
                    rest = [i for i in lst if i not in held]
                    if rest:
                        lst2 = rest
                    else:
                        lst2 = lst
                else:
                    lst2 = lst
                oldest = min(lst2)
                if same and not (byid[oldest].tab not in (None, cur_tab[0]) and t - ready_t[oldest] > 25000.0):
                    i = min(same)
                else:
                    i = oldest
                lst.remove(i)
                heapq.heapify(lst)
                if byid[i].tab is not None and byid[i].tab != cur_tab[0]:
                    cur_tab[0] = byid[i].tab
                    extra = 1300.0
            else:
                i = heapq.heappop(avail[e])
            n = byid[i]
            n.start = t + extra
            t = t + extra
            free[e] = t + n.cost
            n.fin = t + n.lat
            order.append(n)
            remaining -= 1
            for u in users.get(i, ()):
                ndep[u] -= 1
                if ready_t[u] < n.fin + SEMLAT:
                    ready_t[u] = n.fin + SEMLAT
                if ndep[u] == 0:
                    un = byid[u]
                    heapq.heappush(pend[un.eng], (ready_t[u], u))
        self.sim_time += max([n.fin for n in order] + [0.0])
        if VERBOSE:
            mk = max([n.fin for n in order] + [0.0])
            busy = {e: sum(n.cost for n in order if n.eng == e) for e in self.ENGS}
            print("[sched] phase makespan %.0f us; busy " % (mk / 1e3) + " ".join("%s=%.0f%%" % (e[:4], 100 * busy[e] / mk) for e in self.ENGS))
        prog = {e: [] for e in self.ENGS}
        for n in order:
            E = n.eng
            known = self.known[E]
            need = {}
            for d in n.deps:
                if d in byid:
                    dn = byid[d]
                    ev = (dn.sem, dn.val, dn.clock)
                else:
                    ev = self.done[d]
                sem, val, clock = ev
                if E == "tensor" and sem == self.engsem["tensor"]:
                    continue
                if known.get(sem, 0) >= val:
                    continue
                if need.get(sem, (0, None))[0] < val:
                    need[sem] = (val, clock)
            for sem, (val, clock) in need.items():
                if known.get(sem, 0) >= val:
                    continue
                prog[E].append(("wait", sem, val))
                self.nwaits += 1
                for s2, v2 in clock.items():
                    if known.get(s2, 0) < v2:
                        known[s2] = v2
                known[sem] = val
            if n.kind == "op":
                self.semcount[n.sem] += 1
                prog[E].append(("op", n.fn))
            else:
                self.semcount[n.sem] += 16
                prog[E].append(("dma", n.out, n.in_, n.sem))
            n.val = self.semcount[n.sem]
            n.clock = dict(known)
        for n in order:
            self.done[n.id] = (n.sem, n.val, n.clock)
            n.fn = n.out = n.in_ = None
        for e in (final_engines or self.ENGS):
            known = self.known[e]
            for sem, cnt in enumerate(self.semcount):
                if cnt > 0 and known.get(sem, 0) < cnt and not (e == "tensor" and sem == self.engsem["tensor"]):
                    prog[e].append(("wait", sem, cnt))
                    known[sem] = cnt
        sems = self.sems
        with self.nc.Block() as block:
            def runner(e):
                def run(h):
                    mysem = sems[self.engsem[e]]
                    for item in prog[e]:
                        if item[0] == "wait":
                            h.wait_ge(sems[item[1]], item[2])
                        elif item[0] == "op":
                            item[1](h).then_inc(mysem, 1)
                        else:
                            h.dma_start(out=item[1], in_=item[2]).then_inc(sems[item[3]], 16)
                return run
            block.tensor(runner("tensor"))
            block.vector(runner("vector"))
            block.scalar(runner("scalar"))
            block.gpsimd(runner("gpsimd"))
            block.sync(runner("sync"))


class Ring:
    def __init__(self, tiles, psum=False):
        self.tiles = tiles
        self.bufs = [Buf(psum=psum) for _ in tiles]
        self.i = 0

    def next(self):
        j = self.i % len(self.tiles)
        self.i += 1
        return self.tiles[j], self.bufs[j]


def build_program():
    nc = bass.Bass("TRN2", target_bir_lowering=False)
    dt = lambda name, shape, dtype, kind: nc.dram_tensor(name, shape, dtype, kind=kind).ap()
    x_d = dt("x", [NTOK, D], F32, "ExternalInput")
    vecs_d = dt("vecs", [128, NV], F32, "ExternalInput")
    g2_d = dt("g2", [1, D], F32, "ExternalInput")
    g4_d = dt("g4", [1, D], F32, "ExternalInput")
    ident_d = dt("ident", [128, 128], BF16, "ExternalInput")
    ec_d = dt("ec", [128, 4, 16], F32, "ExternalInput")
    w_in_d = dt("w_in", [D, 3 * DL], F32, "ExternalInput")
    rg_wa_d = dt("rg_wa", [2, 8, 64, 64], F32, "ExternalInput")
    rg_wx_d = dt("rg_wx", [2, 8, 64, 64], F32, "ExternalInput")
    pool_w_d = dt("pool_w", [4, 128, 128], F32, "ExternalInput")
    w_out_d = dt("w_out", [D, D], F32, "ExternalInput")
    w_gate_d = dt("w_gate", [D, DFF], F32, "ExternalInput")
    w_up_d = dt("w_up", [D, DFF], F32, "ExternalInput")
    w_down_d = dt("w_down", [DFF, D], F32, "ExternalInput")
    y_d = dt("y", [NTOK, D], F32, "ExternalOutput")
    sa_d = dt("stash_a", [NSEQ, 4, 128, SEQ], F32, "Internal")
    sb_d = dt("stash_b", [NSEQ, 4, 128, SEQ], F32, "Internal")
    sy_d = dt("stash_y", [NSEQ, 4, 128, SEQ], BF16, "Internal")

    with ExitStack() as es:
        S = Sched(nc, es)
        op, dma = S.op, S.dma

        def sb(ctx, name, shape, dtype):
            return ctx.enter_context(nc.sbuf_tensor(name, shape, dtype))

        def ps(ctx, name, shape, dtype):
            return ctx.enter_context(nc.psum_tensor(name, shape, dtype))

        V = sb(es, "V", [128, NV], F32); bV = Buf()
        CC = sb(es, "CC", [128, 44], F32); bCC = Buf()
        IDN = sb(es, "IDN", [128, 128], BF16); bIDN = Buf()
        NEGH = sb(es, "NEGH", [128, 1], F32); bNEGH = Buf()
        WOUT = sb(es, "WOUT", [128, 8, D], BF16); bWOUT = Buf()
        HL = sb(es, "HL", [128, NSEQ * (NSTG + 1), 4], F32)
        PPt = sb(es, "PPt", [128, NSEQ * (NSTG + 1), 4], F32)
        HH = sb(es, "HH", [128, NSEQ * (NSTG + 1), 4], F32)
        bHL = [[Buf() for _ in range(NSTG + 1)] for _ in range(NSEQ)]
        bHH = [[Buf() for _ in range(NSTG + 1)] for _ in range(NSEQ)]
        ST = sb(es, "ST", [128, 64], F32)
        bST = [Buf() for _ in range(16)]
        st_i = [0]

        def hidx(s, k):
            return s * (NSTG + 1) + k

        dma("sync", V[:], vecs_d, writes=[bV], sembuf=bV)
        dma("sync", IDN[:], ident_d, writes=[bIDN], sembuf=bIDN)
        op("gpsimd", lambda e: e.memset(NEGH[:], -0.5), writes=[bNEGH], n=1)

        lam = V[:, VC_LAM:VC_LAM + 8]
        t0, t1, t2 = CC[:, 32:40], CC[:, 0:8], CC[:, 8:16]
        op("vector", lambda e: e.tensor_scalar(out=t0, in0=lam, scalar1=-1.0, scalar2=None, op0=ALU.mult), reads=[bV], writes=[bCC], n=8)
        op("scalar", lambda e: e.activation(out=t1, in_=t0, func=AF.Abs), reads=[bCC], writes=[bCC], n=8)
        op("scalar", lambda e: e.activation(out=t1, in_=t1, func=AF.Exp, scale=-1.0), reads=[bCC], writes=[bCC], n=8, tab="ln")
        op("scalar", lambda e: e.activation(out=t1, in_=t1, func=AF.Ln, bias=1.0), reads=[bCC], writes=[bCC], n=8, tab="ln")
        op("vector", lambda e: e.tensor_scalar(out=t0, in0=t0, scalar1=0.0, scalar2=None, op0=ALU.max), reads=[bCC], writes=[bCC], n=8)
        op("vector", lambda e: e.tensor_tensor(out=t0, in0=t0, in1=t1, op=ALU.add), reads=[bCC], writes=[bCC], n=8)
        op("vector", lambda e: e.tensor_scalar(out=t1, in0=t0, scalar1=-8.0, scalar2=None, op0=ALU.mult), reads=[bCC], writes=[bCC], n=8)
        op("vector", lambda e: e.tensor_scalar(out=t2, in0=t0, scalar1=-4.0, scalar2=None, op0=ALU.mult), reads=[bCC], writes=[bCC], n=8)
        op("vector", lambda e: e.tensor_scalar(out=CC[:, 16:32], in0=V[:, VC_BA:VC_BA + 16], scalar1=0.5, scalar2=None, op0=ALU.mult),
           reads=[bV, bCC], writes=[bCC], n=16)
        op("vector", lambda e: e.tensor_tensor(out=CC[:, 40:44], in0=V[:, VC_PB:VC_PB + 4], in1=V[:, VC_PS:VC_PS + 4], op=ALU.mult),
           reads=[bV, bCC], writes=[bCC], n=4)

        def stat_slot():
            j = st_i[0] % 16
            st_i[0] += 1
            return ST[:, 4 * j:4 * j + 4], bST[j]

        def rstd_from(c, b, cols, width):
            if cols == 2:
                op("gpsimd", lambda e: e.tensor_tensor(out=c[:, 0:1], in0=c[:, 0:1], in1=c[:, 1:2], op=ALU.add), reads=[b], writes=[b], n=1, two=True)
            op("gpsimd", lambda e: e.tensor_scalar(out=c[:, 2:3], in0=c[:, 0:1], scalar1=1.0 / width, scalar2=EPS, op0=ALU.mult, op1=ALU.add),
               reads=[b], writes=[b], n=1)
            op("gpsimd", lambda e: e.tensor_tensor(out=c[:, 3:4], in0=c[:, 2:3], in1=NEGH[:], op=ALU.pow), reads=[b, bNEGH], writes=[b], n=1, two=True)
            return c[:, 3:4]

        with ExitStack() as p1:
            WIN = sb(p1, "WIN", [128, 8, 3 * DL], BF16); bWIN = Buf()
            BD = sb(p1, "BD", [128, 16, 128], BF16); bBD1 = Buf()
            PWt = sb(p1, "PWt", [128, 4, 128], BF16); bPWt = Buf()
            IDF = sb(p1, "IDF", [128, 128], F32); bIDF = Buf()
            DG = sb(p1, "DG", [128, 16, 128], F32); bDG = Buf()
            EC = sb(p1, "EC", [128, 4, 16], F32); bEC = Buf()
            ZER = sb(p1, "ZER", [128, T1], F32); bZER = Buf()
            HCt = sb(p1, "HCt", [128, 4], F32); bHC = [Buf() for _ in range(4)]
            XB = Ring([sb(p1, "XB%d" % i, [128, D], F32) for i in range(4)])
            XN = Ring([sb(p1, "XN%d" % i, [128, D], BF16) for i in range(2)])
            HT = [sb(p1, "HT%d" % i, [128, 8, T1], BF16) for i in range(2)]
            bHT = [[Buf() for _ in range(4)] for _ in range(2)]
            RU = [sb(p1, "RU%d" % i, [128, 4, RW], F32) for i in range(2)]
            RG = [sb(p1, "RG%d" % i, [128, 4, RW], F32) for i in range(2)]
            RP = [sb(p1, "RP%d" % i, [128, 4, RW], F32) for i in range(2)]
            bRU = [[Buf() for _ in range(4)] for _ in range(2)]
            bRG = [[Buf() for _ in range(4)] for _ in range(2)]
            bRP = [[Buf() for _ in range(4)] for _ in range(2)]
            bHALO = [[Buf() for _ in range(2)] for _ in range(3)]
            FU = sb(p1, "FU", [128, 4, 32], F32); FG = sb(p1, "FG", [128, 4, 32], F32); FP = sb(p1, "FP", [128, 4, 32], F32)
            bFU = [Buf() for _ in range(4)]; bFG = [Buf() for _ in range(4)]; bFP = [Buf() for _ in range(4)]
            UU = Ring([sb(p1, "UU%d" % i, [128, T1], F32) for i in range(3)])
            UB = Ring([sb(p1, "UB%d" % i, [128, T1], BF16) for i in range(2)])
            THR = Ring([sb(p1, "THR%d" % i, [128, T1], F32) for i in range(2)])
            THI = [Ring([sb(p1, "THI%d_%d" % (d, i), [128, T1], F32) for i in range(2)]) for d in range(2)]
            AAr = [Ring([sb(p1, "AA%d_%d" % (d, i), [128, T1], F32) for i in range(2)]) for d in range(2)]
            A2r = [Ring([sb(p1, "A2%d_%d" % (d, i), [128, T1], F32) for i in range(2)]) for d in range(2)]
            HFr = Ring([sb(p1, "HF%d" % i, [128, T1], F32) for i in range(2)])
            HLr = Ring([sb(p1, "HLr%d" % i, [128, T1], F32) for i in range(2)])
            PCr = Ring([sb(p1, "PC%d" % i, [128, T1], F32) for i in range(2)])
            AOr = Ring([sb(p1, "AO%d" % i, [128, T1], F32) for i in range(2)])
            SGB = {}
            PS1 = sb(p1, "PS1", [128, RW], F32); bPS1 = Buf()
            PS2 = sb(p1, "PS2", [128, RW], F32); bPS2 = Buf()
            PGr = Ring([sb(p1, "PG%d" % i, [128, T1], BF16) for i in range(2)])
            YPr = Ring([sb(p1, "YP%d" % i, [128, T1], BF16) for i in range(2)])
            TPp = Ring([ps(p1, "TP%d" % i, [128, 8, 128], BF16) for i in range(2)], psum=True)
            WPp = Ring([ps(p1, "WP%d" % i, [128, T1], F32) for i in range(2)], psum=True)
            CVp = Ring([ps(p1, "CV%d" % i, [128, T1], F32) for i in range(1)], psum=True)
            GPp = Ring([ps(p1, "GP%d" % i, [128, T1], F32) for i in range(2)], psum=True)
            PMp = Ring([ps(p1, "PM%d" % i, [128, T1], F32) for i in range(1)], psum=True)

            dma("gpsimd", WIN[:], w_in_d.rearrange("(kc p) f -> p kc f", p=128), writes=[bWIN], sembuf=bWIN, nbytes=6 << 20)
            op("gpsimd", lambda e: e.memset(BD[:], 0.0), writes=[bBD1], n=2048)
            for m, wsrc in enumerate((rg_wa_d, rg_wx_d)):
                for d in range(2):
                    for h in range(8):
                        c, hf_ = h // 2, h % 2
                        idx = m * 8 + d * 4 + c
                        dma("gpsimd", BD[hf_ * 64:(hf_ + 1) * 64, idx, hf_ * 64:(hf_ + 1) * 64], wsrc[d, h],
                            writes=[bBD1], sembuf=bBD1, group=True, nbytes=16384)
            dma("gpsimd", PWt[:], pool_w_d.rearrange("g i j -> i g j"), writes=[bPWt], sembuf=bPWt, nbytes=1 << 18)
            dma("gpsimd", WOUT[:], w_out_d.rearrange("(kc p) f -> p kc f", p=128), writes=[bWOUT], sembuf=bWOUT, nbytes=4 << 20)
            dma("sync", EC[:], ec_d, writes=[bEC], sembuf=bEC)
            op("vector", lambda e: e.tensor_copy(out=IDF[:], in_=IDN[:]), reads=[bIDN], writes=[bIDF], n=128)
            for k in range(4):
                for c in range(4):
                    op("vector", lambda e, k=k, c=c: e.tensor_scalar(out=DG[:, 4 * k + c, :], in0=IDF[:], scalar1=V[:, VC_CW + 4 * k + c:VC_CW + 4 * k + c + 1],
                                                                     scalar2=None, op0=ALU.mult),
                       reads=[bIDF, bV, bDG], writes=[bDG], n=128)
            op("gpsimd", lambda e: e.memset(ZER[:], 0.0), writes=[bZER])
            op("gpsimd", lambda e: e.memset(FU[:], 0.0), writes=bFU, n=128)
            op("gpsimd", lambda e: e.memset(FG[:], 0.0), writes=bFG, n=128)
            op("gpsimd", lambda e: e.memset(FP[:], 0.0), writes=bFP, n=128)
            op("gpsimd", lambda e: e.memset(HH[:], 0.0), writes=[b for row in bHH for b in row], n=120)

            def x_block(i, tb):
                slot = i % 2
                r0 = i * T1 + tb * 128
                xb, bxb = XB.next()
                dma("sync", xb[:], x_d[r0:r0 + 128, :], writes=[bxb], sembuf=bxb, nbytes=1 << 19)
                xn, bxn = XN.next()
                c, bst = stat_slot()
                op("scalar", lambda e: e.activation(out=xn[:], in_=xb[:], func=AF.Square, accum_out=c[:, 0:1]), reads=[bxb], writes=[bst, bxn], n=D)
                rstd = rstd_from(c, bst, 1, D)
                op("vector", lambda e: e.tensor_scalar(out=xn[:], in0=xb[:], scalar1=rstd, scalar2=None, op0=ALU.mult), reads=[bxb, bst], writes=[bxn], n=D)
                tp, btp = TPp.next()
                for cc in range(8):
                    op("tensor", lambda e, cc=cc: e.transpose(out=tp[:, cc, :], in_=xn[:, cc * 128:(cc + 1) * 128], identity=IDN[:]),
                       reads=[bxn, bIDN], writes=[btp], n=128)
                op("vector", lambda e: e.tensor_tensor(out=HT[slot][:, :, tb * 128:(tb + 1) * 128], in0=tp[:],
                                                       in1=V[:, VC_G1:VC_G1 + 8].unsqueeze(2).to_broadcast([128, 8, 128]), op=ALU.mult),
                   reads=[btp, bV], writes=[bHT[slot][tb]], n=D)

            def w_chunk(i, oc):
                slot = i % 2
                k = i % NT1
                kind, c = oc // 4, oc % 4
                R, bR = ((RU, bRU), (RG, bRG), (RP, bRP))[kind]
                if c == 0:
                    if k == 0:
                        op("gpsimd", lambda e: e.memset(R[slot][:, :, 0:HALO], 0.0), writes=bR[slot], n=64)
                    else:
                        dma("sync", R[slot][:, :, 0:HALO], R[1 - slot][:, :, T1:RW], reads=bR[1 - slot], writes=bR[slot],
                            sembuf=bHALO[kind][slot], nbytes=32768)
                wp, bwp = WPp.next()
                for kc in range(8):
                    op("tensor", lambda e, kc=kc: e.matmul(out=wp[:], lhsT=WIN[:, kc, oc * 128:(oc + 1) * 128], rhs=HT[slot][:, kc, :],
                                                           start=(kc == 0), stop=(kc == 7)),
                       reads=[bWIN] + bHT[slot], writes=[bwp], n=T1)
                op("scalar", lambda e: e.activation(out=R[slot][:, c, HALO:RW], in_=wp[:], func=AF.Copy), reads=[bwp], writes=[bR[slot][c]], n=T1)

            def w_gelu(i):
                slot = i % 2
                op("scalar", lambda e: e.activation(out=RG[slot][:, :, HALO:RW], in_=RG[slot][:, :, HALO:RW], func=AF.Gelu_apprx_tanh),
                   reads=bRG[slot], writes=bRG[slot], n=4 * T1, tab="gelu")

            def lru_front(c, Ru, bRu, c0, n):
                u, bu = UU.next()
                cv, bcv = CVp.next()
                for k in range(4):
                    op("tensor", lambda e, k=k: e.matmul(out=cv[:, :n], lhsT=DG[:, 4 * k + c, :], rhs=Ru[:, c, c0 - 2 + k:c0 - 2 + k + n],
                                                         start=(k == 0), stop=(k == 3)),
                       reads=[bDG, bRu[c]], writes=[bcv], n=4 * n)
                op("scalar", lambda e: e.activation(out=u[:, :n], in_=cv[:, :n], func=AF.Identity, bias=V[:, VC_CB + c:VC_CB + c + 1], scale=1.0),
                   reads=[bcv, bV], writes=[bu], n=n)
                ub, bub = UB.next()
                op("vector", lambda e: e.tensor_scalar(out=ub[:, :n], in0=cv[:, :n], scalar1=V[:, VC_CB + c:VC_CB + c + 1], scalar2=None, op0=ALU.add),
                   reads=[bcv, bV], writes=[bub], n=n)
                res = {"u": (u, bu)}

                def per_dir(d):
                    j = d * 4 + c
                    rp, brp = GPp.next()
                    op("tensor", lambda e: e.matmul(out=rp[:, :n], lhsT=BD[:, j, :], rhs=ub[:, :n], start=True, stop=True),
                       reads=[bBD1, bub], writes=[brp], n=n)
                    thr, bthr = THR.next()
                    op("scalar", lambda e: e.activation(out=thr[:, :n], in_=rp[:, :n], func=AF.Tanh, scale=0.5, bias=CC[:, 16 + j:17 + j]),
                       reads=[brp, bCC], writes=[bthr], n=n, tab="exp")
                    ip, bip = GPp.next()
                    op("tensor", lambda e: e.matmul(out=ip[:, :n], lhsT=BD[:, 8 + j, :], rhs=ub[:, :n], start=True, stop=True),
                       reads=[bBD1, bub], writes=[bip], n=n)
                    thi, bthi = THI[d].next()
                    op("scalar", lambda e: e.activation(out=thi[:, :n], in_=ip[:, :n], func=AF.Tanh, scale=0.5, bias=CC[:, 24 + j:25 + j]),
                       reads=[bip, bCC], writes=[bthi], n=n, tab="exp")
                    aa, baa = AAr[d].next()
                    op("scalar", lambda e: e.activation(out=aa[:, :n], in_=thr[:, :n], func=AF.Exp, scale=CC[:, 8 + j:9 + j], bias=CC[:, 8 + j:9 + j]),
                       reads=[bthr, bCC], writes=[baa], n=n, tab="exp")
                    a2, ba2 = A2r[d].next()
                    if d == 0:
                        op("scalar", lambda e: e.activation(out=a2[:, :n], in_=thr[:, :n], func=AF.Exp, scale=CC[:, j:j + 1], bias=CC[:, j:j + 1]),
                           reads=[bthr, bCC], writes=[ba2], n=n, tab="exp")
                    else:
                        op("gpsimd", lambda e: e.tensor_tensor(out=a2[:, :n], in0=aa[:, :n], in1=aa[:, :n], op=ALU.mult),
                           reads=[baa], writes=[ba2], n=n, two=True)
                    op("vector", lambda e: e.scalar_tensor_tensor(out=thi[:, :n], in0=thi[:, :n], scalar=1.0, in1=u[:, :n], op0=ALU.add, op1=ALU.mult),
                       reads=[bthi, bu], writes=[bthi], n=1.2 * n)
                    res[d] = ((thi, bthi), (aa, baa), (a2, ba2))
                per_dir(0)
                per_dir(1)
                return res

            def lru_back(c, res, Rg, bRg, c0, n, s, kst, tok0):
                def sq_dir(d):
                    (thi, bthi), (aa, baa), (a2, ba2) = res[d]
                    op("scalar", lambda e: e.activation(out=a2[:, :n], in_=a2[:, :n], func=AF.Sqrt, scale=-0.25, bias=0.25),
                       reads=[ba2], writes=[ba2], n=n, tab="sqrt")
                    op("gpsimd", lambda e: e.tensor_tensor(out=thi[:, :n], in0=thi[:, :n], in1=a2[:, :n], op=ALU.mult),
                       reads=[bthi, ba2], writes=[bthi], n=n, two=True)
                sq_dir(0)
                sq_dir(1)
                (bt0, bbt0), (a0, ba0), _ = res[0]
                (bt1, bbt1), (a1, ba1), _ = res[1]
                hf, bhf = HFr.next()
                op("vector", lambda e: e.tensor_tensor_scan(out=hf[:, :n], data0=a0[:, :n], data1=bt0[:, :n], initial=HCt[:, c:c + 1],
                                                            op0=ALU.mult, op1=ALU.add),
                   reads=[ba0, bbt0, bHC[c]], writes=[bhf], n=2 * n)
                dma("sync", HCt[:, c:c + 1], hf[:, n - 1:n], reads=[bhf], writes=[bHC[c]], sembuf=bHC[c], nbytes=512)
                hl, bhl = HLr.next()
                op("vector", lambda e: e.tensor_tensor_scan(out=hl[:, 0:n][:, ::-1], data0=a1[:, 0:n][:, ::-1], data1=bt1[:, 0:n][:, ::-1],
                                                            initial=0.0, op0=ALU.mult, op1=ALU.add),
                   reads=[ba1, bbt1], writes=[bhl], n=2 * n)
                pc, bpc = PCr.next()
                op("vector", lambda e: e.tensor_tensor_scan(out=pc[:, 0:n][:, ::-1], data0=a1[:, 0:n][:, ::-1], data1=ZER[:, :n],
                                                            initial=1.0, op0=ALU.mult, op1=ALU.add),
                   reads=[ba1, bZER], writes=[bpc], n=2 * n)
                hi = hidx(s, kst)
                ao, bao = AOr.next()
                op("gpsimd", lambda e: e.tensor_tensor(out=ao[:, :n], in0=hf[:, :n], in1=hl[:, :n], op=ALU.add), reads=[bhf, bhl], writes=[bao], n=n, two=True)
                dma("sync", sa_d[s, c, :, tok0:tok0 + n], ao[:, :n], reads=[bao], sembuf=bao, nbytes=n * 512)
                dma("sync", sb_d[s, c, :, tok0:tok0 + n], pc[:, :n], reads=[bpc], sembuf=bpc, nbytes=n * 512)
                dma("sync", HL[:, hi, c:c + 1], hl[:, 0:1], reads=[bhl], writes=[bHL[s][kst]], sembuf=SGB.setdefault(("hl", id(hl)), Buf()), group=True, nbytes=512)
                dma("sync", PPt[:, hi, c:c + 1], pc[:, 0:1], reads=[bpc], writes=[bHL[s][kst]], sembuf=SGB.setdefault(("pp", id(pc)), Buf()), group=True, nbytes=512)
                sgb = SGB.setdefault(id(Rg), [Buf() for _ in range(4)])[c]
                ta = tok0
                while ta < tok0 + n:
                    k2 = ta // T2
                    tb_ = min(tok0 + n, (k2 + 1) * T2)
                    row0 = (s * NT2 + k2) * T2
                    dma("sync", y_d[row0:row0 + 128, c * T2 + (ta - k2 * T2):c * T2 + (tb_ - k2 * T2)],
                        Rg[:, c, c0 + (ta - tok0):c0 + (tb_ - tok0)], reads=[bRg[c]], sembuf=sgb, group=False, nbytes=(tb_ - ta) * 512)
                    ta = tb_

            def pool_group(g, Rp, bRp, c0, n, s, tok0, first, last):
                w = WINS[g]
                half = w // 2
                src = Rp[:, g, :]
                width = c0 + n + 8
                cur, bcur = src, bRp[g]
                tmps = [(PS1, bPS1), (PS2, bPS2)]
                step = 1
                ti = 0
                while step < half:
                    (dst, bdst) = tmps[ti % 2]
                    ti += 1
                    L = width - (2 * step - 1)
                    op("gpsimd", lambda e, cur=cur, dst=dst, step=step, L=L: e.tensor_tensor(out=dst[:, 0:L], in0=cur[:, 0:L], in1=cur[:, step:step + L], op=ALU.add),
                       reads=[bcur], writes=[bdst], n=L, two=True)
                    cur, bcur = dst, bdst
                    step *= 2
                (dst, bdst) = tmps[ti % 2]
                op("gpsimd", lambda e: e.tensor_tensor(out=dst[:, 0:n], in0=cur[:, c0 - half:c0 - half + n], in1=cur[:, c0:c0 + n], op=ALU.add),
                   reads=[bcur], writes=[bdst], n=n, two=True)
                pg, bpg = PGr.next()
                op("vector", lambda e: e.scalar_tensor_tensor(out=pg[:, :n], in0=dst[:, 0:n], scalar=1.0 / w, in1=src[:, c0:c0 + n],
                                                              op0=ALU.mult, op1=ALU.subtract),
                   reads=[bdst, bRp[g]], writes=[bpg], n=1.5 * n)
                if first or last:
                    e0, off = (0, 0) if first else (8, n - 8)
                    op("vector", lambda e: e.tensor_tensor(out=dst[:, off:off + 8], in0=dst[:, off:off + 8], in1=EC[:, g, e0:e0 + 8], op=ALU.mult),
                       reads=[bdst, bEC], writes=[bdst], n=8)
                    op("vector", lambda e: e.tensor_tensor(out=pg[:, off:off + 8], in0=dst[:, off:off + 8], in1=src[:, c0 + off:c0 + off + 8], op=ALU.subtract),
                       reads=[bdst, bRp[g]], writes=[bpg], n=8)
                pm, bpm = PMp.next()
                op("tensor", lambda e: e.matmul(out=pm[:, :n], lhsT=PWt[:, g, :], rhs=pg[:, :n], start=True, stop=True), reads=[bPWt, bpg], writes=[bpm], n=n)
                yp, byp = YPr.next()
                op("scalar", lambda e: e.activation(out=yp[:, :n], in_=pm[:, :n], func=AF.Identity, scale=V[:, VC_PS + g:VC_PS + g + 1],
                                                    bias=CC[:, 40 + g:41 + g]),
                   reads=[bpm, bV, bCC], writes=[byp], n=n)
                dma("sync", sy_d[s, g, :, tok0:tok0 + n], yp[:, :n], reads=[byp], sembuf=byp, nbytes=n * 256)

            def mix_stage(s, kst, Ru, bRu, Rg, bRg, Rp, bRp, c0, n, tok0):
                first = (kst == 0)
                last = (kst == NT1)
                fr = {}
                fr[0] = lru_front(0, Ru, bRu, c0, n)
                fr[1] = lru_front(1, Ru, bRu, c0, n)
                for c in range(4):
                    lru_back(c, fr[c], Rg, bRg, c0, n, s, kst, tok0)
                    if c + 2 < 4:
                        fr[c + 2] = lru_front(c + 2, Ru, bRu, c0, n)
                    pool_group(c, Rp, bRp, c0, n, s, tok0, first, last)

            NTILES = NSEQ * NT1
            for tb in range(4):
                x_block(0, tb)
            for oc in range(12):
                w_chunk(0, oc)
            w_gelu(0)
            for tb in range(4):
                x_block(1, tb)
            for i in range(NTILES):
                s, k = i // NT1, i % NT1
                slot = i % 2
                if k == 0:
                    for c in range(4):
                        op("gpsimd", lambda e, c=c: e.memset(HCt[:, c:c + 1], 0.0), writes=[bHC[c]], n=1)
                if k == 0:
                    mix_stage(s, 0, RU[slot], bRU[slot], RG[slot], bRG[slot], RP[slot], bRP[slot], HALO, T1 - 8, 0)
                else:
                    mix_stage(s, k, RU[slot], bRU[slot], RG[slot], bRG[slot], RP[slot], bRP[slot], 8, T1, k * T1 - 8)
                if i + 1 < NTILES:
                    for oc in range(12):
                        w_chunk(i + 1, oc)
                    w_gelu(i + 1)
                if i + 2 < NTILES:
                    for tb in range(4):
                        x_block(i + 2, tb)
                if k == NT1 - 1:
                    for c in range(4):
                        op("gpsimd", lambda e, c=c, slot=slot: e.tensor_copy(out=FU[:, c, 0:HALO], in_=RU[slot][:, c, T1:RW]), reads=[bRU[slot][c]], writes=[bFU[c]], n=16)
                        op("gpsimd", lambda e, c=c, slot=slot: e.tensor_copy(out=FG[:, c, 0:HALO], in_=RG[slot][:, c, T1:RW]), reads=[bRG[slot][c]], writes=[bFG[c]], n=16)
                        op("gpsimd", lambda e, c=c, slot=slot: e.tensor_copy(out=FP[:, c, 0:HALO], in_=RP[slot][:, c, T1:RW]), reads=[bRP[slot][c]], writes=[bFP[c]], n=16)
                    mix_stage(s, NT1, FU, bFU, FG, bFG, FP, bFP, 8, 8, SEQ - 8)
                    for kk in range(NT1, -1, -1):
                        hi, hn = hidx(s, kk), hidx(s, kk + 1)
                        op("vector", lambda e, hi=hi, hn=hn: e.tensor_tensor(out=HH[:, hi, :], in0=PPt[:, hi, :], in1=HH[:, hn, :], op=ALU.mult),
                           reads=[bHL[s][kk], bHH[s][kk + 1]], writes=[bHH[s][kk]], n=4)
                        op("vector", lambda e, hi=hi: e.tensor_tensor(out=HH[:, hi, :], in0=HH[:, hi, :], in1=HL[:, hi, :], op=ALU.add),
                           reads=[bHL[s][kk], bHH[s][kk]], writes=[bHH[s][kk]], n=4)
            S.flush()

        with ExitStack() as p2:
            WG = sb(p2, "WG", [128, 8, DFF], BF16); bWGp = [Buf() for _ in range(4)]
            WU = sb(p2, "WU", [128, 8, DFF], BF16); bWUp = [Buf() for _ in range(4)]
            WD = sb(p2, "WD", [128, NFF, D], BF16); bWDp = [Buf() for _ in range(4)]
            JP = (0, 4, 10, 16, NFF)

            def piece_of(jf):
                return max(q for q in range(4) if JP[q] <= jf)
            G2 = sb(p2, "G2", [128, D], F32); bG2 = Buf()
            G4 = sb(p2, "G4", [128, D], F32); bG4 = Buf()
            X2 = [sb(p2, "X2_%d" % i, [128, 2, D], F32) for i in range(2)]
            bX2 = [[Buf() for _ in range(2)] for _ in range(2)]
            At = sb(p2, "At", [128, 4, T2], F32); bAt = Buf()
            Bt = sb(p2, "Bt", [128, 4, T2], F32); bBt = Buf()
            Gt = sb(p2, "Gt", [128, 4, T2], F32); bGt = Buf()
            YT = sb(p2, "YT", [128, 8, T2], BF16); bYTl = Buf(); bYTp = Buf()
            H2T = sb(p2, "H2T", [128, 8, T2], BF16); bH2T = [Buf() for _ in range(2)]
            XN2 = Ring([sb(p2, "XN2_%d" % i, [128, D], BF16) for i in range(2)])
            TMP = Ring([sb(p2, "TMP%d" % i, [128, D], F32) for i in range(1)])
            SG = Ring([sb(p2, "SG%d" % i, [128, T2], F32) for i in range(2)])
            ACr = Ring([sb(p2, "AC%d" % i, [128, T2], BF16) for i in range(5)])
            MF = [ps(p2, "MF%d" % i, [128, D], F32) for i in range(2)]; bMF = [Buf(psum=True) for _ in range(2)]
            GUp = Ring([ps(p2, "GU%d" % i, [128, 2, T2], F32) for i in range(3)], psum=True)
            AUX = ps(p2, "AUX", [128, 512], F32); bAUX = Buf(psum=True)
            AUXT = AUX[:].bitcast(BF16).rearrange("p (c t) -> p c t", c=8)

            dma("sync", G2[:], g2_d.partition_broadcast(128), writes=[bG2], sembuf=bG2, nbytes=1 << 19)
            dma("sync", G4[:], g4_d.partition_broadcast(128), writes=[bG4], sembuf=bG4, nbytes=1 << 19)
            wg_v = w_gate_d.rearrange("(kc p) f -> p kc f", p=128)
            wu_v = w_up_d.rearrange("(kc p) f -> p kc f", p=128)
            wd_v = w_down_d.rearrange("(j p) f -> p j f", p=128)
            for q in range(4):
                f0, f1 = JP[q] * 128, JP[q + 1] * 128
                nb = (f1 - f0) * 4096
                dma("gpsimd", WG[:, :, f0:f1], wg_v[:, :, f0:f1], writes=[bWGp[q]], sembuf=bWGp[q], nbytes=nb)
                dma("gpsimd", WU[:, :, f0:f1], wu_v[:, :, f0:f1], writes=[bWUp[q]], sembuf=bWUp[q], nbytes=nb)
                dma("gpsimd", WD[:, JP[q]:JP[q + 1], :], wd_v[:, JP[q]:JP[q + 1], :], writes=[bWDp[q]], sembuf=bWDp[q], nbytes=nb)

            NT = NSEQ * NT2

            def tile_info(i2):
                s, j = i2 // NT2, i2 % NT2
                return s, j, i2 % 2, j * T2, s * SEQ + j * T2

            def front_pieces(i2):
                s, j, slot, t0_, r0 = tile_info(i2)
                pieces = []

                def loads():
                    dma("sync", X2[slot][:], x_d[r0:r0 + T2, :].rearrange("(b p) f -> p b f", p=128), writes=bX2[slot], sembuf=bX2[slot][0], nbytes=1 << 20)
                    dma("sync", At[:], sa_d[s, :, :, t0_:t0_ + T2].rearrange("c p t -> p c t"), writes=[bAt], sembuf=bAt, nbytes=1 << 19)
                    dma("sync", Bt[:], sb_d[s, :, :, t0_:t0_ + T2].rearrange("c p t -> p c t"), writes=[bBt], sembuf=bBt, nbytes=1 << 19)
                    dma("sync", Gt[:], y_d[r0:r0 + 128, :].rearrange("p (c t) -> p c t", c=4), writes=[bGt], sembuf=bGt, nbytes=1 << 19)
                    dma("sync", YT[:, 4:8, :], sy_d[s, :, :, t0_:t0_ + T2].rearrange("c p t -> p c t"), writes=[bYTp], sembuf=bYTp, nbytes=1 << 18)
                pieces.append(loads)

                def ylru():
                    kst = j // 2
                    for c in range(4):
                        segs = [(0, T2, kst + 1)] if j % 2 == 0 else [(0, T2 - 8, kst + 1), (T2 - 8, T2, kst + 2)]
                        for (a_, b_, kh) in segs:
                            hi = hidx(s, kh)
                            op("vector", lambda e, c=c, a_=a_, b_=b_, hi=hi: e.scalar_tensor_tensor(
                                out=At[:, c, a_:b_], in0=Bt[:, c, a_:b_], scalar=HH[:, hi, c:c + 1], in1=At[:, c, a_:b_], op0=ALU.mult, op1=ALU.add),
                               reads=[bAt, bBt, bHH[s][kh]], writes=[bAt], n=1.2 * (b_ - a_))
                    op("vector", lambda e: e.tensor_tensor(out=YT[:, 0:4, :], in0=At[:], in1=Gt[:], op=ALU.mult), reads=[bAt, bGt], writes=[bYTl], n=4 * T2)
                pieces.append(ylru)

                state = {}

                def wout_half(tb, hf_):
                    def f():
                        if hf_ == 0:
                            state[tb] = (stat_slot(), TMP.next(), XN2.next())
                        (c, bst), (tmp, btmp), (xn, bxn) = state[tb]
                        for kc in range(8):
                            op("tensor", lambda e, kc=kc: e.matmul(out=AUX[:], lhsT=YT[:, kc, tb * 128:(tb + 1) * 128],
                                                                   rhs=WOUT[:, kc, hf_ * 512:(hf_ + 1) * 512], start=(kc == 0), stop=(kc == 7)),
                               reads=[bYTl, bYTp, bWOUT], writes=[bAUX], n=512)
                        op("scalar", lambda e: e.activation(out=xn[:, hf_ * 512:(hf_ + 1) * 512], in_=AUX[:], func=AF.Square, accum_out=c[:, hf_:hf_ + 1]),
                           reads=[bAUX], writes=[bst, bxn], n=512)
                        op("vector", lambda e: e.tensor_tensor(out=tmp[:, hf_ * 512:(hf_ + 1) * 512], in0=AUX[:], in1=G2[:, hf_ * 512:(hf_ + 1) * 512], op=ALU.mult),
                           reads=[bAUX, bG2], writes=[btmp], n=512)
                    return f

                def norm_block(tb):
                    def f():
                        (c, bst), (tmp, btmp), (xn, bxn) = state[tb]
                        xrow, bx = X2[slot][:, tb, :], bX2[slot][tb]
                        rstd = rstd_from(c, bst, 2, D)
                        op("vector", lambda e: e.scalar_tensor_tensor(out=xrow, in0=tmp[:], scalar=rstd, in1=xrow, op0=ALU.mult, op1=ALU.add),
                           reads=[btmp, bst, bx], writes=[bx], n=1.2 * D)
                        c2, bst2 = stat_slot()
                        op("scalar", lambda e: e.activation(out=xn[:], in_=xrow, func=AF.Square, accum_out=c2[:, 0:1]), reads=[bx], writes=[bst2, bxn], n=D)
                        rstd2 = rstd_from(c2, bst2, 1, D)
                        op("scalar", lambda e: e.activation(out=xn[:], in_=xrow, func=AF.Copy, scale=rstd2), reads=[bx, bst2], writes=[bxn], n=D)
                    return f

                for tb in range(2):
                    pieces.append(wout_half(tb, 0))
                    pieces.append(wout_half(tb, 1))
                    pieces.append(norm_block(tb))

                def transposes():
                    for tb in range(2):
                        (c, bst), (tmp, btmp), (xn, bxn) = state[tb]
                        if tb == 0:
                            tpv, btpv = AUXT, bAUX
                        else:
                            gu, btpv = GUp.next()
                            tpv = gu[:].rearrange("p a t -> p (a t)").bitcast(BF16).rearrange("p (c t) -> p c t", c=8)
                        for cc in range(8):
                            op("tensor", lambda e, cc=cc, xn=xn, tpv=tpv: e.transpose(out=tpv[:, cc, :], in_=xn[:, cc * 128:(cc + 1) * 128], identity=IDN[:]),
                               reads=[bxn, bIDN], writes=[btpv], n=128)
                        op("vector", lambda e, tb=tb, tpv=tpv: e.tensor_tensor(out=H2T[:, :, tb * 128:(tb + 1) * 128], in0=tpv,
                                                                               in1=V[:, VC_G3:VC_G3 + 8].unsqueeze(2).to_broadcast([128, 8, 128]), op=ALU.mult),
                           reads=[btpv, bV], writes=[bH2T[tb]], n=D)
                return pieces, transposes

            def down(i2, jf, ac, bac):
                for tb in range(2):
                    for hf_ in range(2):
                        op("tensor", lambda e, tb=tb, hf_=hf_: e.matmul(
                            out=MF[tb][:, hf_ * 512:(hf_ + 1) * 512], lhsT=ac[:, tb * 128:(tb + 1) * 128],
                            rhs=WD[:, jf, hf_ * 512:(hf_ + 1) * 512], start=(jf == 0), stop=(jf == NFF - 1)),
                           reads=[bac, bWDp[piece_of(jf)]], writes=[bMF[tb]], n=512)

            def final_block(i2, tb):
                s, j, slot, t0_, r0 = tile_info(i2)
                xrow, bx = X2[slot][:, tb, :], bX2[slot][tb]
                c, bst = stat_slot()
                tmp, btmp = TMP.next()
                op("scalar", lambda e: e.activation(out=tmp[:].bitcast(BF16)[:, 0:D], in_=MF[tb][:], func=AF.Square, accum_out=c[:, 0:1]),
                   reads=[bMF[tb]], writes=[bst, btmp], n=D)
                op("vector", lambda e: e.tensor_tensor(out=tmp[:], in0=MF[tb][:], in1=G4[:], op=ALU.mult), reads=[bMF[tb], bG4], writes=[btmp], n=D)
                rstd = rstd_from(c, bst, 1, D)
                op("vector", lambda e: e.scalar_tensor_tensor(out=xrow, in0=tmp[:], scalar=rstd, in1=xrow, op0=ALU.mult, op1=ALU.add),
                   reads=[btmp, bst, bx], writes=[bx], n=1.2 * D)

            pieces, transposes = front_pieces(0)
            for p in pieces:
                p()
            transposes()
            for i2 in range(NT):
                s, j, slot, t0_, r0 = tile_info(i2)
                if i2 + 1 < NT:
                    nxt_pieces, nxt_transposes = front_pieces(i2 + 1)
                else:
                    nxt_pieces, nxt_transposes = [], None
                at = {0: 0, 1: 1, 3: 2, 5: 3, 7: 4, 10: 5, 12: 6, 14: 7}
                prevq = []
                for jf in range(NFF):
                    gu, bgu = GUp.next()
                    for kc in range(8):
                        op("tensor", lambda e, kc=kc, gu=gu, jf=jf: e.matmul(out=gu[:, 0, :], lhsT=WG[:, kc, jf * 128:(jf + 1) * 128], rhs=H2T[:, kc, :],
                                                                            start=(kc == 0), stop=(kc == 7)),
                           reads=[bWGp[piece_of(jf)]] + bH2T, writes=[bgu], n=T2)
                    for kc in range(8):
                        op("tensor", lambda e, kc=kc, gu=gu, jf=jf: e.matmul(out=gu[:, 1, :], lhsT=WU[:, kc, jf * 128:(jf + 1) * 128], rhs=H2T[:, kc, :],
                                                                            start=(kc == 0), stop=(kc == 7)),
                           reads=[bWUp[piece_of(jf)]] + bH2T, writes=[bgu], n=T2)
                    if len(prevq) >= 3:
                        down(i2, *prevq.pop(0))
                    if jf in at and at[jf] < len(nxt_pieces):
                        nxt_pieces[at[jf]]()
                    sg, bsg = SG.next()
                    op("scalar", lambda e, sg=sg, gu=gu: e.activation(out=sg[:], in_=gu[:, 0, :], func=AF.Silu), reads=[bgu], writes=[bsg], n=T2, tab="silu")
                    ac, bac = ACr.next()
                    op("vector", lambda e, sg=sg, gu=gu, ac=ac: e.tensor_tensor(out=ac[:], in0=sg[:], in1=gu[:, 1, :], op=ALU.mult), reads=[bsg, bgu], writes=[bac], n=T2)
                    prevq.append((jf, ac, bac))
                down(i2, *prevq.pop(0))
                if nxt_transposes is not None:
                    nxt_transposes()
                down(i2, *prevq.pop(0))
                down(i2, *prevq.pop(0))
                for tb in range(2):
                    final_block(i2, tb)
                dma("sync", y_d[r0:r0 + T2, :].rearrange("(b p) f -> p b f", p=128), X2[slot][:], reads=bX2[slot], sembuf=bX2[slot][1], nbytes=1 << 20)
            S.flush(final_engines=["sync"])
        print("[kernel] ninst=%d nwaits=%d nsems=%d sim_us=%.0f" % (S.ninst, S.nwaits, len(S.sems), S.sim_time / 1e3))
    return nc


_NC_CACHE = {}


def _cols(v, n):
    return np.ascontiguousarray(np.asarray(v, np.float32).reshape(n, 128).T)


def kernel(x_prompt, x_sample, pre_norm_mix, post_norm_mix, w_in, conv_w, conv_b, rg_wa, rg_ba, rg_wx, rg_bx,
           rg_lam, pool_w, pool_b, pool_scale, w_out, pre_norm_ffn, post_norm_ffn, w_gate, w_up, w_down):
    f = lambda a: np.ascontiguousarray(np.asarray(a, dtype=np.float32))
    x_prompt, x_sample = f(x_prompt), f(x_sample)
    vecs = np.concatenate([
        _cols(conv_w[0], 16), _cols(conv_b[0], 4), _cols(rg_ba[0], 8), _cols(rg_bx[0], 8), _cols(rg_lam[0], 8),
        _cols(pool_b[0], 4), _cols(pool_scale[0], 4), _cols(pre_norm_mix[0], 8), _cols(pre_norm_ffn[0], 8)], axis=1)
    assert vecs.shape == (128, NV)
    ident = np.eye(128, dtype=np.float32).astype(ml_dtypes.bfloat16)
    ec = np.zeros((128, 4, 16), np.float32)
    for g, w in enumerate(WINS):
        half = w // 2
        for e in range(16):
            t = e if e < 8 else SEQ - 16 + e
            cnt = min(t + half, SEQ) - max(t - half, 0)
            ec[:, g, e] = 1.0 / cnt
    common = {
        "vecs": np.ascontiguousarray(vecs), "g2": f(post_norm_mix[0]).reshape(1, D), "g4": f(post_norm_ffn[0]).reshape(1, D),
        "ident": ident, "ec": ec, "w_in": f(w_in[0]), "rg_wa": f(rg_wa[0]), "rg_wx": f(rg_wx[0]), "pool_w": f(pool_w[0]),
        "w_out": f(w_out[0]), "w_gate": f(w_gate[0]), "w_up": f(w_up[0]), "w_down": f(w_down[0]),
    }
    in_maps = []
    for c in range(NCORES):
        xs = np.concatenate([x_prompt[2 * c].reshape(SEQ, D), x_prompt[2 * c + 1].reshape(SEQ, D), x_sample[c].reshape(SEQ, D)], axis=0)
        m = dict(common)
        m["x"] = np.ascontiguousarray(xs)
        in_maps.append(m)
    if "nc" not in _NC_CACHE:
        _NC_CACHE["nc"] = build_program()
    nc = _NC_CACHE["nc"]
    res = run_bass_kernel_spmd(nc, in_maps, core_ids=list(range(NCORES)))
    y_prompt = np.empty_like(x_prompt)
    y_sample = np.empty_like(x_sample)
    for c in range(NCORES):
        y = np.asarray(res.results[c]["y"], dtype=np.float32).reshape(NSEQ, SEQ, D)
        y_prompt[2 * c] = y[0]
        y_prompt[2 * c + 1] = y[1]
        y_sample[c] = y[2]
    return (y_prompt, y_sample)
```
